# Optimizing a Trainium2 kernel written in Bass

```python
import jax, jax.numpy as jnp
from jax import lax
import numpy as np


D_MODEL = 2048
BATCH = 4
SEQ = 2048
DEPTH = 2

HEAD_DIM = 128
ROPE_THETA = 10000.0
NORM_EPS = 1e-6
Q_BLOCK = 128
NEG = -1e30

NSA_HEADS = D_MODEL // (2 * HEAD_DIM)
NSA_KV_GROUPS = 2
CMP_LEN = 32
CMP_STRIDE = 16
CMP_HIDDEN = 256
SLC_LEN = 64
SLC_TOP = 8
WIN = 512
FOX_HEADS = D_MODEL // (2 * HEAD_DIM)
MLA_HEADS = D_MODEL // HEAD_DIM
MLA_Q_RANK = 512
MLA_KV_RANK = 512
MLA_NOPE = 128
MLA_ROPE = 64
MLA_V = 128
D_FF = 5632

NSA_Q = NSA_HEADS * HEAD_DIM
NSA_KV = NSA_KV_GROUPS * HEAD_DIM
FOX_W = FOX_HEADS * HEAD_DIM
HYB_SPLITS = (NSA_Q, NSA_KV, NSA_KV, NSA_KV, NSA_KV, NSA_KV, NSA_KV, 3 * NSA_HEADS, FOX_W, FOX_W, FOX_W, FOX_HEADS)
HYB_IN = sum(HYB_SPLITS)
HYB_OUT = NSA_Q + FOX_W
MLA_SPLITS = (MLA_Q_RANK, MLA_KV_RANK, MLA_ROPE)
MLA_IN = sum(MLA_SPLITS)
N_EVEN = (DEPTH + 1) // 2
N_ODD = DEPTH // 2

kernel_name = 'hybrid_nsa_fox_mla_macaron'


def split_cols(a, sizes):
    out, start = [], 0
    for s in sizes:
        out.append(a[..., start:start + s])
        start += s
    return out


def rms_norm(x, g):
    x32 = x.astype(jnp.float32)
    y = x32 * lax.rsqrt(jnp.mean(x32 * x32, axis=-1, keepdims=True) + NORM_EPS)
    return (y * g.astype(jnp.float32)).astype(x.dtype)


def rope(x, pos):
    d = x.shape[-1]
    inv = 1.0 / (ROPE_THETA ** (np.arange(0, d, 2, dtype=np.float32) / d))
    ang = pos.astype(jnp.float32)[:, None] * jnp.asarray(inv, jnp.float32)[None, :]
    cos = jnp.cos(ang)[None, :, None, :]
    sin = jnp.sin(ang)[None, :, None, :]
    x32 = x.astype(jnp.float32)
    x1, x2 = x32[..., : d // 2], x32[..., d // 2:]
    return jnp.concatenate([x1 * cos - x2 * sin, x2 * cos + x1 * sin], axis=-1).astype(x.dtype)


def swiglu(x, w_in, w_out):
    h = x @ w_in
    return (jax.nn.silu(h[..., :D_FF]) * h[..., D_FF:]) @ w_out


def to_chunks(a):
    b, t = a.shape[:2]
    return jnp.moveaxis(a.reshape(b, t // Q_BLOCK, Q_BLOCK, *a.shape[2:]), 1, 0)


def from_chunks(a):
    a = jnp.moveaxis(a, 0, 1)
    return a.reshape(a.shape[0], -1, *a.shape[3:])


def blocked_causal_attention(q, k, v, scale, cum_log_f=None):
    t_len = q.shape[1]
    kpos = jnp.arange(t_len)
    xs = [to_chunks(q), jnp.arange(t_len).reshape(-1, Q_BLOCK)]
    ck = None
    if cum_log_f is not None:
        xs.append(to_chunks(cum_log_f))
        ck = jnp.moveaxis(cum_log_f, 1, 2)[:, :, None, :]

    def block(args):
        qc, qpos = args[0], args[1]
        s = jnp.einsum('bqhd,bkhd->bhqk', qc, k).astype(jnp.float32) * scale
        if ck is not None:
            s = s + (jnp.moveaxis(args[2], 1, 2)[..., None] - ck)
        s = jnp.where(kpos[None, :] <= qpos[:, None], s, NEG)
        p = jax.nn.softmax(s, axis=-1)
        return jnp.einsum('bhqk,bkhd->bqhd', p.astype(v.dtype), v)

    return from_chunks(lax.map(block, tuple(xs)))


def cmp_to_slc_matrix(n_cmp, n_slc):
    c0 = np.arange(n_cmp) * CMP_STRIDE
    c1 = c0 + CMP_LEN
    s0 = np.arange(n_slc) * SLC_LEN
    s1 = s0 + SLC_LEN
    ov = np.clip(np.minimum(c1[:, None], s1[None, :]) - np.maximum(c0[:, None], s0[None, :]), 0, None)
    return (ov / CMP_LEN).astype(np.float32)


def nsa_attention(q, k_cmp, v_cmp, k_slc, v_slc, k_win, v_win, gates, pos, cmp_pe, cmp_w1, cmp_w2):
    b, t_len, h, d = q.shape
    g = k_cmp.shape[2]
    hg = h // g
    scale = d ** -0.5
    q_plain = q.reshape(b, t_len, g, hg, d)
    q_rot = rope(q, pos).reshape(b, t_len, g, hg, d)
    k_slc = rope(k_slc, pos)
    k_win = rope(k_win, pos)

    n_cmp = (t_len - CMP_LEN) // CMP_STRIDE + 1
    blk = np.arange(n_cmp)[:, None] * CMP_STRIDE + np.arange(CMP_LEN)[None, :]

    def compress(x, pe, w1, w2):
        xb = x[:, blk] + pe[None, None, :, None, :]
        xb = jnp.moveaxis(xb, 3, 2).reshape(b, n_cmp, g, CMP_LEN * d)
        return jax.nn.gelu(xb @ w1) @ w2

    kc = compress(k_cmp, cmp_pe[0], cmp_w1[0], cmp_w2[0])
    vc = compress(v_cmp, cmp_pe[1], cmp_w1[1], cmp_w2[1])
    cmp_end = jnp.asarray(np.arange(n_cmp) * CMP_STRIDE + CMP_LEN - 1)
    cmask = cmp_end[None, :] <= pos[:, None]
    s = jnp.einsum('btghd,bngd->bghtn', q_plain, kc).astype(jnp.float32) * scale
    p_cmp = jax.nn.softmax(jnp.where(cmask, s, NEG), axis=-1) * cmask
    o_cmp = jnp.einsum('bghtn,bngd->btghd', p_cmp.astype(vc.dtype), vc)

    n_slc = t_len // SLC_LEN
    top = min(SLC_TOP, n_slc)
    imp = jnp.einsum('bghtn,nj->btgj', p_cmp, jnp.asarray(cmp_to_slc_matrix(n_cmp, n_slc)))
    j = jnp.arange(n_slc)[None, None, :]
    cur = (pos // SLC_LEN)[:, None, None]
    forced = (j == 0) | (j == cur) | (j == cur - 1)
    imp = jnp.where(j > cur, -jnp.inf, jnp.where(forced, jnp.inf, imp))
    _, sel = lax.top_k(imp, top)
    kblk = jnp.moveaxis(k_slc.reshape(b, n_slc, SLC_LEN, g, d), 3, 1)
    vblk = jnp.moveaxis(v_slc.reshape(b, n_slc, SLC_LEN, g, d), 3, 1)
    bi = jnp.arange(b)[:, None, None, None]
    gi = jnp.arange(g)[None, None, :, None]
    offs = jnp.arange(SLC_LEN)

    def slc_block(args):
        qc, sc, qpos = args
        kg = kblk[bi, gi, sc]
        vg = vblk[bi, gi, sc]
        kpos = sc[..., None] * SLC_LEN + offs
        m = (kpos <= qpos[None, :, None, None, None])[:, :, :, None]
        s = jnp.einsum('bqghd,bqgkld->bqghkl', qc, kg).astype(jnp.float32) * scale
        s = jnp.where(m, s, NEG)
        bq = qc.shape[1]
        p = jax.nn.softmax(s.reshape(b, bq, g, hg, top * SLC_LEN), axis=-1)
        p = p.reshape(b, bq, g, hg, top, SLC_LEN)
        return jnp.einsum('bqghkl,bqgkld->bqghd', p.astype(vg.dtype), vg)

    o_slc = from_chunks(lax.map(slc_block, (to_chunks(q_rot), to_chunks(sel), pos.reshape(-1, Q_BLOCK))))

    nc = t_len // Q_BLOCK
    band = np.arange(nc)[:, None] * Q_BLOCK + np.arange(WIN + Q_BLOCK)[None, :]
    pad = ((0, 0), (WIN, 0), (0, 0), (0, 0))
    kb = jnp.pad(k_win, pad)[:, band]
    vb = jnp.pad(v_win, pad)[:, band]
    qq = np.arange(Q_BLOCK)[:, None]
    kk = np.arange(WIN + Q_BLOCK)[None, :]
    wmask = ((kk > qq) & (kk <= qq + WIN))[None] & ((np.arange(nc)[:, None, None] * Q_BLOCK + kk[None]) >= WIN)
    qw = q_rot.reshape(b, nc, Q_BLOCK, g, hg, d)
    s = jnp.einsum('bcqghd,bckgd->bcghqk', qw, kb).astype(jnp.float32) * scale
    s = jnp.where(jnp.asarray(wmask)[None, :, None, None], s, NEG)
    p = jax.nn.softmax(s, axis=-1)
    o_win = jnp.einsum('bcghqk,bckgd->bcqghd', p.astype(vb.dtype), vb).reshape(b, t_len, g, hg, d)

    gt = gates.reshape(b, t_len, g, hg, 3)
    return o_cmp * gt[..., 0:1] + o_slc * gt[..., 1:2] + o_win * gt[..., 2:3]


def hybrid_mixer(h, w_in, w_out, cmp_pe, cmp_w1, cmp_w2, f_bias):
    b, t_len, _ = h.shape
    pos = jnp.arange(t_len)
    (q_n, k_c, v_c, k_s, v_s, k_w, v_w, g_logit,
     q_f, k_f, v_f, f_logit) = split_cols(h @ w_in, HYB_SPLITS)
    heads = lambda a, n: a.reshape(b, t_len, n, HEAD_DIM)
    gates = jax.nn.sigmoid(g_logit.astype(jnp.float32)).reshape(b, t_len, NSA_HEADS, 3).astype(h.dtype)
    o_nsa = nsa_attention(heads(q_n, NSA_HEADS),
                          heads(k_c, NSA_KV_GROUPS), heads(v_c, NSA_KV_GROUPS),
                          heads(k_s, NSA_KV_GROUPS), heads(v_s, NSA_KV_GROUPS),
                          heads(k_w, NSA_KV_GROUPS), heads(v_w, NSA_KV_GROUPS),
                          gates, pos, cmp_pe, cmp_w1, cmp_w2)
    log_f = jax.nn.log_sigmoid(f_logit.astype(jnp.float32) + f_bias.astype(jnp.float32))
    cum = jnp.cumsum(log_f, axis=1)
    o_fox = blocked_causal_attention(heads(q_f, FOX_HEADS), heads(k_f, FOX_HEADS),
                                     heads(v_f, FOX_HEADS), HEAD_DIM ** -0.5, cum)
    o = jnp.concatenate([o_nsa.reshape(b, t_len, NSA_Q), o_fox.reshape(b, t_len, FOX_W)], axis=-1)
    return o @ w_out


def mla_mixer(h, w_in, q_norm, kv_norm, w_uq, w_ukv, w_out):
    b, t_len, _ = h.shape
    pos = jnp.arange(t_len)
    c_q, c_kv, k_r = split_cols(h @ w_in, MLA_SPLITS)
    q = (rms_norm(c_q, q_norm) @ w_uq).reshape(b, t_len, MLA_HEADS, MLA_NOPE + MLA_ROPE)
    q = jnp.concatenate([q[..., :MLA_NOPE], rope(q[..., MLA_NOPE:], pos)], axis=-1)
    kv = (rms_norm(c_kv, kv_norm) @ w_ukv).reshape(b, t_len, MLA_HEADS, MLA_NOPE + MLA_V)
    k_r = rope(k_r[:, :, None, :], pos)
    k = jnp.concatenate([kv[..., :MLA_NOPE], jnp.broadcast_to(k_r, (b, t_len, MLA_HEADS, MLA_ROPE))], axis=-1)
    o = blocked_causal_attention(q, k, kv[..., MLA_NOPE:], (MLA_NOPE + MLA_ROPE) ** -0.5)
    return o.reshape(b, t_len, MLA_HEADS * MLA_V) @ w_out


def setup_inputs(seed: int = 0) -> dict:
    key = jax.random.key(seed)
    ks = jax.random.split(key, 21)
    f32 = jnp.float32

    def w(k, shape, fan_in):
        return jax.random.normal(k, shape, f32) * (fan_in ** -0.5)

    def gain(k, shape):
        return 1.0 + 0.02 * jax.random.normal(k, shape, f32)

    return {
        'x': jax.random.normal(ks[0], (BATCH, SEQ, D_MODEL), f32),
        'ffn1_norm': gain(ks[1], (DEPTH, D_MODEL)),
        'ffn1_w_in': w(ks[2], (DEPTH, D_MODEL, 2 * D_FF), D_MODEL),
        'ffn1_w_out': w(ks[3], (DEPTH, D_FF, D_MODEL), D_FF),
        'mix_norm': gain(ks[4], (DEPTH, D_MODEL)),
        'ffn2_norm': gain(ks[5], (DEPTH, D_MODEL)),
        'ffn2_w_in': w(ks[6], (DEPTH, D_MODEL, 2 * D_FF), D_MODEL),
        'ffn2_w_out': w(ks[7], (DEPTH, D_FF, D_MODEL), D_FF),
        'hyb_w_in': w(ks[8], (N_EVEN, D_MODEL, HYB_IN), D_MODEL),
        'hyb_w_out': w(ks[9], (N_EVEN, HYB_OUT, D_MODEL), HYB_OUT),
        'nsa_cmp_pe': 0.02 * jax.random.normal(ks[10], (N_EVEN, 2, CMP_LEN, HEAD_DIM), f32),
        'nsa_cmp_w1': w(ks[11], (N_EVEN, 2, CMP_LEN * HEAD_DIM, CMP_HIDDEN), CMP_LEN * HEAD_DIM),
        'nsa_cmp_w2': w(ks[12], (N_EVEN, 2, CMP_HIDDEN, HEAD_DIM), CMP_HIDDEN),
        'fox_f_bias': jax.random.uniform(ks[13], (N_EVEN, FOX_HEADS), f32, 1.0, 4.0),
        'mla_w_in': w(ks[14], (N_ODD, D_MODEL, MLA_IN), D_MODEL),
        'mla_q_norm': gain(ks[15], (N_ODD, MLA_Q_RANK)),
        'mla_kv_norm': gain(ks[16], (N_ODD, MLA_KV_RANK)),
        'mla_w_uq': w(ks[17], (N_ODD, MLA_Q_RANK, MLA_HEADS * (MLA_NOPE + MLA_ROPE)), MLA_Q_RANK),
        'mla_w_ukv': w(ks[18], (N_ODD, MLA_KV_RANK, MLA_HEADS * (MLA_NOPE + MLA_V)), MLA_KV_RANK),
        'mla_w_out': w(ks[19], (N_ODD, MLA_HEADS * MLA_V, D_MODEL), MLA_HEADS * MLA_V),
        'final_norm': gain(ks[20], (D_MODEL,)),
    }


def reference(x, ffn1_norm, ffn1_w_in, ffn1_w_out, mix_norm, ffn2_norm, ffn2_w_in, ffn2_w_out,
              hyb_w_in, hyb_w_out, nsa_cmp_pe, nsa_cmp_w1, nsa_cmp_w2, fox_f_bias,
              mla_w_in, mla_q_norm, mla_kv_norm, mla_w_uq, mla_w_ukv, mla_w_out, final_norm):
    for i in range(DEPTH):
        x = x + 0.5 * swiglu(rms_norm(x, ffn1_norm[i]), ffn1_w_in[i], ffn1_w_out[i])
        h = rms_norm(x, mix_norm[i])
        if i % 2 == 0:
            e = i // 2
            x = x + hybrid_mixer(h, hyb_w_in[e], hyb_w_out[e], nsa_cmp_pe[e], nsa_cmp_w1[e],
                                 nsa_cmp_w2[e], fox_f_bias[e])
        else:
            o = i // 2
            x = x + mla_mixer(h, mla_w_in[o], mla_q_norm[o], mla_kv_norm[o], mla_w_uq[o],
                              mla_w_ukv[o], mla_w_out[o])
        x = x + 0.5 * swiglu(rms_norm(x, ffn2_norm[i]), ffn2_w_in[i], ffn2_w_out[i])
    return rms_norm(x, final_norm)
```

```python
import contextlib
import numpy as np
import ml_dtypes
import concourse.bass as bass
import concourse.mybir as mybir
from concourse.bass_utils import run_bass_kernel_spmd

F32 = mybir.dt.float32
BF16 = mybir.dt.bfloat16
AF = mybir.ActivationFunctionType
ALU = mybir.AluOpType
AX = mybir.AxisListType
NPBF = ml_dtypes.bfloat16

D_MODEL = 2048
SEQ = 2048
BATCH = 4
D_FF = 5632
EPS = 1e-6


class _Op:
    __slots__ = ("eng", "fn", "deps", "dma_key", "signal", "count")

    def __init__(self, eng, fn, dma_key):
        self.eng = eng
        self.fn = fn
        self.deps = set()
        self.dma_key = dma_key
        self.signal = False
        self.count = 0


class Sched:
    ENGS = ("pe", "act", "dve", "pool", "sp")

    def __init__(self):
        self.ops = []
        self.kw = {}
        self.kr = {}
        self.excl = set()

    def add(self, eng, fn, reads=(), writes=(), dma_key=None):
        idx = len(self.ops)
        op = _Op(eng, fn, dma_key)
        for k in reads:
            w = self.kw.get(k)
            if w is not None:
                op.deps.add(w)
            if k in self.excl:
                for r in self.kr.get(k, ()):
                    if self.ops[r].eng != eng:
                        op.deps.add(r)
        for k in writes:
            w = self.kw.get(k)
            if w is not None:
                op.deps.add(w)
            for r in self.kr.get(k, ()):
                op.deps.add(r)
        for k in reads:
            self.kr.setdefault(k, []).append(idx)
        for k in writes:
            self.kw[k] = idx
            self.kr[k] = []
        op.deps.discard(idx)
        self.ops.append(op)
        return idx

    @staticmethod
    def _skip(dop, o):
        return dop.dma_key is None and dop.eng == "pe" and o.eng == "pe" and o.dma_key is None

    def emit(self, nc, stack):
        ops = self.ops
        for o in ops:
            for d in o.deps:
                dop = ops[d]
                if dop.dma_key is None and not self._skip(dop, o):
                    dop.signal = True
        eng_cnt = {e: 0 for e in self.ENGS}
        dma_cnt = {}
        for o in ops:
            if o.dma_key is not None:
                dma_cnt[o.dma_key] = dma_cnt.get(o.dma_key, 0) + 16
                o.count = dma_cnt[o.dma_key]
            elif o.signal:
                eng_cnt[o.eng] += 1
                o.count = eng_cnt[o.eng]
        assert max(eng_cnt.values()) < 60000, eng_cnt
        eng_sem = {e: stack.enter_context(nc.semaphore("es_" + e)) for e in self.ENGS}
        dma_sem = {}
        for i, k in enumerate(dma_cnt):
            dma_sem[k] = stack.enter_context(nc.semaphore("ds_%d" % i))
        streams = {e: [o for o in ops if o.eng == e] for e in self.ENGS}
        block = stack.enter_context(nc.Block())

        def make(e):
            def body(engobj):
                waited = {}
                for o in streams[e]:
                    need = {}
                    for d in o.deps:
                        dop = ops[d]
                        if dop.dma_key is not None:
                            key = ("d", dop.dma_key)
                            sem = dma_sem[dop.dma_key]
                        else:
                            if self._skip(dop, o):
                                continue
                            key = ("e", dop.eng)
                            sem = eng_sem[dop.eng]
                        if need.get(key, (None, 0))[1] < dop.count:
                            need[key] = (sem, dop.count)
                    for key, (sem, v) in need.items():
                        if waited.get(key, 0) < v:
                            engobj.wait_ge(sem, v)
                            waited[key] = v
                    ins = o.fn(engobj)
                    if o.dma_key is not None:
                        ins.then_inc(dma_sem[o.dma_key], 16)
                    elif o.signal:
                        ins.then_inc(eng_sem[e], 1)
                if e == "sp":
                    for k, sem in dma_sem.items():
                        engobj.wait_ge(sem, dma_cnt[k])
            return body

        block.tensor(make("pe"))
        block.scalar(make("act"))
        block.vector(make("dve"))
        block.gpsimd(make("pool"))
        block.sync(make("sp"))


class KB:
    def __init__(self, nc, stack):
        self.nc = nc
        self.stack = stack
        self.s = Sched()
        self.n = 0
        self.psn = 0

    def sb(self, shape, dt, name=None):
        self.n += 1
        return self.stack.enter_context(
            self.nc.sbuf_tensor("s_" + (name or ("sb%d" % self.n)), list(shape), dt))

    def ps(self, shape, dt, name=None):
        self.psn += 1
        return self.stack.enter_context(
            self.nc.psum_tensor("p_" + (name or ("ps%d" % self.psn)), list(shape), dt))

    def dram(self, name, shape, dt, kind):
        return self.nc.dram_tensor(name, list(shape), dt, kind=kind).ap()

    def add(self, *a, **k):
        return self.s.add(*a, **k)

    def finish(self):
        self.s.emit(self.nc, self.stack)


class Rot:
    def __init__(self, bufs):
        self.bufs = bufs
        self.i = 0

    def next(self):
        b = self.bufs[self.i % len(self.bufs)]
        self.i += 1
        return b


class Common:
    def __init__(self, kb, ident_dram, nf=6, nb=2):
        self.kb = kb
        self.psf_list = [(kb.ps([128, 512], F32), "psf%d" % i) for i in range(nf)]
        self.psb_list = [(kb.ps([128, 1024], BF16), "psb%d" % i) for i in range(nb)]
        self.psf = Rot(self.psf_list)
        self.psb = Rot(self.psb_list)
        kb.s.excl.update(k for _, k in self.psf_list + self.psb_list)
        self.ident = kb.sb([128, 128], BF16, "ident")
        kb.add("sp", lambda e: e.dma_start(out=self.ident[:], in_=ident_dram),
               writes=["ident"], dma_key="ident")
        self.eps = kb.sb([128, 1], F32, "eps")
        kb.add("dve", lambda e: e.memset(self.eps[:], EPS), writes=["eps"])


def emit_load_x(kb, X, xkey, x_dram, NT):
    for i in range(NT):
        kb.add("sp", (lambda e, i=i: e.dma_start(out=X[:, i, :], in_=x_dram[i * 128:(i + 1) * 128, :])),
               writes=[(xkey, i)], dma_key=(xkey, "ld", i))


def emit_store_x(kb, X, xkey, out_dram, NT):
    for i in range(NT):
        kb.add("sp", (lambda e, i=i: e.dma_start(out=out_dram[i * 128:(i + 1) * 128, :], in_=X[:, i, :])),
               reads=[(xkey, i)], dma_key=(xkey, "st", i))


def emit_norm_T(kb, cm, X, xkey, NT, D, gT, gkey, xnT, xnkey, scr, pre=None):
    KT = D // 128
    for i in range(NT):
        if pre is not None:
            Xi, xik = pre(i)
        else:
            Xi, xik = X[:, i, :], (xkey, i)
        sq, sqk = scr["sq"].next()
        ss, ssk = scr["ss"].next()
        xs, xsk = scr["xs"].next()
        kb.add("act", (lambda e, i=i, sq=sq, ss=ss, Xi=Xi: e.activation(
            out=sq[:, :D], in_=Xi, func=AF.Square, accum_out=ss[:, 0:1])),
            reads=[xik], writes=[sqk, ssk])
        kb.add("act", (lambda e, ss=ss: e.activation(
            out=ss[:, 1:2], in_=ss[:, 0:1], func=AF.Sqrt, scale=1.0 / D, bias=cm.eps[:, 0:1])),
            reads=[ssk, "eps"], writes=[ssk])
        kb.add("dve", (lambda e, ss=ss: e.reciprocal(out=ss[:, 2:3], in_=ss[:, 1:2])),
            reads=[ssk], writes=[ssk])
        kb.add("act", (lambda e, i=i, xs=xs, ss=ss, Xi=Xi: e.activation(
            out=xs[:, :D], in_=Xi, func=AF.Copy, scale=ss[:, 2:3])),
            reads=[xik, ssk], writes=[xsk])
        for k0 in range(0, KT, 8):
            nk = min(8, KT - k0)
            (pb, pbk) = cm.psb.next()
            for j in range(nk):
                kb.add("pe", (lambda e, j=j, k0=k0, pb=pb, xs=xs: e.transpose(
                    out=pb[:, j * 128:(j + 1) * 128], in_=xs[:, (k0 + j) * 128:(k0 + j + 1) * 128],
                    identity=cm.ident[:])),
                    reads=[xsk, "ident"], writes=[pbk])
            kb.add("dve", (lambda e, k0=k0, nk=nk, pb=pb, i=i: e.tensor_tensor(
                out=xnT[:, k0:k0 + nk, i * 128:(i + 1) * 128],
                in0=pb[:, :nk * 128].rearrange("p (k t) -> p k t", t=128),
                in1=gT[:, k0:k0 + nk].unsqueeze(2).to_broadcast([128, nk, 128]),
                op=ALU.mult)),
                reads=[pbk, gkey], writes=[(xnkey, i)])


def emit_ffn(kb, cm, X, xkey, NT, D, F, g_dram, w_in, w_out, res_scale, bufs):
    T = NT * 128
    KT = D // 128
    FC = F // 128
    xnT, gT, gTb, scr = bufs["xnT"], bufs["gT"], bufs["gTb"], bufs["scr"]
    CG = bufs["CG"]
    assert FC % CG == 0
    NG = FC // CG
    gkey = "gTvec"
    kb.add("sp", lambda e: e.dma_start(out=gT[:, :KT], in_=g_dram), writes=[gkey], dma_key="gT")
    emit_norm_T(kb, cm, X, xkey, NT, D, gT, gkey, xnT, "xnT", scr)
    xn_reads = [("xnT", i) for i in range(NT)]
    w_in_v = w_in.rearrange("(kt p) n -> p kt n", p=128)
    w_out_v = w_out.rearrange("(c p) n -> p c n", p=128)
    TH = (T + 511) // 512
    NCG = D // 512
    for g in range(NG):
        for cc in range(CG):
            c = g * CG + cc
            wg, wgk = scr["win"].next()
            wu, wuk = scr["win"].next()
            wgkeys = load_w_cols(kb, wg, wgk, w_in_v, KT, [(0, c * 128, 128)])
            wukeys = load_w_cols(kb, wu, wuk, w_in_v, KT, [(0, F + c * 128, 128)])
            for th in range(TH):
                t0 = th * 512
                tn = min(512, T - t0)
                pg, pgk = cm.psf.next()
                pu, puk = cm.psf.next()
                for (wt, wks, pp, ppk) in ((wg, wgkeys, pg, pgk), (wu, wukeys, pu, puk)):
                    for kt in range(KT):
                        kb.add("pe", (lambda e, wt=wt, pp=pp, kt=kt, t0=t0, tn=tn: e.matmul(
                            pp[:, :tn], lhsT=wt[:, kt, :], rhs=xnT[:, kt, t0:t0 + tn],
                            start=(kt == 0), stop=(kt == KT - 1))),
                            reads=wks + xn_reads, writes=[ppk])
                sg, sgk = scr["sg"].next()
                kb.add("act", (lambda e, sg=sg, pg=pg, tn=tn: e.activation(
                    out=sg[:, :tn], in_=pg[:, :tn], func=AF.Silu)),
                    reads=[pgk], writes=[sgk])
                kb.add("dve", (lambda e, sg=sg, pu=pu, cc=cc, t0=t0, tn=tn: e.tensor_tensor(
                    out=gTb[:, cc, t0:t0 + tn], in0=sg[:, :tn], in1=pu[:, :tn], op=ALU.mult)),
                    reads=[sgk, puk], writes=[("gTb", cc)])
        for cg in range(NCG):
            wo, wok = scr["wout"].next()
            wokeys = load_w_cols(kb, wo, wok, w_out_v[:, g * CG:(g + 1) * CG, :], CG,
                                 [(0, cg * 512, 512)], kgrp=4)
            for i in range(NT):
                po, pok = cm.psf.next()
                for cc in range(CG):
                    kb.add("pe", (lambda e, po=po, wo=wo, cc=cc, i=i: e.matmul(
                        po[:, :], lhsT=gTb[:, cc, i * 128:(i + 1) * 128], rhs=wo[:, cc, :],
                        start=(cc == 0), stop=(cc == CG - 1))),
                        reads=[("gTb", cc)] + wokeys, writes=[pok])
                kb.add("dve", (lambda e, po=po, i=i, cg=cg: e.scalar_tensor_tensor(
                    out=X[:, i, cg * 512:(cg + 1) * 512], in0=po[:, :], scalar=res_scale,
                    in1=X[:, i, cg * 512:(cg + 1) * 512], op0=ALU.mult, op1=ALU.add)),
                    reads=[pok, (xkey, i)], writes=[(xkey, i)])


def make_ffn_bufs(kb, NT, D, F, CG):
    T = NT * 128
    KT = D // 128
    scr = {
        "sq": Rot([(kb.sb([128, D], BF16), "sq0")]),
        "ss": Rot([(kb.sb([128, 4], F32), "ss%d" % i) for i in range(2)]),
        "xs": Rot([(kb.sb([128, D], BF16), "xs%d" % i) for i in range(1)]),
        "win": Rot([(kb.sb([128, KT, 128], BF16), "win%d" % i) for i in range(4)]),
        "wout": Rot([(kb.sb([128, max(CG, KT), 512], BF16), "wout%d" % i) for i in range(2)]),
        "sg": Rot([(kb.sb([128, 512], F32), "sg%d" % i) for i in range(2)]),
    }
    return {
        "xnT": kb.sb([128, KT, T], BF16, "xnT"),
        "gT": kb.sb([128, KT], F32, "gTvec"),
        "gTb": kb.sb([128, CG, T], BF16, "gTb"),
        "scr": scr,
        "CG": CG,
    }


def ident_np():
    return np.eye(128, dtype=np.float32).astype(NPBF)


def lay_vec(g):
    g = np.asarray(g, dtype=np.float32)
    return np.ascontiguousarray(g.reshape(-1, 128).T)


def wview(W):
    return W.rearrange("(kt p) n -> p kt n", p=128)


def load_w_cols(kb, wt, wk, Wv, KT, segs, kgrp=8):
    for k0 in range(0, KT, kgrp):
        nk = min(kgrp, KT - k0)
        for (d0, s0, n) in segs:
            kb.add("pool", (lambda e, k0=k0, nk=nk, d0=d0, s0=s0, n=n: e.dma_start(
                out=wt[:, k0:k0 + nk, d0:d0 + n], in_=Wv[:, k0:k0 + nk, s0:s0 + n])),
                writes=[(wk, k0, d0)], dma_key=(wk, k0))
    return [(wk, k0, d0) for k0 in range(0, KT, kgrp) for (d0, s0, n) in segs]


def mm_fm(kb, cm, wt, wkeys, m, aT, akeys, KT, t0, tn):
    pp, ppk = cm.psf.next()
    for kt in range(KT):
        kb.add("pe", (lambda e, kt=kt, pp=pp: e.matmul(
            pp[:m, :tn], lhsT=wt[:, kt, :m], rhs=aT[:, kt, t0:t0 + tn],
            start=(kt == 0), stop=(kt == KT - 1))),
            reads=wkeys + akeys, writes=[ppk])
    return pp, ppk


def mm_tm(kb, cm, wt, wkeys, n, aT, akey_i, KT, i):
    pp, ppk = cm.psf.next()
    for kt in range(KT):
        kb.add("pe", (lambda e, kt=kt, pp=pp: e.matmul(
            pp[:, :n], lhsT=aT[:, kt, i * 128:(i + 1) * 128], rhs=wt[:, kt, :n],
            start=(kt == 0), stop=(kt == KT - 1))),
            reads=wkeys + [akey_i], writes=[ppk])
    return pp, ppk


class Evac:
    def __init__(self, kb):
        self.kb = kb
        self.i = 0

    def copy(self, out, in_, reads, writes, scale=None):
        self.i += 1
        kb = self.kb
        if self.i % 2 == 0:
            if scale is None:
                kb.add("act", (lambda e: e.activation(out=out, in_=in_, func=AF.Copy)),
                       reads=reads, writes=writes)
            else:
                kb.add("act", (lambda e: e.activation(out=out, in_=in_, func=AF.Copy, scale=float(scale))),
                       reads=reads, writes=writes)
        else:
            if scale is None:
                kb.add("dve", (lambda e: e.tensor_copy(out=out, in_=in_)), reads=reads, writes=writes)
            else:
                kb.add("dve", (lambda e: e.tensor_scalar_mul(out=out, in0=in_, scalar1=float(scale))),
                       reads=reads, writes=writes)


def emit_rope_evac(kb, pa, pak, pb, pbk, m, tn, cosT, sinT, tkeys, t0, scr, out):
    t1, t1k = scr["r1"].next()
    t2, t2k = scr["r2"].next()
    kb.add("dve", (lambda e: e.tensor_tensor(out=t1[:m, :tn], in0=pa[:m, :tn], in1=cosT[:m, t0:t0 + tn], op=ALU.mult)),
           reads=[pak] + tkeys, writes=[t1k])
    kb.add("dve", (lambda e: e.tensor_tensor(out=t2[:m, :tn], in0=pb[:m, :tn], in1=sinT[:m, t0:t0 + tn], op=ALU.mult)),
           reads=[pbk] + tkeys, writes=[t2k])
    return t1, t1k, t2, t2k


ROPE_DBG = set()


def emit_proj_fm(kb, cm, ev, Wv, KT, aT, akeys, T, chunks, out_dram, tabs, scr):
    default_out = out_dram
    for ch in chunks:
        col0, m = ch["col0"], ch["m"]
        out_dram = ch.get("out", default_out)
        wt, wk = scr["win"].next()
        wkeys = load_w_cols(kb, wt, wk, Wv, KT, [(0, col0, m)])
        rope = ch["kind"] == "rope"
        if rope:
            hs = ch["half"]
            wt2, wk2 = scr["win"].next()
            kb.add("act", (lambda e, wt=wt, wt2=wt2, hs=hs: e.activation(
                out=wt2[:, :KT, 0:hs], in_=wt[:, :KT, hs:2 * hs], func=AF.Copy)),
                reads=wkeys, writes=[(wk2, k0, 0) for k0 in range(0, KT, 8)])
            kb.add("dve", (lambda e, wt=wt, wt2=wt2, hs=hs: e.tensor_copy(
                out=wt2[:, :KT, hs:2 * hs], in_=wt[:, :KT, 0:hs])),
                reads=wkeys, writes=[(wk2, k0, 0) for k0 in range(0, KT, 8)])
            wkeys2 = [(wk2, k0, 0) for k0 in range(0, KT, 8)]
        for t0 in range(0, T, 512):
            tn = min(512, T - t0)
            pa, pak = mm_fm(kb, cm, wt, wkeys, m, aT, akeys, KT, t0, tn)
            plain_key = []
            if ch["kind"] in ("copy", "scale") or ch.get("plain_row0") is not None:
                st, stk = scr["stg"].next()
                plain_key = [stk]
                sc = ch.get("scale")
                ev.copy(st[:m, :tn], pa[:m, :tn], [pak], [stk], scale=sc)
                r0 = ch["row0"] if not rope else ch["plain_row0"]
                od_ = ch.get("plain_out", out_dram) if rope else out_dram
                kb.add("sp", (lambda e, st=st, r0=r0, t0=t0, tn=tn, m=m, od_=od_: e.dma_start(
                    out=od_[r0:r0 + m, t0:t0 + tn], in_=st[:m, :tn])),
                    reads=[stk], dma_key=(stk, "st"))
            if rope:
                pb, pbk = mm_fm(kb, cm, wt2, wkeys2, m, aT, akeys, KT, t0, tn)
                cosT, sinT, tkeys = tabs[ch["tab"]]
                st, stk = scr["stg"].next()
                if "nomul" in ROPE_DBG:
                    ev.copy(st[:m, :tn], pb[:m, :tn], [pbk], [stk])
                else:
                    t1, t1k, t2, t2k = emit_rope_evac(kb, pa, pak, pb, pbk, m, tn, cosT, sinT, tkeys + plain_key, t0, scr, None)
                    kb.add("pool", (lambda e, st=st, t1=t1, t2=t2, m=m, tn=tn: e.tensor_tensor(
                        out=st[:m, :tn], in0=t1[:m, :tn], in1=t2[:m, :tn], op=ALU.add)),
                        reads=[t1k, t2k], writes=[stk])
                r0 = ch["row0"]
                kb.add("sp", (lambda e, st=st, r0=r0, t0=t0, tn=tn, m=m, od_=out_dram: e.dma_start(
                    out=od_[r0:r0 + m, t0:t0 + tn], in_=st[:m, :tn])),
                    reads=[stk], dma_key=(stk, "st"))


def emit_proj_tm(kb, cm, ev, Wv, KT, aT, akey_fn, NT, groups, scr):
    for gp in groups:
        wt, wk = scr["wout"].next()
        wkeys = load_w_cols(kb, wt, wk, Wv, KT, gp["segs"], kgrp=4)
        n = gp["n"]
        for i in range(NT):
            pp, ppk = mm_tm(kb, cm, wt, wkeys, n, aT, akey_fn(i), KT, i)
            if gp.get("sink") is not None:
                gp["sink"](i, pp, ppk)
                continue
            if gp["dt"] == "bf16":
                st, stk = scr["stg"].next()
            else:
                st, stk = scr["stgf"].next()
            ev.copy(st[:, :n], pp[:, :n], [ppk], [stk])
            od, c0 = gp["out"]
            kb.add("sp", (lambda e, st=st, od=od, c0=c0, i=i, n=n: e.dma_start(
                out=od[i * 128:(i + 1) * 128, c0:c0 + n], in_=st[:, :n])),
                reads=[stk], dma_key=(stk, "st"))


def make_proj_scr(kb, scr):
    scr["stg"] = Rot([(kb.sb([128, 512], BF16), "stg%d" % i) for i in range(4)])
    scr["stgf"] = Rot([(kb.sb([128, 512], F32), "stgf%d" % i) for i in range(1)])
    scr["r1"] = Rot([(kb.sb([128, 512], F32), "r1_%d" % i) for i in range(1)])
    scr["r2"] = Rot([(kb.sb([128, 512], F32), "r2_%d" % i) for i in range(1)])


def load_tab(kb, shape, dram, name):
    t = kb.sb(shape, dram.dtype, name)
    kb.add("sp", (lambda e: e.dma_start(out=t[:], in_=dram)), writes=[name], dma_key=name)
    return t


HYB_IN = 5664
A_FM_ROWS = 5120
SC128 = 128.0 ** -0.5
SC192 = 192.0 ** -0.5


def build_A(NT=8, dbg=()):
    T = NT * 128
    D = D_MODEL
    KT = D // 128
    nc = bass.Bass("TRN2", target_bir_lowering=False)
    with contextlib.ExitStack() as st:
        kb = KB(nc, st)
        x_d = kb.dram("x", [T, D], F32, "ExternalInput")
        g1_d = kb.dram("g1", [128, KT], F32, "ExternalInput")
        wi_d = kb.dram("w_in", [D, 2 * D_FF], F32, "ExternalInput")
        wo_d = kb.dram("w_out", [D_FF, D], F32, "ExternalInput")
        gm_d = kb.dram("gm", [128, KT], F32, "ExternalInput")
        hw_d = kb.dram("hyb_w_in", [D, HYB_IN], F32, "ExternalInput")
        id_d = kb.dram("ident", [128, 128], BF16, "ExternalInput")
        tab_d = {n: kb.dram(n, [128, T], F32, "ExternalInput") for n in ("cosq", "sinq", "cosk", "sink")}
        x1_d = kb.dram("x1", [T, D], F32, "ExternalOutput")
        fm_d = kb.dram("fmA", [A_FM_ROWS, T], BF16, "ExternalOutput")
        gt_d = kb.dram("gatesT", [24, T], F32, "ExternalOutput")
        tm_d = kb.dram("tmA", [T, 1536], BF16, "ExternalOutput")
        fl_d = kb.dram("flog", [T, 8], F32, "ExternalOutput")

        cm = Common(kb, id_d)
        ev = Evac(kb)
        X = kb.sb([128, NT, D], F32, "X")
        bufs = make_ffn_bufs(kb, NT, D, D_FF, 11)
        scr = bufs["scr"]
        make_proj_scr(kb, scr)
        tabs_sb = {n: load_tab(kb, [128, T], tab_d[n], "t_" + n) for n in tab_d}
        tabs = {"q": (tabs_sb["cosq"], tabs_sb["sinq"], ["t_cosq", "t_sinq"]),
                "k": (tabs_sb["cosk"], tabs_sb["sink"], ["t_cosk", "t_sink"])}
        emit_load_x(kb, X, "X", x_d, NT)
        if "noffn" not in dbg:
            emit_ffn(kb, cm, X, "X", NT, D, D_FF, g1_d, wi_d, wo_d, 0.5, bufs)
        emit_store_x(kb, X, "X", x1_d, NT)
        xnT, gT = bufs["xnT"], bufs["gT"]
        kb.add("sp", lambda e: e.dma_start(out=gT[:, :KT], in_=gm_d), writes=["gTvec"], dma_key="gT")
        emit_norm_T(kb, cm, X, "X", NT, D, gT, "gTvec", xnT, "xnT", scr)
        akeys = [("xnT", i) for i in range(NT)]
        Wv = wview(hw_d)
        chunks = []
        for h in range(8):
            chunks.append(dict(col0=h * 128, m=128, kind="rope", half=64, tab="q",
                               row0=1024 + h * 128, plain_row0=h * 128, scale=SC128))
        for j in range(2):
            chunks.append(dict(col0=1024 + j * 128, m=128, kind="copy", row0=2048 + j * 128))
            chunks.append(dict(col0=1280 + j * 128, m=128, kind="copy", row0=2304 + j * 128))
            chunks.append(dict(col0=1536 + j * 128, m=128, kind="rope", half=64, tab="k", row0=2560 + j * 128))
            chunks.append(dict(col0=2048 + j * 128, m=128, kind="rope", half=64, tab="k", row0=2816 + j * 128))
        for h in range(8):
            chunks.append(dict(col0=2584 + h * 128, m=128, kind="scale", scale=SC128, row0=3072 + h * 128))
            chunks.append(dict(col0=3608 + h * 128, m=128, kind="copy", row0=4096 + h * 128))
        if "nofm" in dbg:
            chunks = []
        if "fm1" in dbg:
            chunks = chunks[8:]
        if "fm2" in dbg:
            chunks = chunks[:8]
        if "fmcopy" in dbg:
            chunks = [c for c in chunks if c["kind"] != "rope"]
        emit_proj_fm(kb, cm, ev, Wv, KT, xnT, akeys, T, chunks, fm_d, tabs, scr)
        wt, wk = scr["win"].next()
        wkeys = load_w_cols(kb, wt, wk, Wv, KT, [(0, 2560, 24)])
        for t0 in range(0, T if "nogates" not in dbg else 0, 512):
            pa, pak = mm_fm(kb, cm, wt, wkeys, 24, xnT, akeys, KT, t0, 512)
            sg, sgk = scr["stgf"].next()
            kb.add("act", (lambda e, sg=sg, pa=pa: e.activation(out=sg[:24, :], in_=pa[:24, :], func=AF.Sigmoid)),
                   reads=[pak], writes=[sgk])
            kb.add("sp", (lambda e, sg=sg, t0=t0: e.dma_start(out=gt_d[:, t0:t0 + 512], in_=sg[:24, :])),
                   reads=[sgk], dma_key=(sgk, "st"))
        groups = [
            dict(segs=[(0, 1792, 256), (256, 2304, 256)], n=512, out=(tm_d, 0), dt="bf16"),
            dict(segs=[(0, 4632, 512)], n=512, out=(tm_d, 512), dt="bf16"),
            dict(segs=[(0, 5144, 512)], n=512, out=(tm_d, 1024), dt="bf16"),
            dict(segs=[(0, 5656, 8)], n=8, out=(fl_d, 0), dt="f32"),
        ]
        if "notm" in dbg:
            groups = []
        if "noflog" in dbg:
            groups = groups[:3]
        emit_proj_tm(kb, cm, ev, Wv, KT, xnT, lambda i: ("xnT", i), NT, groups, scr)
        kb.finish()
    return nc


def rope_tables(d, pos, scale=1.0):
    inv = 1.0 / (10000.0 ** (np.arange(0, d, 2, dtype=np.float32) / d))
    ang = pos.astype(np.float32)[None, :] * inv.astype(np.float32)[:, None]
    c = np.cos(ang).astype(np.float32)
    s = np.sin(ang).astype(np.float32)
    cosf = np.concatenate([c, c], 0) * np.float32(scale)
    sinf = np.concatenate([-s, s], 0) * np.float32(scale)
    return np.ascontiguousarray(cosf.astype(np.float32)), np.ascontiguousarray(sinf.astype(np.float32))


NEG = -30000.0


class AttnRes:
    def __init__(self, kb, cm):
        self.sc = Rot(cm.psf_list[0:2])
        self.O = Rot(cm.psf_list[2:4])
        self.S = Rot(cm.psf_list[4:6])
        self.misc = cm.psf_list[6]
        self.pt = Rot([(kb.sb([128, 512], BF16), "pt%d" % i) for i in range(3)])
        self.ones = kb.sb([128, 128], BF16, "ones")
        kb.add("dve", lambda e: e.memset(self.ones[:], 1.0), writes=["ones"])
        self.rs = Rot([(kb.sb([128, 512], F32), "rs%d" % i) for i in range(2)])
        self.ost = Rot([(kb.sb([128, 512], BF16), "ost%d" % i) for i in range(2)])


def attn_qblock(kb, R, kts, score_fn, v_fn, np_fn, dv=128):
    po, pok = R.O.next()
    psm, psmk = R.S.next()
    n = len(kts)
    for idx, kt in enumerate(kts):
        sc, sck = R.sc.next()
        terms = score_fn(kt)
        npart = np_fn(kt)
        nt = len(terms)
        for ti, (l, r, rd) in enumerate(terms):
            kb.add("pe", (lambda e, sc=sc, l=l, r=r, ti=ti, nt=nt, npart=npart: e.matmul(
                sc[:npart, :], lhsT=l, rhs=r, start=(ti == 0), stop=(ti == nt - 1))),
                reads=rd, writes=[sck])
        pt, ptk = R.pt.next()
        kb.add("act", (lambda e, pt=pt, sc=sc, npart=npart: e.activation(
            out=pt[:npart, :], in_=sc[:npart, :], func=AF.Exp)),
            reads=[sck], writes=[ptk])
        v, vr = v_fn(kt)
        kb.add("pe", (lambda e, po=po, v=v, pt=pt, idx=idx, npart=npart: e.matmul(
            po[:dv, :], lhsT=v, rhs=pt[:npart, :], start=(idx == 0), stop=(idx == n - 1))),
            reads=[ptk] + vr, writes=[pok])
        kb.add("pe", (lambda e, psm=psm, pt=pt, idx=idx, npart=npart: e.matmul(
            psm[:, :], lhsT=R.ones[:npart, :], rhs=pt[:npart, :], start=(idx == 0), stop=(idx == n - 1))),
            reads=[ptk, "ones"], writes=[psmk])
    return po, pok, psm, psmk


def finalize_plain(kb, R, po, pok, psm, psmk, out_dram, row0, q0):
    rs, rsk = R.rs.next()
    kb.add("dve", (lambda e: e.reciprocal(out=rs[:, :], in_=psm[:, :])), reads=[psmk], writes=[rsk])
    ost, ostk = R.ost.next()
    kb.add("dve", (lambda e: e.tensor_tensor(out=ost[:, :], in0=po[:, :], in1=rs[:, :], op=ALU.mult)),
           reads=[pok, rsk], writes=[ostk])
    kb.add("sp", (lambda e: e.dma_start(out=out_dram[row0:row0 + 128, q0:q0 + 512], in_=ost[:, :])),
           reads=[ostk], dma_key=(ostk, "st"))


def load_fm(kb, tile, key, dram, nchunk, rows=128):
    v = dram.rearrange("(c p) t -> p c t", p=rows)
    for c in range(nchunk):
        kb.add("sp", (lambda e, c=c: e.dma_start(out=tile[:rows, c, :], in_=v[:, c, :])),
               writes=[(key, c)], dma_key=(key, c % 4))


def load_tm(kb, tile, key, dram, ntile):
    v = dram.rearrange("(i p) n -> p i n", p=128)
    for i in range(ntile):
        kb.add("sp", (lambda e, i=i: e.dma_start(out=tile[:, i, :], in_=v[:, i, :])),
               writes=[(key, i)], dma_key=(key, i % 4))


def neg_masks_np():
    k = np.arange(128)[:, None, None]
    j = np.arange(4)[None, :, None]
    q = np.arange(512)[None, None, :]
    negC = np.where(q >= 128 * j + k, 0.0, NEG).astype(np.float32).astype(NPBF)
    negW = np.where(q < 128 * j + k, 0.0, NEG).astype(np.float32).astype(NPBF)
    return np.ascontiguousarray(negC), np.ascontiguousarray(negW)


def build_D(S=SEQ, NH=8):
    nc = bass.Bass("TRN2", target_bir_lowering=False)
    NKT = S // 128
    NQB = S // 512
    with contextlib.ExitStack() as st:
        kb = KB(nc, st)
        qn_d = kb.dram("qn", [NH * 128, S], BF16, "ExternalInput")
        qr_d = kb.dram("qr", [NH * 64, S], BF16, "ExternalInput")
        kn_d = kb.dram("kn", [NH * 128, S], BF16, "ExternalInput")
        kr_d = kb.dram("kr", [64, S], BF16, "ExternalInput")
        v_d = kb.dram("v", [S, NH * 128], BF16, "ExternalInput")
        negC_d = kb.dram("negC", [128, 4, 512], BF16, "ExternalInput")
        id_d = kb.dram("ident", [128, 128], BF16, "ExternalInput")
        o_d = kb.dram("oT", [NH * 128, S], BF16, "ExternalOutput")
        cm = Common(kb, id_d, nf=7, nb=1)
        R = AttnRes(kb, cm)
        QN = kb.sb([128, NH, S], BF16, "QN")
        QR = kb.sb([64, NH, S], BF16, "QR")
        KN = kb.sb([128, NH, S], BF16, "KN")
        KR = kb.sb([64, 1, S], BF16, "KR")
        V = kb.sb([128, NKT, NH * 128], BF16, "V")
        negC = kb.sb([128, 4, 512], BF16, "negC")
        kb.add("sp", lambda e: e.dma_start(out=negC[:], in_=negC_d), writes=["negC"], dma_key="negC")
        load_fm(kb, KR, "KR", kr_d, 1, rows=64)
        for h in range(NH):
            kb.add("sp", (lambda e, h=h: e.dma_start(out=KN[:, h, :], in_=kn_d[h * 128:(h + 1) * 128, :])),
                   writes=[("KN", h)], dma_key=("KN", h % 4))
            kb.add("sp", (lambda e, h=h: e.dma_start(out=QN[:, h, :], in_=qn_d[h * 128:(h + 1) * 128, :])),
                   writes=[("QN", h)], dma_key=("QN", h % 4))
            kb.add("sp", (lambda e, h=h: e.dma_start(out=QR[:, h, :], in_=qr_d[h * 64:(h + 1) * 64, :])),
                   writes=[("QR", h)], dma_key=("QR", h % 4))
            if h == 0:
                load_tm(kb, V, "V", v_d, NKT)
        vkeys = [("V", i) for i in range(NKT)]
        for h in range(NH):
            for qb in range(NQB):
                q0 = qb * 512

                def score_fn(kt, h=h, qb=qb, q0=q0):
                    t = [(KN[:, h, kt * 128:(kt + 1) * 128], QN[:, h, q0:q0 + 512], [("KN", h), ("QN", h)]),
                         (KR[:, 0, kt * 128:(kt + 1) * 128], QR[:, h, q0:q0 + 512], [("KR", 0), ("QR", h)])]
                    j = kt - 4 * qb
                    if j >= 0:
                        t.append((cm.ident[:], negC[:, j, :], ["ident", "negC"]))
                    return t

                def v_fn(kt, h=h):
                    return V[:, kt, h * 128:(h + 1) * 128], [("V", kt)]

                po, pok, psm, psmk = attn_qblock(kb, R, list(range(4 * qb + 4)), score_fn, v_fn, lambda kt: 128)
                finalize_plain(kb, R, po, pok, psm, psmk, o_d, h * 128, q0)
        kb.finish()
    return nc


def consts_B_np(S=SEQ):
    negC, negW = neg_masks_np()
    n = np.arange(128)[:, None]
    t = np.arange(S)[None, :]
    negcm = np.where((16 * n + 31 <= t) & (n < 127), 0.0, NEG).astype(np.float32).astype(NPBF)
    n_cmp, n_slc = 127, S // 64
    c0 = np.arange(n_cmp) * 16
    c1 = c0 + 32
    s0 = np.arange(n_slc) * 64
    s1 = s0 + 64
    ovm = np.clip(np.minimum(c1[:, None], s1[None, :]) - np.maximum(c0[:, None], s0[None, :]), 0, None) / 32.0
    ov = np.zeros((128, n_slc), np.float32)
    ov[:127] = ovm
    E = (np.arange(S)[None, :] // 64 == np.arange(n_slc)[:, None]).astype(np.float32)
    tt = np.arange(S)
    cur = tt // 64
    j = np.arange(n_slc)[None, :]
    forced = (j == 0) | (j == cur[:, None]) | (j == cur[:, None] - 1)
    fbv = np.where(j > cur[:, None], -1e9, np.where(forced, 1e9, 0.0)).astype(np.float32)
    fb = np.ascontiguousarray(fbv.reshape(S // 128, 128, n_slc).transpose(1, 0, 2))
    sel4 = np.zeros((4, 4 * 128), np.float32)
    selrowneg = np.zeros((4, 4 * 512), np.float32)
    for h in range(4):
        sel4[h, h * 128:(h + 1) * 128] = 1.0
        selrowneg[h, h * 512:(h + 1) * 512] = -1.0
    selg = np.zeros((12, 12 * 128), np.float32)
    for i in range(12):
        selg[i, i * 128:(i + 1) * 128] = 1.0
    bf = lambda a: np.ascontiguousarray(a.astype(np.float32).astype(NPBF))
    return {"negC": negC, "negW": negW, "negcm": np.ascontiguousarray(negcm), "ov": bf(ov), "E": bf(E), "fb": fb,
            "sel4": bf(sel4), "selrowneg": bf(selrowneg), "selg": bf(selg), "ident": ident_np()}


def split3(kb, src, skey, npart, S, name):
    outs = []
    cur, curk = src, skey
    for i in range(3):
        c = kb.sb([npart, S], BF16, "%s_c%d" % (name, i))
        ck = "%s_c%d" % (name, i)
        kb.add("dve", (lambda e, c=c, cur=cur: e.tensor_copy(out=c[:, :], in_=cur[:, :])), reads=[curk], writes=[ck])
        outs.append((c, ck))
        if i < 2:
            r = kb.sb([npart, S], F32, "%s_r%d" % (name, i))
            rk = "%s_r%d" % (name, i)
            kb.add("dve", (lambda e, r=r, cur=cur, c=c: e.tensor_tensor(out=r[:, :], in0=cur[:, :], in1=c[:, :], op=ALU.subtract)),
                   reads=[curk, ck], writes=[rk])
            cur, curk = r, rk
    return outs


def _build_B_fox(kb, nc, cm, R, C, FM, TM, fl_d, fbias_d, o_d, S, QF, KF):
    NQB = S // 512
    ident = cm.ident
    negC = C["negC"]
    one1 = kb.sb([128, 1], F32, "one1f")
    kb.add("dve", lambda e: e.memset(one1[:], 1.0), writes=["one1"])
    FL = kb.sb([4, S], F32, "FL")
    FBI = kb.sb([4, 1], F32, "FBI")
    kb.add("sp", lambda e: e.dma_start(out=FL[:], in_=fl_d), writes=["FL"], dma_key="FL")
    kb.add("sp", lambda e: e.dma_start(out=FBI[:], in_=fbias_d), writes=["FBI"], dma_key="FBI")
    Z = kb.sb([4, S], F32, "Z")
    ONES4 = kb.sb([4, S], F32, "ONES4")
    CUM = kb.sb([4, S], F32, "CUM")
    kb.add("dve", lambda e: e.memset(ONES4[:], 1.0), writes=["ONES4"])
    kb.add("dve", lambda e: e.tensor_scalar(out=Z[:, :], in0=FL[:, :], scalar1=FBI[:, 0:1], scalar2=None, op0=ALU.add),
           reads=["FL", "FBI"], writes=["Z"])
    kb.add("act", lambda e: e.activation(out=Z[:, :], in_=Z[:, :], func=AF.Exp, scale=-1.0), reads=["Z"], writes=["Z"])
    kb.add("act", lambda e: e.activation(out=Z[:, :], in_=Z[:, :], func=AF.Ln, bias=one1[:4, 0:1]),
           reads=["Z", "one1"], writes=["Z"])
    kb.add("dve", lambda e: e.tensor_scalar_mul(out=Z[:, :], in0=Z[:, :], scalar1=-1.0), reads=["Z"], writes=["Z"])
    kb.add("dve", lambda e: e.tensor_tensor_scan(out=CUM[:, :], data0=ONES4[:, :], data1=Z[:, :], initial=0.0,
                                                 op0=ALU.mult, op1=ALU.add),
           reads=["Z", "ONES4"], writes=["CUM"])
    cparts = split3(kb, CUM, "CUM", 4, S, "cum")
    sel4, selrowneg = C["sel4"], C["selrowneg"]

    def fox_head(h):
        for qb in range(NQB):
            q0 = qb * 512

            def score_fn(kt, h=h, qb=qb, q0=q0):
                t = [(FM[:, KF + h, kt * 128:(kt + 1) * 128], FM[:, QF + h, q0:q0 + 512], [("FM", KF + h), ("FM", QF + h)])]
                for (c, ck) in cparts:
                    t.append((sel4[:, h * 128:(h + 1) * 128], c[:, q0:q0 + 512], ["c_sel4", ck]))
                    t.append((c[:, kt * 128:(kt + 1) * 128], selrowneg[:, h * 512:(h + 1) * 512], ["c_selrowneg", ck]))
                j = kt - 4 * qb
                if j >= 0:
                    t.append((ident[:], negC[:, j, :], ["ident", "c_negC"]))
                return t

            def v_fn(kt, h=h):
                return TM[:, kt, h * 128:(h + 1) * 128], [("TM", kt)]

            po, pok, psm, psmk = attn_qblock(kb, R, list(range(4 * qb + 4)), score_fn, v_fn, lambda kt: 128)
            finalize_plain(kb, R, po, pok, psm, psmk, o_d, h * 128, q0)


    for h in range(4):
        fox_head(h)
    kb.finish()
    return nc


def build_B(mode, S=SEQ, dbg=()):
    nc = bass.Bass("TRN2", target_bir_lowering=False)
    NKT = S // 128
    NQB = S // 512
    with contextlib.ExitStack() as st:
        kb = KB(nc, st)
        NCH = 12 if mode == "nsa" else 8
        TMW = 256 if mode == "nsa" else 512
        fm_d = kb.dram("fmB", [NCH * 128, S], BF16, "ExternalInput")
        tm_d = kb.dram("tmB", [S, TMW], BF16, "ExternalInput")
        g_d = fl_d = fbias_d = pe_d = w1_d = w2_d = None
        if mode == "nsa":
            g_d = kb.dram("gates", [12, S], F32, "ExternalInput")
            pe_d = kb.dram("peT", [128, 64], F32, "ExternalInput")
            w1_d = kb.dram("w1", [2, 4096, 256], F32, "ExternalInput")
            w2_d = kb.dram("w2", [2, 256, 128], F32, "ExternalInput")
            cd = {"negC": ([128, 4, 512], BF16), "negW": ([128, 4, 512], BF16), "negcm": ([128, S], BF16),
                  "ov": ([128, 32], BF16), "E": ([32, S], BF16), "fb": ([128, NKT, 32], F32), "selg": ([12, 1536], BF16)}
        else:
            fl_d = kb.dram("flogT", [4, S], F32, "ExternalInput")
            fbias_d = kb.dram("fbias", [4, 1], F32, "ExternalInput")
            cd = {"negC": ([128, 4, 512], BF16), "sel4": ([4, 512], BF16), "selrowneg": ([4, 2048], BF16)}
        c_d = {n: kb.dram(n, shp, dt, "ExternalInput") for n, (shp, dt) in cd.items()}
        id_d = kb.dram("ident", [128, 128], BF16, "ExternalInput")
        o_d = kb.dram("oT", [512, S], BF16, "ExternalOutput")
        selT_d = kb.dram("selT", [32, S], BF16, "ExternalOutput") if (mode == "nsa" and "dbgsel" in dbg) else None
        imp_d = kb.dram("impd", [S, 32], F32, "ExternalOutput") if (mode == "nsa" and "dbgsel" in dbg) else None
        cm = Common(kb, id_d, nf=7, nb=1)
        R = AttnRes(kb, cm)
        C = {n: load_tab(kb, cd[n][0], c_d[n], "c_" + n) for n in cd}
        FM = kb.sb([128, NCH, S], BF16, "FM")
        TM = kb.sb([128, NKT, TMW], BF16, "TM")
        load_fm(kb, FM, "FM", fm_d, NCH)
        load_tm(kb, TM, "TM", tm_d, NKT)
        QP, QR, KC, VC, KS, KW, QF, KF = 0, 4, 8, 9, 10, 11, 0, 4
        ident = cm.ident
        tiny = kb.sb([128, 1], F32, "tiny")
        one1 = kb.sb([128, 1], F32, "one1")
        kb.add("dve", lambda e: e.memset(tiny[:], 1e-30), writes=["tiny"])
        kb.add("dve", lambda e: e.memset(one1[:], 1.0), writes=["one1"])

        if mode == "fox":
            return _build_B_fox(kb, nc, cm, R, C, FM, TM, fl_d, fbias_d, o_d, S, QF, KF)
        G = kb.sb([12, S], F32, "G")
        kb.add("sp", lambda e: e.dma_start(out=G[:], in_=g_d), writes=["G"], dma_key="G")
        ghi = kb.sb([12, S], BF16, "ghi")
        gres = kb.sb([12, S], F32, "gres")
        glo = kb.sb([12, S], BF16, "glo")
        kb.add("dve", lambda e: e.tensor_copy(out=ghi[:, :], in_=G[:, :]), reads=["G"], writes=["ghi"])
        kb.add("dve", lambda e: e.tensor_tensor(out=gres[:, :], in0=G[:, :], in1=ghi[:, :], op=ALU.subtract),
               reads=["G", "ghi"], writes=["gres"])
        kb.add("dve", lambda e: e.tensor_copy(out=glo[:, :], in_=gres[:, :]), reads=["gres"], writes=["glo"])

        negC, negW, negcm, ovT, Em, fbT = C["negC"], C["negW"], C["negcm"], C["ov"], C["E"], C["fb"]
        selg = C["selg"]
        PE_ = kb.sb([128, 64], F32, "PEt")
        kb.add("sp", lambda e: e.dma_start(out=PE_[:], in_=pe_d), writes=["PEt"], dma_key="PEt")
        W1 = kb.sb([128, 32, 256], BF16, "W1")
        W2 = kb.sb([128, 2, 128], BF16, "W2")
        XB = kb.sb([128, 32, 128], BF16, "XB")
        HX = kb.sb([128, 128], F32, "HX")
        H2 = kb.sb([128, 128], F32, "H2")
        HT = kb.sb([128, 2, 128], BF16, "HT")
        KCC = kb.sb([128, 128], BF16, "KCC")
        VCC = kb.sb([128, 128], BF16, "VCC")
        for which in range(2):
            src = KC if which == 0 else VC
            w1keys = load_w_cols(kb, W1, "W1", w1_d[which].rearrange("(l p) n -> p l n", p=128), 32, [(0, 0, 256)])
            w2keys = load_w_cols(kb, W2, "W2", w2_d[which].rearrange("(c p) n -> p c n", p=128), 2, [(0, 0, 128)])
            for l in range(32):
                kb.add("dve", (lambda e, l=l, src=src, which=which: e.tensor_scalar(
                    out=XB[:, l, :127], in0=FM[:, src, l:l + 16 * 126 + 1:16],
                    scalar1=PE_[:, which * 32 + l:which * 32 + l + 1], scalar2=None, op0=ALU.add)),
                    reads=[("FM", src), "PEt"], writes=[("XB", l)])
            for hc in range(2):
                ph, phk = R.sc.next()
                for l in range(32):
                    kb.add("pe", (lambda e, ph=ph, l=l, hc=hc: e.matmul(
                        ph[:, :127], lhsT=W1[:, l, hc * 128:(hc + 1) * 128], rhs=XB[:, l, :127],
                        start=(l == 0), stop=(l == 31))),
                        reads=w1keys + [("XB", l)], writes=[phk])
                kb.add("act", (lambda e, ph=ph: e.activation(out=HX[:, :127], in_=ph[:, :127], func=AF.Copy)),
                       reads=[phk], writes=["HX"])
                kb.add("dve", lambda e: e.tensor_tensor(out=H2[:, :127], in0=HX[:, :127], in1=HX[:, :127], op=ALU.mult),
                       reads=["HX"], writes=["H2"])
                kb.add("dve", lambda e: e.tensor_tensor(out=H2[:, :127], in0=H2[:, :127], in1=HX[:, :127], op=ALU.mult),
                       reads=["HX", "H2"], writes=["H2"])
                kb.add("dve", lambda e: e.scalar_tensor_tensor(out=H2[:, :127], in0=H2[:, :127], scalar=0.044715,
                                                               in1=HX[:, :127], op0=ALU.mult, op1=ALU.add),
                       reads=["HX", "H2"], writes=["H2"])
                kb.add("act", lambda e: e.activation(out=H2[:, :127], in_=H2[:, :127], func=AF.Sigmoid, scale=1.5957691216057308),
                       reads=["H2"], writes=["H2"])
                kb.add("dve", (lambda e, hc=hc: e.tensor_tensor(out=HT[:, hc, :127], in0=HX[:, :127], in1=H2[:, :127], op=ALU.mult)),
                       reads=["HX", "H2"], writes=[("HT", hc)])
            pc, pck = R.sc.next()
            if which == 0:
                for hc in range(2):
                    kb.add("pe", (lambda e, pc=pc, hc=hc: e.matmul(pc[:, :127], lhsT=W2[:, hc, :], rhs=HT[:, hc, :127],
                                                                   start=(hc == 0), stop=(hc == 1))),
                           reads=w2keys + [("HT", 0), ("HT", 1)], writes=[pck])
                kb.add("dve", (lambda e, pc=pc: e.tensor_copy(out=KCC[:, :127], in_=pc[:, :127])), reads=[pck], writes=["KCC"])
            else:
                for hc in range(2):
                    kb.add("pe", (lambda e, pc=pc, hc=hc: e.matmul(pc[:127, :128], lhsT=HT[:, hc, :127], rhs=W2[:, hc, :],
                                                                   start=(hc == 0), stop=(hc == 1))),
                           reads=w2keys + [("HT", 0), ("HT", 1)], writes=[pck])
                kb.add("dve", (lambda e, pc=pc: e.tensor_copy(out=VCC[:127, :], in_=pc[:127, :128])), reads=[pck], writes=["VCC"])

        ACC = kb.sb([128, 4, 512], F32, "ACC")
        TMPW = Rot([(kb.sb([128, 512], F32), "tmpw%d" % i) for i in range(2)])
        TMPO = Rot([(kb.sb([128, 512], F32), "tmpo%d" % i) for i in range(2)])
        PN = Rot([(kb.sb([128, 512], BF16), "pn%d" % i) for i in range(2)])
        IMP2 = kb.sb([128, 4, 32], F32, "IMP2")
        TOP8 = kb.sb([128, 4, 8], F32, "TOP8")
        NSEL = kb.sb([128, 4, 32], BF16, "NSEL")
        NSELT = kb.sb([32, 512], BF16, "NSELT")
        impP, impk = R.misc
        psb, psbk = cm.psb_list[0]

        def finalize_gated(po, pok, psm, psmk, h, br, q0, rs=None, rsk=None):
            if rs is None:
                rs, rsk = R.rs.next()
                kb.add("dve", (lambda e: e.tensor_scalar(out=rs[:, :], in0=psm[:, :], scalar1=tiny[:, 0:1], scalar2=None, op0=ALU.max)),
                       reads=[psmk, "tiny"], writes=[rsk])
                kb.add("dve", (lambda e: e.reciprocal(out=rs[:, :], in_=rs[:, :])), reads=[rsk], writes=[rsk])
            i = h * 3 + br
            for gi, (gt, gk) in enumerate(((ghi, "ghi"), (glo, "glo"))):
                kb.add("pe", (lambda e, gt=gt, gi=gi: e.matmul(psm[:, :], lhsT=selg[:, i * 128:(i + 1) * 128], rhs=gt[:, q0:q0 + 512],
                                                               start=(gi == 0), stop=(gi == 1))),
                       reads=["c_selg", gk, rsk], writes=[psmk])
            w, wk = TMPW.next()
            kb.add("dve", (lambda e: e.tensor_tensor(out=w[:, :], in0=psm[:, :], in1=rs[:, :], op=ALU.mult)),
                   reads=[psmk, rsk], writes=[wk])
            if br == 0:
                kb.add("dve", (lambda e: e.tensor_tensor(out=ACC[:, h, :], in0=po[:, :], in1=w[:, :], op=ALU.mult)),
                       reads=[pok, wk], writes=[("ACC", h)])
            else:
                to, tok = TMPO.next()
                kb.add("dve", (lambda e: e.tensor_tensor(out=to[:, :], in0=po[:, :], in1=w[:, :], op=ALU.mult)),
                       reads=[pok, wk], writes=[tok])
                kb.add("pool", (lambda e: e.tensor_tensor(out=ACC[:, h, :], in0=ACC[:, h, :], in1=to[:, :], op=ALU.add)),
                       reads=[tok, ("ACC", h)], writes=[("ACC", h)])

        def nsa_qblock(qb):
            q0 = qb * 512
            for h in range(4):
                sc, sck = R.sc.next()
                kb.add("pe", (lambda e, sc=sc, h=h: e.matmul(sc[:127, :], lhsT=KCC[:, :127], rhs=FM[:, QP + h, q0:q0 + 512],
                                                            start=True, stop=False)),
                       reads=["KCC", ("FM", QP + h)], writes=[sck])
                kb.add("pe", (lambda e, sc=sc: e.matmul(sc[:127, :], lhsT=ident[:127, :127], rhs=negcm[:127, q0:q0 + 512],
                                                       start=False, stop=True)),
                       reads=["ident", "c_negcm"], writes=[sck])
                pt, ptk = R.pt.next()
                kb.add("act", (lambda e, pt=pt, sc=sc: e.activation(out=pt[:127, :], in_=sc[:127, :], func=AF.Exp)),
                       reads=[sck], writes=[ptk])
                po, pok = R.O.next()
                psm, psmk = R.S.next()
                kb.add("pe", (lambda e, po=po, pt=pt: e.matmul(po[:, :], lhsT=VCC[:127, :], rhs=pt[:127, :], start=True, stop=True)),
                       reads=["VCC", ptk], writes=[pok])
                kb.add("pe", (lambda e, psm=psm, pt=pt: e.matmul(psm[:, :], lhsT=R.ones[:127, :], rhs=pt[:127, :], start=True, stop=True)),
                       reads=["ones", ptk], writes=[psmk])
                rs, rsk = R.rs.next()
                kb.add("dve", (lambda e, rs=rs, psm=psm: e.tensor_scalar(out=rs[:, :], in0=psm[:, :], scalar1=tiny[:, 0:1], scalar2=None, op0=ALU.max)),
                       reads=[psmk, "tiny"], writes=[rsk])
                kb.add("dve", (lambda e, rs=rs: e.reciprocal(out=rs[:, :], in_=rs[:, :])), reads=[rsk], writes=[rsk])
                pn, pnk = PN.next()
                kb.add("dve", (lambda e, pn=pn, pt=pt, rs=rs: e.tensor_tensor(out=pn[:127, :], in0=pt[:127, :], in1=rs[:127, :], op=ALU.mult)),
                       reads=[ptk, rsk], writes=[pnk])
                for s_ in range(4):
                    kb.add("pe", (lambda e, pn=pn, s_=s_, h=h: e.matmul(
                        impP[:, h * 128 + s_ * 32:h * 128 + (s_ + 1) * 32], lhsT=pn[:127, s_ * 128:(s_ + 1) * 128],
                        rhs=ovT[:127, :], start=True, stop=True)),
                        reads=[pnk, "c_ov"], writes=[impk])
                finalize_gated(po, pok, psm, psmk, h, 0, q0, rs, rsk)
            kb.add("dve", (lambda e: e.tensor_tensor(out=IMP2[:, :, :], in0=impP[:, :128].rearrange("p (s j) -> p s j", j=32),
                                                     in1=fbT[:, 4 * qb:4 * qb + 4, :], op=ALU.add)),
                   reads=[impk, "c_fb"], writes=["IMP2"])
            for h in range(1, 4):
                kb.add("dve", (lambda e, h=h: e.tensor_tensor(
                    out=IMP2[:, :, :], in0=impP[:, h * 128:(h + 1) * 128].rearrange("p (s j) -> p s j", j=32),
                    in1=IMP2[:, :, :], op=ALU.add)),
                    reads=[impk, "IMP2"], writes=["IMP2"])
            if imp_d is not None:
                kb.add("sp", (lambda e: e.dma_start(out=imp_d[q0:q0 + 512, :].rearrange("(s p) j -> p s j", p=128), in_=IMP2[:, :, :])),
                       reads=["IMP2"], dma_key="IMP2_st")
            for s_ in range(4):
                kb.add("dve", (lambda e, s_=s_: e.max(out=TOP8[:, s_, :], in_=IMP2[:, s_, :])), reads=["IMP2"], writes=[("TOP8", s_)])
                kb.add("dve", (lambda e, s_=s_: e.tensor_scalar(out=NSEL[:, s_, :], in0=IMP2[:, s_, :], scalar1=TOP8[:, s_, 7:8],
                                                                scalar2=NEG, op0=ALU.is_lt, op1=ALU.mult)),
                       reads=["IMP2", ("TOP8", s_)], writes=[("NSEL", s_)])
                kb.add("pe", (lambda e, s_=s_: e.transpose(out=psb[:32, s_ * 128:(s_ + 1) * 128], in_=NSEL[:, s_, :], identity=ident[:])),
                       reads=[("NSEL", s_), "ident"], writes=[psbk])
            kb.add("dve", (lambda e: e.tensor_copy(out=NSELT[:, :], in_=psb[:32, :512])), reads=[psbk], writes=["NSELT"])
            if selT_d is not None:
                kb.add("sp", (lambda e: e.dma_start(out=selT_d[:, q0:q0 + 512], in_=NSELT[:, :])), reads=["NSELT"], dma_key="NSELT_st")
            for h in range(4):
                def score_slc(kt, h=h):
                    t = [(FM[:, KS, kt * 128:(kt + 1) * 128], FM[:, QR + h, q0:q0 + 512], [("FM", KS), ("FM", QR + h)]),
                         (Em[:, kt * 128:(kt + 1) * 128], NSELT[:, :], ["c_E", "NSELT"])]
                    j = kt - 4 * qb
                    if j >= 0:
                        t.append((ident[:], negC[:, j, :], ["ident", "c_negC"]))
                    return t

                po, pok, psm, psmk = attn_qblock(kb, R, list(range(4 * qb + 4)), score_slc,
                                                 lambda kt: (TM[:, kt, 0:128], [("TM", kt)]), lambda kt: 128)
                finalize_gated(po, pok, psm, psmk, h, 1, q0)

                def score_win(kt, h=h):
                    t = [(FM[:, KW, kt * 128:(kt + 1) * 128], FM[:, QR + h, q0:q0 + 512], [("FM", KW), ("FM", QR + h)])]
                    j = kt - 4 * qb
                    if j >= 0:
                        t.append((ident[:], negC[:, j, :], ["ident", "c_negC"]))
                    else:
                        t.append((ident[:], negW[:, j + 4, :], ["ident", "c_negW"]))
                    return t

                po, pok, psm, psmk = attn_qblock(kb, R, list(range(max(0, 4 * qb - 4), 4 * qb + 4)), score_win,
                                                 lambda kt: (TM[:, kt, 128:256], [("TM", kt)]), lambda kt: 128)
                finalize_gated(po, pok, psm, psmk, h, 2, q0)
                ost, ostk = R.ost.next()
                kb.add("act", (lambda e, ost=ost, h=h: e.activation(out=ost[:, :], in_=ACC[:, h, :], func=AF.Copy)),
                       reads=[("ACC", h)], writes=[ostk])
                kb.add("sp", (lambda e, ost=ost, h=h: e.dma_start(out=o_d[h * 128:(h + 1) * 128, q0:q0 + 512], in_=ost[:, :])),
                       reads=[ostk], dma_key=(ostk, "st"))

        for qb in range(NQB):
            nsa_qblock(qb)
        kb.finish()
    return nc


def emit_load_aT(kb, aT, akey, a_dram, KT):
    v = a_dram.rearrange("(kt p) t -> p kt t", p=128)
    for kt in range(KT):
        kb.add("sp", (lambda e, kt=kt: e.dma_start(out=aT[:, kt, :], in_=v[:, kt, :])),
               writes=[(akey, "all")], dma_key=(akey, "ld"))


def emit_outproj(kb, cm, X, xkey, NT, aT, akey, KT, W_dram, scr):
    Wv = wview(W_dram)
    for cg in range(D_MODEL // 512):
        wt, wk = scr["wout"].next()
        wkeys = load_w_cols(kb, wt, wk, Wv, KT, [(0, cg * 512, 512)], kgrp=4)
        for i in range(NT):
            pp, ppk = mm_tm(kb, cm, wt, wkeys, 512, aT, (akey, "all"), KT, i)
            kb.add("dve", (lambda e, pp=pp, i=i, cg=cg: e.tensor_tensor(
                out=X[:, i, cg * 512:(cg + 1) * 512], in0=pp[:, :], in1=X[:, i, cg * 512:(cg + 1) * 512], op=ALU.add)),
                reads=[ppk, (xkey, i)], writes=[(xkey, i)])


def build_C1(NT=8):
    T, D, KT = NT * 128, D_MODEL, D_MODEL // 128
    nc = bass.Bass("TRN2", target_bir_lowering=False)
    with contextlib.ExitStack() as st:
        kb = KB(nc, st)
        x_d = kb.dram("x", [T, D], F32, "ExternalInput")
        o_d = kb.dram("oT", [D, T], BF16, "ExternalInput")
        wo_d = kb.dram("w_o", [D, D], F32, "ExternalInput")
        ga_d = kb.dram("ga", [128, KT], F32, "ExternalInput")
        wia_d = kb.dram("w_in_a", [D, 2 * D_FF], F32, "ExternalInput")
        woa_d = kb.dram("w_out_a", [D_FF, D], F32, "ExternalInput")
        gb_d = kb.dram("gb", [128, KT], F32, "ExternalInput")
        wib_d = kb.dram("w_in_b", [D, 2 * D_FF], F32, "ExternalInput")
        wob_d = kb.dram("w_out_b", [D_FF, D], F32, "ExternalInput")
        id_d = kb.dram("ident", [128, 128], BF16, "ExternalInput")
        y_d = kb.dram("y", [T, D], F32, "ExternalOutput")
        cm = Common(kb, id_d)
        X = kb.sb([128, NT, D], F32, "X")
        bufs = make_ffn_bufs(kb, NT, D, D_FF, 11)
        emit_load_x(kb, X, "X", x_d, NT)
        emit_load_aT(kb, bufs["xnT"], "xnTa", o_d, KT)
        emit_outproj(kb, cm, X, "X", NT, bufs["xnT"], "xnTa", KT, wo_d, bufs["scr"])
        kb.add("dve", lambda e: e.memset(cm.eps[:], EPS), writes=[("xnTa", "all"), "eps"] + [("xnT", i) for i in range(NT)])
        emit_ffn(kb, cm, X, "X", NT, D, D_FF, ga_d, wia_d, woa_d, 0.5, bufs)
        emit_ffn(kb, cm, X, "X", NT, D, D_FF, gb_d, wib_d, wob_d, 0.5, bufs)
        emit_store_x(kb, X, "X", y_d, NT)
        kb.finish()
    return nc


def build_E(NT=8):
    T, D, KT = NT * 128, D_MODEL, D_MODEL // 128
    nc = bass.Bass("TRN2", target_bir_lowering=False)
    with contextlib.ExitStack() as st:
        kb = KB(nc, st)
        x_d = kb.dram("x", [T, D], F32, "ExternalInput")
        o_d = kb.dram("oT", [D, T], BF16, "ExternalInput")
        wo_d = kb.dram("w_o", [D, D], F32, "ExternalInput")
        ga_d = kb.dram("ga", [128, KT], F32, "ExternalInput")
        wia_d = kb.dram("w_in_a", [D, 2 * D_FF], F32, "ExternalInput")
        woa_d = kb.dram("w_out_a", [D_FF, D], F32, "ExternalInput")
        gf_d = kb.dram("gfull", [128, D], F32, "ExternalInput")
        id_d = kb.dram("ident", [128, 128], BF16, "ExternalInput")
        y_d = kb.dram("y", [T, D], F32, "ExternalOutput")
        cm = Common(kb, id_d)
        X = kb.sb([128, NT, D], F32, "X")
        bufs = make_ffn_bufs(kb, NT, D, D_FF, 11)
        scr = bufs["scr"]
        GF = load_tab(kb, [128, D], gf_d, "GF")
        YS = kb.sb([128, D], F32, "YS")
        emit_load_x(kb, X, "X", x_d, NT)
        emit_load_aT(kb, bufs["xnT"], "xnTa", o_d, KT)
        emit_outproj(kb, cm, X, "X", NT, bufs["xnT"], "xnTa", KT, wo_d, scr)
        kb.add("dve", lambda e: e.memset(cm.eps[:], EPS), writes=[("xnTa", "all"), "eps"] + [("xnT", i) for i in range(NT)])
        emit_ffn(kb, cm, X, "X", NT, D, D_FF, ga_d, wia_d, woa_d, 0.5, bufs)
        for i in range(NT):
            sq, sqk = scr["sq"].next()
            ss, ssk = scr["ss"].next()
            kb.add("act", (lambda e, i=i, sq=sq, ss=ss: e.activation(out=sq[:, :D], in_=X[:, i, :], func=AF.Square, accum_out=ss[:, 0:1])),
                   reads=[("X", i)], writes=[sqk, ssk])
            kb.add("act", (lambda e, ss=ss: e.activation(out=ss[:, 1:2], in_=ss[:, 0:1], func=AF.Sqrt, scale=1.0 / D, bias=cm.eps[:, 0:1])),
                   reads=[ssk, "eps"], writes=[ssk])
            kb.add("dve", (lambda e, ss=ss: e.reciprocal(out=ss[:, 2:3], in_=ss[:, 1:2])), reads=[ssk], writes=[ssk])
            kb.add("act", (lambda e, i=i, ss=ss: e.activation(out=YS[:, :], in_=X[:, i, :], func=AF.Copy, scale=ss[:, 2:3])),
                   reads=[("X", i), ssk], writes=["YS"])
            kb.add("pool", (lambda e: e.tensor_tensor(out=YS[:, :], in0=YS[:, :], in1=GF[:, :], op=ALU.mult)),
                   reads=["YS", "GF"], writes=["YS"])
            kb.add("sp", (lambda e, i=i: e.dma_start(out=y_d[i * 128:(i + 1) * 128, :], in_=YS[:, :])),
                   reads=["YS"], dma_key="YS_st")
        kb.finish()
    return nc


def build_C2(NT=8):
    T, D, KT = NT * 128, D_MODEL, D_MODEL // 128
    nc = bass.Bass("TRN2", target_bir_lowering=False)
    with contextlib.ExitStack() as st:
        kb = KB(nc, st)
        x_d = kb.dram("x", [T, D], F32, "ExternalInput")
        gm_d = kb.dram("gm", [128, KT], F32, "ExternalInput")
        wi_d = kb.dram("mla_w_in", [D, 1088], F32, "ExternalInput")
        qn_g_d = kb.dram("qnorm", [128, 4], F32, "ExternalInput")
        kvn_g_d = kb.dram("kvnorm", [128, 4], F32, "ExternalInput")
        wuq_d = kb.dram("w_uq", [512, 3072], F32, "ExternalInput")
        wukv_d = kb.dram("w_ukv", [512, 4096], F32, "ExternalInput")
        id_d = kb.dram("ident", [128, 128], BF16, "ExternalInput")
        tab_d = {n: kb.dram(n, [64, T], F32, "ExternalInput") for n in ("cosq", "sinq", "cosk", "sink")}
        qn_d = kb.dram("qn", [2048, T], BF16, "ExternalOutput")
        qr_d = kb.dram("qr", [1024, T], BF16, "ExternalOutput")
        kn_d = kb.dram("kn", [2048, T], BF16, "ExternalOutput")
        kr_d = kb.dram("kr", [64, T], BF16, "ExternalOutput")
        v_d = kb.dram("v", [T, 2048], BF16, "ExternalOutput")
        cm = Common(kb, id_d)
        ev = Evac(kb)
        scr = {
            "sq": Rot([(kb.sb([128, D], BF16), "sq0")]),
            "ss": Rot([(kb.sb([128, 4], F32), "ss%d" % i) for i in range(2)]),
            "xs": Rot([(kb.sb([128, D], BF16), "xs%d" % i) for i in range(1)]),
            "win": Rot([(kb.sb([128, KT, 128], BF16), "win%d" % i) for i in range(4)]),
            "wout": Rot([(kb.sb([128, KT, 512], BF16), "wout%d" % i) for i in range(2)]),
        }
        make_proj_scr(kb, scr)
        xnT = kb.sb([128, KT, T], BF16, "xnT")
        gT = kb.sb([128, KT], F32, "gTvec")
        XR = kb.sb([128, 2, D], F32, "XR")
        CQ = kb.sb([128, NT, 512], F32, "CQ")
        CKV = kb.sb([128, NT, 512], F32, "CKV")
        cqT = kb.sb([128, 4, T], BF16, "cqT")
        ckvT = kb.sb([128, 4, T], BF16, "ckvT")
        gq = kb.sb([128, 4], F32, "gq")
        gkv = kb.sb([128, 4], F32, "gkv")
        tabs_sb = {}
        for n in tab_d:
            t = kb.sb([64, T], F32, "t_" + n)
            kb.add("sp", (lambda e, t=t, n=n: e.dma_start(out=t[:], in_=tab_d[n])), writes=["t_" + n], dma_key="t_" + n)
            tabs_sb[n] = t
        tabs = {"q": (tabs_sb["cosq"], tabs_sb["sinq"], ["t_cosq", "t_sinq"]),
                "k": (tabs_sb["cosk"], tabs_sb["sink"], ["t_cosk", "t_sink"])}
        kb.add("sp", lambda e: e.dma_start(out=gT[:, :KT], in_=gm_d), writes=["gTvec"], dma_key="gT")
        kb.add("sp", lambda e: e.dma_start(out=gq[:, :], in_=qn_g_d), writes=["gq"], dma_key="gq")
        kb.add("sp", lambda e: e.dma_start(out=gkv[:, :], in_=kvn_g_d), writes=["gkv"], dma_key="gkv")

        def pre(i):
            sl = i % 2
            kb.add("sp", (lambda e, i=i, sl=sl: e.dma_start(out=XR[:, sl, :], in_=x_d[i * 128:(i + 1) * 128, :])),
                   writes=[("XR", sl)], dma_key=("XR", sl))
            return XR[:, sl, :], ("XR", sl)

        emit_norm_T(kb, cm, None, None, NT, D, gT, "gTvec", xnT, "xnT", scr, pre=pre)
        akeys = [("xnT", i) for i in range(NT)]
        Wv = wview(wi_d)

        def sink_to(Ct, ckey):
            def f(i, pp, ppk):
                ev.copy(Ct[:, i, :], pp[:, :512], [ppk], [(ckey, i)])
            return f

        groups = [dict(segs=[(0, 0, 512)], n=512, sink=sink_to(CQ, "CQ")),
                  dict(segs=[(0, 512, 512)], n=512, sink=sink_to(CKV, "CKV"))]
        emit_proj_tm(kb, cm, ev, Wv, KT, xnT, lambda i: ("xnT", i), NT, groups, scr)
        emit_proj_fm(kb, cm, ev, Wv, KT, xnT, akeys, T,
                     [dict(col0=1024, m=64, kind="rope", half=32, tab="k", row0=0)], kr_d, tabs, scr)
        emit_norm_T(kb, cm, CQ, "CQ", NT, 512, gq, "gq", cqT, "cqT", scr)
        emit_norm_T(kb, cm, CKV, "CKV", NT, 512, gkv, "gkv", ckvT, "ckvT", scr)
        qkeys = [("cqT", i) for i in range(NT)]
        kvkeys = [("ckvT", i) for i in range(NT)]
        chunks = []
        for h in range(16):
            chunks.append(dict(col0=h * 192, m=128, kind="scale", scale=SC192, row0=h * 128, out=qn_d))
            chunks.append(dict(col0=h * 192 + 128, m=64, kind="rope", half=32, tab="q", row0=h * 64, out=qr_d))
        emit_proj_fm(kb, cm, ev, wview(wuq_d), 4, cqT, qkeys, T, chunks, qn_d, tabs, scr)
        chunks = [dict(col0=h * 256, m=128, kind="copy", row0=h * 128) for h in range(16)]
        emit_proj_fm(kb, cm, ev, wview(wukv_d), 4, ckvT, kvkeys, T, chunks, kn_d, tabs, scr)
        groups = [dict(segs=[(j * 128, (4 * g + j) * 256 + 128, 128) for j in range(4)], n=512, out=(v_d, g * 512), dt="bf16")
                  for g in range(4)]
        emit_proj_tm(kb, cm, ev, wview(wukv_d), 4, ckvT, lambda i: ("ckvT", i), NT, groups, scr)
        kb.finish()
    return nc


TOK = 1024


def _run(nc, in_maps):
    res = run_bass_kernel_spmd(nc, in_maps, core_ids=list(range(8)))
    return res.results


def _cat_t(rs, name, b):
    return np.concatenate([np.asarray(rs[2 * b][name]), np.asarray(rs[2 * b + 1][name])], axis=1)


def _cat_r(rs, name, b):
    return np.concatenate([np.asarray(rs[2 * b][name]), np.asarray(rs[2 * b + 1][name])], axis=0)


def kernel(x, ffn1_norm, ffn1_w_in, ffn1_w_out, mix_norm, ffn2_norm, ffn2_w_in, ffn2_w_out,
           hyb_w_in, hyb_w_out, nsa_cmp_pe, nsa_cmp_w1, nsa_cmp_w2, fox_f_bias,
           mla_w_in, mla_q_norm, mla_kv_norm, mla_w_uq, mla_w_ukv, mla_w_out, final_norm):
    f = lambda a: np.asarray(a, dtype=np.float32)
    x = f(x)
    ffn1_norm, ffn1_w_in, ffn1_w_out = f(ffn1_norm), f(ffn1_w_in), f(ffn1_w_out)
    ffn2_norm, ffn2_w_in, ffn2_w_out = f(ffn2_norm), f(ffn2_w_in), f(ffn2_w_out)
    mix_norm, hyb_w_in, hyb_w_out = f(mix_norm), f(hyb_w_in), f(hyb_w_out)
    nsa_cmp_pe, nsa_cmp_w1, nsa_cmp_w2, fox_f_bias = f(nsa_cmp_pe), f(nsa_cmp_w1), f(nsa_cmp_w2), f(fox_f_bias)
    mla_w_in, mla_q_norm, mla_kv_norm = f(mla_w_in), f(mla_q_norm), f(mla_kv_norm)
    mla_w_uq, mla_w_ukv, mla_w_out, final_norm = f(mla_w_uq), f(mla_w_ukv), f(mla_w_out), f(final_norm)
    ca = np.ascontiguousarray
    T = TOK
    ident = ident_np()
    xf = ca(x.reshape(BATCH * SEQ, D_MODEL))

    in_maps = []
    for c in range(8):
        pos = np.arange((c % 2) * T, (c % 2 + 1) * T)
        cq, sq = rope_tables(128, pos, SC128)
        ck, sk = rope_tables(128, pos, 1.0)
        in_maps.append({"x": xf[c * T:(c + 1) * T], "g1": lay_vec(ffn1_norm[0]), "w_in": ffn1_w_in[0],
                        "w_out": ffn1_w_out[0], "gm": lay_vec(mix_norm[0]), "hyb_w_in": hyb_w_in[0], "ident": ident,
                        "cosq": cq, "sinq": sq, "cosk": ck, "sink": sk})
    rA = _run(build_A(), in_maps)

    CB = consts_B_np()
    peT = ca(np.concatenate([nsa_cmp_pe[0, 0].T, nsa_cmp_pe[0, 1].T], axis=1))
    maps_n, maps_f = [], []
    for b in range(BATCH):
        fm_b = _cat_t(rA, "fmA", b)
        tm_b = _cat_r(rA, "tmA", b)
        g_b = _cat_t(rA, "gatesT", b)
        fl_b = _cat_r(rA, "flog", b)
        for g in range(2):
            fmB = np.concatenate([fm_b[g * 512:(g + 1) * 512], fm_b[1024 + g * 512:1024 + (g + 1) * 512],
                                  fm_b[2048 + g * 128:2048 + (g + 1) * 128], fm_b[2304 + g * 128:2304 + (g + 1) * 128],
                                  fm_b[2560 + g * 128:2560 + (g + 1) * 128], fm_b[2816 + g * 128:2816 + (g + 1) * 128]], axis=0)
            tmB = np.concatenate([tm_b[:, g * 128:(g + 1) * 128], tm_b[:, 256 + g * 128:256 + (g + 1) * 128]], axis=1)
            m = {"fmB": ca(fmB), "tmB": ca(tmB), "gates": ca(g_b[g * 12:(g + 1) * 12]), "peT": peT,
                 "w1": nsa_cmp_w1[0], "w2": nsa_cmp_w2[0], "ident": ident}
            for n in ("negC", "negW", "negcm", "ov", "E", "fb", "selg"):
                m[n] = CB[n]
            maps_n.append(m)
            hh = g
            fmF = np.concatenate([fm_b[3072 + hh * 512:3072 + (hh + 1) * 512], fm_b[4096 + hh * 512:4096 + (hh + 1) * 512]], axis=0)
            m = {"fmB": ca(fmF), "tmB": ca(tm_b[:, 512 + hh * 512:512 + (hh + 1) * 512]),
                 "flogT": ca(fl_b[:, hh * 4:(hh + 1) * 4].T), "fbias": ca(fox_f_bias[0, hh * 4:(hh + 1) * 4].reshape(4, 1)),
                 "ident": ident}
            for n in ("negC", "sel4", "selrowneg"):
                m[n] = CB[n]
            maps_f.append(m)
    rBn = _run(build_B("nsa"), maps_n)
    rBf = _run(build_B("fox"), maps_f)

    in_maps = []
    for c in range(8):
        b, half = c // 2, c % 2
        oT = np.concatenate([np.asarray(rBn[2 * b]["oT"]), np.asarray(rBn[2 * b + 1]["oT"]),
                             np.asarray(rBf[2 * b]["oT"]), np.asarray(rBf[2 * b + 1]["oT"])], axis=0)
        in_maps.append({"x": np.asarray(rA[c]["x1"]), "oT": ca(oT[:, half * T:(half + 1) * T]), "w_o": hyb_w_out[0],
                        "ga": lay_vec(ffn2_norm[0]), "w_in_a": ffn2_w_in[0], "w_out_a": ffn2_w_out[0],
                        "gb": lay_vec(ffn1_norm[1]), "w_in_b": ffn1_w_in[1], "w_out_b": ffn1_w_out[1], "ident": ident})
    rC1 = _run(build_C1(), in_maps)

    in_maps = []
    for c in range(8):
        pos = np.arange((c % 2) * T, (c % 2 + 1) * T)
        cq, sq = rope_tables(64, pos, SC192)
        ck, sk = rope_tables(64, pos, 1.0)
        in_maps.append({"x": np.asarray(rC1[c]["y"]), "gm": lay_vec(mix_norm[1]), "mla_w_in": mla_w_in[0],
                        "qnorm": lay_vec(mla_q_norm[0]), "kvnorm": lay_vec(mla_kv_norm[0]),
                        "w_uq": mla_w_uq[0], "w_ukv": mla_w_ukv[0], "ident": ident,
                        "cosq": cq, "sinq": sq, "cosk": ck, "sink": sk})
    rC2 = _run(build_C2(), in_maps)

    negC = CB["negC"]
    in_maps = []
    for b in range(BATCH):
        qn_b, qr_b, kn_b, kr_b = _cat_t(rC2, "qn", b), _cat_t(rC2, "qr", b), _cat_t(rC2, "kn", b), _cat_t(rC2, "kr", b)
        v_b = _cat_r(rC2, "v", b)
        for hh in range(2):
            in_maps.append({"qn": ca(qn_b[hh * 1024:(hh + 1) * 1024]), "qr": ca(qr_b[hh * 512:(hh + 1) * 512]),
                            "kn": ca(kn_b[hh * 1024:(hh + 1) * 1024]), "kr": ca(kr_b),
                            "v": ca(v_b[:, hh * 1024:(hh + 1) * 1024]), "negC": negC, "ident": ident})
    rD = _run(build_D(), in_maps)

    gfull = ca(np.broadcast_to(final_norm.reshape(1, D_MODEL), (128, D_MODEL)))
    in_maps = []
    for c in range(8):
        b, half = c // 2, c % 2
        oT = np.concatenate([np.asarray(rD[2 * b]["oT"]), np.asarray(rD[2 * b + 1]["oT"])], axis=0)
        in_maps.append({"x": np.asarray(rC1[c]["y"]), "oT": ca(oT[:, half * T:(half + 1) * T]), "w_o": mla_w_out[0],
                        "ga": lay_vec(ffn2_norm[1]), "w_in_a": ffn2_w_in[1], "w_out_a": ffn2_w_out[1],
                        "gfull": gfull, "ident": ident})
    rE = _run(build_E(), in_maps)
    out = np.concatenate([np.asarray(rE[c]["y"]) for c in range(8)], axis=0)
    return out.reshape(BATCH, SEQ, D_MODEL).astype(np.float32)
```

```python
import contextlib
import numpy as np
import ml_dtypes
import concourse.bass as bass
import concourse.mybir as mybir
from concourse.bass_utils import run_bass_kernel_spmd

F32 = mybir.dt.float32
BF16 = mybir.dt.bfloat16
AF = mybir.ActivationFunctionType
ALU = mybir.AluOpType
AX = mybir.AxisListType
NPBF = ml_dtypes.bfloat16

D_MODEL = 2048
SEQ = 2048
BATCH = 4
D_FF = 5632
EPS = 1e-6


class _Op:
    __slots__ = ("eng", "fn", "deps", "dma_key", "signal", "count")

    def __init__(self, eng, fn, dma_key):
        self.eng = eng
        self.fn = fn
        self.deps = set()
        self.dma_key = dma_key
        self.signal = False
        self.count = 0


class Sched:
    ENGS = ("pe", "act", "dve", "pool", "sp")

    def __init__(self):
        self.ops = []
        self.kw = {}
        self.kr = {}
        self.excl = set()

    def add(self, eng, fn, reads=(), writes=(), dma_key=None):
        idx = len(self.ops)
        op = _Op(eng, fn, dma_key)
        for k in reads:
            w = self.kw.get(k)
            if w is not None:
                op.deps.add(w)
            if k in self.excl:
                for r in self.kr.get(k, ()):
                    if self.ops[r].eng != eng:
                        op.deps.add(r)
        for k in writes:
            w = self.kw.get(k)
            if w is not None:
                op.deps.add(w)
            for r in self.kr.get(k, ()):
                op.deps.add(r)
        for k in reads:
            self.kr.setdefault(k, []).append(idx)
        for k in writes:
            self.kw[k] = idx
            self.kr[k] = []
        op.deps.discard(idx)
        self.ops.append(op)
        return idx

    @staticmethod
    def _skip(dop, o):
        return dop.dma_key is None and dop.eng == "pe" and o.eng == "pe" and o.dma_key is None

    def emit(self, nc, stack):
        ops = self.ops
        for o in ops:
            for d in o.deps:
                dop = ops[d]
                if dop.dma_key is None and not self._skip(dop, o):
                    dop.signal = True
        eng_cnt = {e: 0 for e in self.ENGS}
        dma_cnt = {}
        for o in ops:
            if o.dma_key is not None:
                dma_cnt[o.dma_key] = dma_cnt.get(o.dma_key, 0) + 16
                o.count = dma_cnt[o.dma_key]
            elif o.signal:
                eng_cnt[o.eng] += 1
                o.count = eng_cnt[o.eng]
        assert max(eng_cnt.values()) < 60000, eng_cnt
        eng_sem = {e: stack.enter_context(nc.semaphore("es_" + e)) for e in self.ENGS}
        dma_sem = {}
        for i, k in enumerate(dma_cnt):
            dma_sem[k] = stack.enter_context(nc.semaphore("ds_%d" % i))
        streams = {e: [o for o in ops if o.eng == e] for e in self.ENGS}
        block = stack.enter_context(nc.Block())

        def make(e):
            def body(engobj):
                waited = {}
                for o in streams[e]:
                    need = {}
                    for d in o.deps:
                        dop = ops[d]
                        if dop.dma_key is not None:
                            key = ("d", dop.dma_key)
                            sem = dma_sem[dop.dma_key]
                        else:
                            if self._skip(dop, o):
                                continue
                            key = ("e", dop.eng)
                            sem = eng_sem[dop.eng]
                        if need.get(key, (None, 0))[1] < dop.count:
                            need[key] = (sem, dop.count)
                    for key, (sem, v) in need.items():
                        if waited.get(key, 0) < v:
                            engobj.wait_ge(sem, v)
                            waited[key] = v
                    ins = o.fn(engobj)
                    if o.dma_key is not None:
                        ins.then_inc(dma_sem[o.dma_key], 16)
                    elif o.signal:
                        ins.then_inc(eng_sem[e], 1)
                if e == "sp":
                    for k, sem in dma_sem.items():
                        engobj.wait_ge(sem, dma_cnt[k])
            return body

        block.tensor(make("pe"))
        block.scalar(make("act"))
        block.vector(make("dve"))
        block.gpsimd(make("pool"))
        block.sync(make("sp"))


class KB:
    def __init__(self, nc, stack):
        self.nc = nc
        self.stack = stack
        self.s = Sched()
        self.n = 0
        self.psn = 0

    def sb(self, shape, dt, name=None):
        self.n += 1
        return self.stack.enter_context(
            self.nc.sbuf_tensor("s_" + (name or ("sb%d" % self.n)), list(shape), dt))

    def ps(self, shape, dt, name=None):
        self.psn += 1
        return self.stack.enter_context(
            self.nc.psum_tensor("p_" + (name or ("ps%d" % self.psn)), list(shape), dt))

    def dram(self, name, shape, dt, kind):
        return self.nc.dram_tensor(name, list(shape), dt, kind=kind).ap()

    def add(self, *a, **k):
        return self.s.add(*a, **k)

    def finish(self):
        self.s.emit(self.nc, self.stack)


class Rot:
    def __init__(self, bufs):
        self.bufs = bufs
        self.i = 0

    def next(self):
        b = self.bufs[self.i % len(self.bufs)]
        self.i += 1
        return b


class Common:
    def __init__(self, kb, ident_dram, nf=6, nb=2):
        self.kb = kb
        self.psf_list = [(kb.ps([128, 512], F32), "psf%d" % i) for i in range(nf)]
        self.psb_list = [(kb.ps([128, 1024], BF16), "psb%d" % i) for i in range(nb)]
        self.psf = Rot(self.psf_list)
        self.psb = Rot(self.psb_list)
        kb.s.excl.update(k for _, k in self.psf_list + self.psb_list)
        self.ident = kb.sb([128, 128], BF16, "ident")
        kb.add("sp", lambda e: e.dma_start(out=self.ident[:], in_=ident_dram),
               writes=["ident"], dma_key="ident")
        self.eps = kb.sb([128, 1], F32, "eps")
        kb.add("dve", lambda e: e.memset(self.eps[:], EPS), writes=["eps"])


def emit_load_x(kb, X, xkey, x_dram, NT):
    for i in range(NT):
        kb.add("sp", (lambda e, i=i: e.dma_start(out=X[:, i, :], in_=x_dram[i * 128:(i + 1) * 128, :])),
               writes=[(xkey, i)], dma_key=(xkey, "ld", i))


def emit_store_x(kb, X, xkey, out_dram, NT):
    for i in range(NT):
        kb.add("sp", (lambda e, i=i: e.dma_start(out=out_dram[i * 128:(i + 1) * 128, :], in_=X[:, i, :])),
               reads=[(xkey, i)], dma_key=(xkey, "st", i))


def emit_norm_T(kb, cm, X, xkey, NT, D, gT, gkey, xnT, xnkey, scr, pre=None):
    KT = D // 128
    for i in range(NT):
        if pre is not None:
            Xi, xik = pre(i)
        else:
            Xi, xik = X[:, i, :], (xkey, i)
        sq, sqk = scr["sq"].next()
        ss, ssk = scr["ss"].next()
        xs, xsk = scr["xs"].next()
        kb.add("act", (lambda e, i=i, sq=sq, ss=ss, Xi=Xi: e.activation(
            out=sq[:, :D], in_=Xi, func=AF.Square, accum_out=ss[:, 0:1])),
            reads=[xik], writes=[sqk, ssk])
        kb.add("act", (lambda e, ss=ss: e.activation(
            out=ss[:, 1:2], in_=ss[:, 0:1], func=AF.Sqrt, scale=1.0 / D, bias=cm.eps[:, 0:1])),
            reads=[ssk, "eps"], writes=[ssk])
        kb.add("dve", (lambda e, ss=ss: e.reciprocal(out=ss[:, 2:3], in_=ss[:, 1:2])),
            reads=[ssk], writes=[ssk])
        kb.add("act", (lambda e, i=i, xs=xs, ss=ss, Xi=Xi: e.activation(
            out=xs[:, :D], in_=Xi, func=AF.Copy, scale=ss[:, 2:3])),
            reads=[xik, ssk], writes=[xsk])
        for k0 in range(0, KT, 8):
            nk = min(8, KT - k0)
            (pb, pbk) = cm.psb.next()
            for j in range(nk):
                kb.add("pe", (lambda e, j=j, k0=k0, pb=pb, xs=xs: e.transpose(
                    out=pb[:, j * 128:(j + 1) * 128], in_=xs[:, (k0 + j) * 128:(k0 + j + 1) * 128],
                    identity=cm.ident[:])),
                    reads=[xsk, "ident"], writes=[pbk])
            kb.add("dve", (lambda e, k0=k0, nk=nk, pb=pb, i=i: e.tensor_tensor(
                out=xnT[:, k0:k0 + nk, i * 128:(i + 1) * 128],
                in0=pb[:, :nk * 128].rearrange("p (k t) -> p k t", t=128),
                in1=gT[:, k0:k0 + nk].unsqueeze(2).to_broadcast([128, nk, 128]),
                op=ALU.mult)),
                reads=[pbk, gkey], writes=[(xnkey, i)])


def emit_ffn(kb, cm, X, xkey, NT, D, F, g_dram, w_in, w_out, res_scale, bufs):
    T = NT * 128
    KT = D // 128
    FC = F // 128
    xnT, gT, gTb, scr = bufs["xnT"], bufs["gT"], bufs["gTb"], bufs["scr"]
    CG = bufs["CG"]
    assert FC % CG == 0
    NG = FC // CG
    gkey = "gTvec"
    kb.add("sp", lambda e: e.dma_start(out=gT[:, :KT], in_=g_dram), writes=[gkey], dma_key="gT")
    emit_norm_T(kb, cm, X, xkey, NT, D, gT, gkey, xnT, "xnT", scr)
    xn_reads = [("xnT", i) for i in range(NT)]
    w_in_v = w_in.rearrange("(kt p) n -> p kt n", p=128)
    w_out_v = w_out.rearrange("(c p) n -> p c n", p=128)
    TH = (T + 511) // 512
    NCG = D // 512
    for g in range(NG):
        for cc in range(CG):
            c = g * CG + cc
            wg, wgk = scr["win"].next()
            wu, wuk = scr["win"].next()
            wgkeys = load_w_cols(kb, wg, wgk, w_in_v, KT, [(0, c * 128, 128)])
            wukeys = load_w_cols(kb, wu, wuk, w_in_v, KT, [(0, F + c * 128, 128)])
            for th in range(TH):
                t0 = th * 512
                tn = min(512, T - t0)
                pg, pgk = cm.psf.next()
                pu, puk = cm.psf.next()
                for (wt, wks, pp, ppk) in ((wg, wgkeys, pg, pgk), (wu, wukeys, pu, puk)):
                    for kt in range(KT):
                        kb.add("pe", (lambda e, wt=wt, pp=pp, kt=kt, t0=t0, tn=tn: e.matmul(
                            pp[:, :tn], lhsT=wt[:, kt, :], rhs=xnT[:, kt, t0:t0 + tn],
                            start=(kt == 0), stop=(kt == KT - 1))),
                            reads=wks + xn_reads, writes=[ppk])
                sg, sgk = scr["sg"].next()
                kb.add("act", (lambda e, sg=sg, pg=pg, tn=tn: e.activation(
                    out=sg[:, :tn], in_=pg[:, :tn], func=AF.Silu)),
                    reads=[pgk], writes=[sgk])
                kb.add("dve", (lambda e, sg=sg, pu=pu, cc=cc, t0=t0, tn=tn: e.tensor_tensor(
                    out=gTb[:, cc, t0:t0 + tn], in0=sg[:, :tn], in1=pu[:, :tn], op=ALU.mult)),
                    reads=[sgk, puk], writes=[("gTb", cc)])
        for cg in range(NCG):
            wo, wok = scr["wout"].next()
            wokeys = load_w_cols(kb, wo, wok, w_out_v[:, g * CG:(g + 1) * CG, :], CG,
                                 [(0, cg * 512, 512)], kgrp=4)
            for i in range(NT):
                po, pok = cm.psf.next()
                for cc in range(CG):
                    kb.add("pe", (lambda e, po=po, wo=wo, cc=cc, i=i: e.matmul(
                        po[:, :], lhsT=gTb[:, cc, i * 128:(i + 1) * 128], rhs=wo[:, cc, :],
                        start=(cc == 0), stop=(cc == CG - 1))),
                        reads=[("gTb", cc)] + wokeys, writes=[pok])
                kb.add("dve", (lambda e, po=po, i=i, cg=cg: e.scalar_tensor_tensor(
                    out=X[:, i, cg * 512:(cg + 1) * 512], in0=po[:, :], scalar=res_scale,
                    in1=X[:, i, cg * 512:(cg + 1) * 512], op0=ALU.mult, op1=ALU.add)),
                    reads=[pok, (xkey, i)], writes=[(xkey, i)])


def make_ffn_bufs(kb, NT, D, F, CG):
    T = NT * 128
    KT = D // 128
    scr = {
        "sq": Rot([(kb.sb([128, D], BF16), "sq0")]),
        "ss": Rot([(kb.sb([128, 4], F32), "ss%d" % i) for i in range(2)]),
        "xs": Rot([(kb.sb([128, D], BF16), "xs%d" % i) for i in range(1)]),
        "win": Rot([(kb.sb([128, KT, 128], BF16), "win%d" % i) for i in range(4)]),
        "wout": Rot([(kb.sb([128, max(CG, KT), 512], BF16), "wout%d" % i) for i in range(2)]),
        "sg": Rot([(kb.sb([128, 512], F32), "sg%d" % i) for i in range(2)]),
    }
    return {
        "xnT": kb.sb([128, KT, T], BF16, "xnT"),
        "gT": kb.sb([128, KT], F32, "gTvec"),
        "gTb": kb.sb([128, CG, T], BF16, "gTb"),
        "scr": scr,
        "CG": CG,
    }


def ident_np():
    return np.eye(128, dtype=np.float32).astype(NPBF)


def lay_vec(g):
    g = np.asarray(g, dtype=np.float32)
    return np.ascontiguousarray(g.reshape(-1, 128).T)


def wview(W):
    return W.rearrange("(kt p) n -> p kt n", p=128)


def load_w_cols(kb, wt, wk, Wv, KT, segs, kgrp=8):
    for k0 in range(0, KT, kgrp):
        nk = min(kgrp, KT - k0)
        for (d0, s0, n) in segs:
            kb.add("pool", (lambda e, k0=k0, nk=nk, d0=d0, s0=s0, n=n: e.dma_start(
                out=wt[:, k0:k0 + nk, d0:d0 + n], in_=Wv[:, k0:k0 + nk, s0:s0 + n])),
                writes=[(wk, k0, d0)], dma_key=(wk, k0))
    return [(wk, k0, d0) for k0 in range(0, KT, kgrp) for (d0, s0, n) in segs]


def mm_fm(kb, cm, wt, wkeys, m, aT, akeys, KT, t0, tn):
    pp, ppk = cm.psf.next()
    for kt in range(KT):
        kb.add("pe", (lambda e, kt=kt, pp=pp: e.matmul(
            pp[:m, :tn], lhsT=wt[:, kt, :m], rhs=aT[:, kt, t0:t0 + tn],
            start=(kt == 0), stop=(kt == KT - 1))),
            reads=wkeys + akeys, writes=[ppk])
    return pp, ppk


def mm_tm(kb, cm, wt, wkeys, n, aT, akey_i, KT, i):
    pp, ppk = cm.psf.next()
    for kt in range(KT):
        kb.add("pe", (lambda e, kt=kt, pp=pp: e.matmul(
            pp[:, :n], lhsT=aT[:, kt, i * 128:(i + 1) * 128], rhs=wt[:, kt, :n],
            start=(kt == 0), stop=(kt == KT - 1))),
            reads=wkeys + [akey_i], writes=[ppk])
    return pp, ppk


class Evac:
    def __init__(self, kb):
        self.kb = kb
        self.i = 0

    def copy(self, out, in_, reads, writes, scale=None):
        self.i += 1
        kb = self.kb
        if self.i % 2 == 0:
            if scale is None:
                kb.add("act", (lambda e: e.activation(out=out, in_=in_, func=AF.Copy)),
                       reads=reads, writes=writes)
            else:
                kb.add("act", (lambda e: e.activation(out=out, in_=in_, func=AF.Copy, scale=float(scale))),
                       reads=reads, writes=writes)
        else:
            if scale is None:
                kb.add("dve", (lambda e: e.tensor_copy(out=out, in_=in_)), reads=reads, writes=writes)
            else:
                kb.add("dve", (lambda e: e.tensor_scalar_mul(out=out, in0=in_, scalar1=float(scale))),
                       reads=reads, writes=writes)


def emit_rope_evac(kb, pa, pak, pb, pbk, m, tn, cosT, sinT, tkeys, t0, scr, out):
    t1, t1k = scr["r1"].next()
    t2, t2k = scr["r2"].next()
    kb.add("dve", (lambda e: e.tensor_tensor(out=t1[:m, :tn], in0=pa[:m, :tn], in1=cosT[:m, t0:t0 + tn], op=ALU.mult)),
           reads=[pak] + tkeys, writes=[t1k])
    kb.add("dve", (lambda e: e.tensor_tensor(out=t2[:m, :tn], in0=pb[:m, :tn], in1=sinT[:m, t0:t0 + tn], op=ALU.mult)),
           reads=[pbk] + tkeys, writes=[t2k])
    return t1, t1k, t2, t2k


ROPE_DBG = set()


def emit_proj_fm(kb, cm, ev, Wv, KT, aT, akeys, T, chunks, out_dram, tabs, scr):
    default_out = out_dram
    for ch in chunks:
        col0, m = ch["col0"], ch["m"]
        out_dram = ch.get("out", default_out)
        wt, wk = scr["win"].next()
        wkeys = load_w_cols(kb, wt, wk, Wv, KT, [(0, col0, m)])
        rope = ch["kind"] == "rope"
        if rope:
            hs = ch["half"]
            wt2, wk2 = scr["win"].next()
            kb.add("act", (lambda e, wt=wt, wt2=wt2, hs=hs: e.activation(
                out=wt2[:, :KT, 0:hs], in_=wt[:, :KT, hs:2 * hs], func=AF.Copy)),
                reads=wkeys, writes=[(wk2, k0, 0) for k0 in range(0, KT, 8)])
            kb.add("dve", (lambda e, wt=wt, wt2=wt2, hs=hs: e.tensor_copy(
                out=wt2[:, :KT, hs:2 * hs], in_=wt[:, :KT, 0:hs])),
                reads=wkeys, writes=[(wk2, k0, 0) for k0 in range(0, KT, 8)])
            wkeys2 = [(wk2, k0, 0) for k0 in range(0, KT, 8)]
        for t0 in range(0, T, 512):
            tn = min(512, T - t0)
            pa, pak = mm_fm(kb, cm, wt, wkeys, m, aT, akeys, KT, t0, tn)
            plain_key = []
            if ch["kind"] in ("copy", "scale") or ch.get("plain_row0") is not None:
                st, stk = scr["stg"].next()
                plain_key = [stk]
                sc = ch.get("scale")
                ev.copy(st[:m, :tn], pa[:m, :tn], [pak], [stk], scale=sc)
                r0 = ch["row0"] if not rope else ch["plain_row0"]
                od_ = ch.get("plain_out", out_dram) if rope else out_dram
                kb.add("sp", (lambda e, st=st, r0=r0, t0=t0, tn=tn, m=m, od_=od_: e.dma_start(
                    out=od_[r0:r0 + m, t0:t0 + tn], in_=st[:m, :tn])),
                    reads=[stk], dma_key=(stk, "st"))
            if rope:
                pb, pbk = mm_fm(kb, cm, wt2, wkeys2, m, aT, akeys, KT, t0, tn)
                cosT, sinT, tkeys = tabs[ch["tab"]]
                st, stk = scr["stg"].next()
                if "nomul" in ROPE_DBG:
                    ev.copy(st[:m, :tn], pb[:m, :tn], [pbk], [stk])
                else:
                    t1, t1k, t2, t2k = emit_rope_evac(kb, pa, pak, pb, pbk, m, tn, cosT, sinT, tkeys + plain_key, t0, scr, None)
                    kb.add("dve", (lambda e, st=st, t1=t1, t2=t2, m=m, tn=tn: e.tensor_tensor(
                        out=st[:m, :tn], in0=t1[:m, :tn], in1=t2[:m, :tn], op=ALU.add)),
                        reads=[t1k, t2k], writes=[stk])
                r0 = ch["row0"]
                kb.add("sp", (lambda e, st=st, r0=r0, t0=t0, tn=tn, m=m, od_=out_dram: e.dma_start(
                    out=od_[r0:r0 + m, t0:t0 + tn], in_=st[:m, :tn])),
                    reads=[stk], dma_key=(stk, "st"))


def emit_proj_tm(kb, cm, ev, Wv, KT, aT, akey_fn, NT, groups, scr):
    for gp in groups:
        wt, wk = scr["wout"].next()
        wkeys = load_w_cols(kb, wt, wk, Wv, KT, gp["segs"], kgrp=4)
        n = gp["n"]
        for i in range(NT):
            pp, ppk = mm_tm(kb, cm, wt, wkeys, n, aT, akey_fn(i), KT, i)
            if gp.get("sink") is not None:
                gp["sink"](i, pp, ppk)
                continue
            if gp["dt"] == "bf16":
                st, stk = scr["stg"].next()
            else:
                st, stk = scr["stgf"].next()
            ev.copy(st[:, :n], pp[:, :n], [ppk], [stk])
            od, c0 = gp["out"]
            kb.add("sp", (lambda e, st=st, od=od, c0=c0, i=i, n=n: e.dma_start(
                out=od[i * 128:(i + 1) * 128, c0:c0 + n], in_=st[:, :n])),
                reads=[stk], dma_key=(stk, "st"))


def make_proj_scr(kb, scr):
    scr["stg"] = Rot([(kb.sb([128, 512], BF16), "stg%d" % i) for i in range(4)])
    scr["stgf"] = Rot([(kb.sb([128, 512], F32), "stgf%d" % i) for i in range(1)])
    scr["r1"] = Rot([(kb.sb([128, 512], F32), "r1_%d" % i) for i in range(1)])
    scr["r2"] = Rot([(kb.sb([128, 512], F32), "r2_%d" % i) for i in range(1)])


def load_tab(kb, shape, dram, name):
    t = kb.sb(shape, dram.dtype, name)
    kb.add("sp", (lambda e: e.dma_start(out=t[:], in_=dram)), writes=[name], dma_key=name)
    return t


HYB_IN = 5664
A_FM_ROWS = 5120
SC128 = 128.0 ** -0.5
SC192 = 192.0 ** -0.5


def build_A(NT=8, dbg=()):
    T = NT * 128
    D = D_MODEL
    KT = D // 128
    nc = bass.Bass("TRN2", target_bir_lowering=False)
    with contextlib.ExitStack() as st:
        kb = KB(nc, st)
        x_d = kb.dram("x", [T, D], F32, "ExternalInput")
        g1_d = kb.dram("g1", [128, KT], F32, "ExternalInput")
        wi_d = kb.dram("w_in", [D, 2 * D_FF], F32, "ExternalInput")
        wo_d = kb.dram("w_out", [D_FF, D], F32, "ExternalInput")
        gm_d = kb.dram("gm", [128, KT], F32, "ExternalInput")
        hw_d = kb.dram("hyb_w_in", [D, HYB_IN], F32, "ExternalInput")
        id_d = kb.dram("ident", [128, 128], BF16, "ExternalInput")
        tab_d = {n: kb.dram(n, [128, T], F32, "ExternalInput") for n in ("cosq", "sinq", "cosk", "sink")}
        x1_d = kb.dram("x1", [T, D], F32, "ExternalOutput")
        fm_d = kb.dram("fmA", [A_FM_ROWS, T], BF16, "ExternalOutput")
        gt_d = kb.dram("gatesT", [24, T], F32, "ExternalOutput")
        tm_d = kb.dram("tmA", [T, 1536], BF16, "ExternalOutput")
        fl_d = kb.dram("flog", [T, 8], F32, "ExternalOutput")

        cm = Common(kb, id_d)
        ev = Evac(kb)
        X = kb.sb([128, NT, D], F32, "X")
        bufs = make_ffn_bufs(kb, NT, D, D_FF, 11)
        scr = bufs["scr"]
        make_proj_scr(kb, scr)
        tabs_sb = {n: load_tab(kb, [128, T], tab_d[n], "t_" + n) for n in tab_d}
        tabs = {"q": (tabs_sb["cosq"], tabs_sb["sinq"], ["t_cosq", "t_sinq"]),
                "k": (tabs_sb["cosk"], tabs_sb["sink"], ["t_cosk", "t_sink"])}
        emit_load_x(kb, X, "X", x_d, NT)
        if "noffn" not in dbg:
            emit_ffn(kb, cm, X, "X", NT, D, D_FF, g1_d, wi_d, wo_d, 0.5, bufs)
        emit_store_x(kb, X, "X", x1_d, NT)
        xnT, gT = bufs["xnT"], bufs["gT"]
        kb.add("sp", lambda e: e.dma_start(out=gT[:, :KT], in_=gm_d), writes=["gTvec"], dma_key="gT")
        emit_norm_T(kb, cm, X, "X", NT, D, gT, "gTvec", xnT, "xnT", scr)
        akeys = [("xnT", i) for i in range(NT)]
        Wv = wview(hw_d)
        chunks = []
        for h in range(8):
            chunks.append(dict(col0=h * 128, m=128, kind="rope", half=64, tab="q",
                               row0=1024 + h * 128, plain_row0=h * 128, scale=SC128))
        for j in range(2):
            chunks.append(dict(col0=1024 + j * 128, m=128, kind="copy", row0=2048 + j * 128))
            chunks.append(dict(col0=1280 + j * 128, m=128, kind="copy", row0=2304 + j * 128))
            chunks.append(dict(col0=1536 + j * 128, m=128, kind="rope", half=64, tab="k", row0=2560 + j * 128))
            chunks.append(dict(col0=2048 + j * 128, m=128, kind="rope", half=64, tab="k", row0=2816 + j * 128))
        for h in range(8):
            chunks.append(dict(col0=2584 + h * 128, m=128, kind="scale", scale=SC128, row0=3072 + h * 128))
            chunks.append(dict(col0=3608 + h * 128, m=128, kind="copy", row0=4096 + h * 128))
        if "nofm" in dbg:
            chunks = []
        if "fm1" in dbg:
            chunks = chunks[8:]
        if "fm2" in dbg:
            chunks = chunks[:8]
        if "fmcopy" in dbg:
            chunks = [c for c in chunks if c["kind"] != "rope"]
        emit_proj_fm(kb, cm, ev, Wv, KT, xnT, akeys, T, chunks, fm_d, tabs, scr)
        wt, wk = scr["win"].next()
        wkeys = load_w_cols(kb, wt, wk, Wv, KT, [(0, 2560, 24)])
        for t0 in range(0, T if "nogates" not in dbg else 0, 512):
            pa, pak = mm_fm(kb, cm, wt, wkeys, 24, xnT, akeys, KT, t0, 512)
            sg, sgk = scr["stgf"].next()
            kb.add("act", (lambda e, sg=sg, pa=pa: e.activation(out=sg[:24, :], in_=pa[:24, :], func=AF.Sigmoid)),
                   reads=[pak], writes=[sgk])
            kb.add("sp", (lambda e, sg=sg, t0=t0: e.dma_start(out=gt_d[:, t0:t0 + 512], in_=sg[:24, :])),
                   reads=[sgk], dma_key=(sgk, "st"))
        groups = [
            dict(segs=[(0, 1792, 256), (256, 2304, 256)], n=512, out=(tm_d, 0), dt="bf16"),
            dict(segs=[(0, 4632, 512)], n=512, out=(tm_d, 512), dt="bf16"),
            dict(segs=[(0, 5144, 512)], n=512, out=(tm_d, 1024), dt="bf16"),
            dict(segs=[(0, 5656, 8)], n=8, out=(fl_d, 0), dt="f32"),
        ]
        if "notm" in dbg:
            groups = []
        if "noflog" in dbg:
            groups = groups[:3]
        emit_proj_tm(kb, cm, ev, Wv, KT, xnT, lambda i: ("xnT", i), NT, groups, scr)
        kb.finish()
    return nc


def rope_tables(d, pos, scale=1.0):
    inv = 1.0 / (10000.0 ** (np.arange(0, d, 2, dtype=np.float32) / d))
    ang = pos.astype(np.float32)[None, :] * inv.astype(np.float32)[:, None]
    c = np.cos(ang).astype(np.float32)
    s = np.sin(ang).astype(np.float32)
    cosf = np.concatenate([c, c], 0) * np.float32(scale)
    sinf = np.concatenate([-s, s], 0) * np.float32(scale)
    return np.ascontiguousarray(cosf.astype(np.float32)), np.ascontiguousarray(sinf.astype(np.float32))


NEG = -30000.0


class AttnRes:
    def __init__(self, kb, cm):
        self.sc = Rot(cm.psf_list[0:2])
        self.O = Rot(cm.psf_list[2:4])
        self.S = Rot(cm.psf_list[4:6])
        self.misc = cm.psf_list[6]
        self.pt = Rot([(kb.sb([128, 512], BF16), "pt%d" % i) for i in range(3)])
        self.ones = kb.sb([128, 128], BF16, "ones")
        kb.add("dve", lambda e: e.memset(self.ones[:], 1.0), writes=["ones"])
        self.rs = Rot([(kb.sb([128, 512], F32), "rs%d" % i) for i in range(2)])
        self.ost = Rot([(kb.sb([128, 512], BF16), "ost%d" % i) for i in range(2)])


def attn_qblock(kb, R, kts, score_fn, v_fn, np_fn, dv=128):
    po, pok = R.O.next()
    psm, psmk = R.S.next()
    n = len(kts)
    for idx, kt in enumerate(kts):
        sc, sck = R.sc.next()
        terms = score_fn(kt)
        npart = np_fn(kt)
        nt = len(terms)
        for ti, (l, r, rd) in enumerate(terms):
            kb.add("pe", (lambda e, sc=sc, l=l, r=r, ti=ti, nt=nt, npart=npart: e.matmul(
                sc[:npart, :], lhsT=l, rhs=r, start=(ti == 0), stop=(ti == nt - 1))),
                reads=rd, writes=[sck])
        pt, ptk = R.pt.next()
        kb.add("act", (lambda e, pt=pt, sc=sc, npart=npart: e.activation(
            out=pt[:npart, :], in_=sc[:npart, :], func=AF.Exp)),
            reads=[sck], writes=[ptk])
        v, vr = v_fn(kt)
        kb.add("pe", (lambda e, po=po, v=v, pt=pt, idx=idx, npart=npart: e.matmul(
            po[:dv, :], lhsT=v, rhs=pt[:npart, :], start=(idx == 0), stop=(idx == n - 1))),
            reads=[ptk] + vr, writes=[pok])
        kb.add("pe", (lambda e, psm=psm, pt=pt, idx=idx, npart=npart: e.matmul(
            psm[:, :], lhsT=R.ones[:npart, :], rhs=pt[:npart, :], start=(idx == 0), stop=(idx == n - 1))),
            reads=[ptk, "ones"], writes=[psmk])
    return po, pok, psm, psmk


def finalize_plain(kb, R, po, pok, psm, psmk, out_dram, row0, q0):
    rs, rsk = R.rs.next()
    kb.add("dve", (lambda e: e.reciprocal(out=rs[:, :], in_=psm[:, :])), reads=[psmk], writes=[rsk])
    ost, ostk = R.ost.next()
    kb.add("dve", (lambda e: e.tensor_tensor(out=ost[:, :], in0=po[:, :], in1=rs[:, :], op=ALU.mult)),
           reads=[pok, rsk], writes=[ostk])
    kb.add("sp", (lambda e: e.dma_start(out=out_dram[row0:row0 + 128, q0:q0 + 512], in_=ost[:, :])),
           reads=[ostk], dma_key=(ostk, "st"))


def load_fm(kb, tile, key, dram, nchunk, rows=128):
    v = dram.rearrange("(c p) t -> p c t", p=rows)
    for c in range(nchunk):
        kb.add("sp", (lambda e, c=c: e.dma_start(out=tile[:rows, c, :], in_=v[:, c, :])),
               writes=[(key, c)], dma_key=(key, c % 4))


def load_tm(kb, tile, key, dram, ntile):
    v = dram.rearrange("(i p) n -> p i n", p=128)
    for i in range(ntile):
        kb.add("sp", (lambda e, i=i: e.dma_start(out=tile[:, i, :], in_=v[:, i, :])),
               writes=[(key, i)], dma_key=(key, i % 4))


def neg_masks_np():
    k = np.arange(128)[:, None, None]
    j = np.arange(4)[None, :, None]
    q = np.arange(512)[None, None, :]
    negC = np.where(q >= 128 * j + k, 0.0, NEG).astype(np.float32).astype(NPBF)
    negW = np.where(q < 128 * j + k, 0.0, NEG).astype(np.float32).astype(NPBF)
    return np.ascontiguousarray(negC), np.ascontiguousarray(negW)


def build_D(S=SEQ, NH=8):
    nc = bass.Bass("TRN2", target_bir_lowering=False)
    NKT = S // 128
    NQB = S // 512
    with contextlib.ExitStack() as st:
        kb = KB(nc, st)
        qn_d = kb.dram("qn", [NH * 128, S], BF16, "ExternalInput")
        qr_d = kb.dram("qr", [NH * 64, S], BF16, "ExternalInput")
        kn_d = kb.dram("kn", [NH * 128, S], BF16, "ExternalInput")
        kr_d = kb.dram("kr", [64, S], BF16, "ExternalInput")
        v_d = kb.dram("v", [S, NH * 128], BF16, "ExternalInput")
        negC_d = kb.dram("negC", [128, 4, 512], BF16, "ExternalInput")
        id_d = kb.dram("ident", [128, 128], BF16, "ExternalInput")
        o_d = kb.dram("oT", [NH * 128, S], BF16, "ExternalOutput")
        cm = Common(kb, id_d, nf=7, nb=1)
        R = AttnRes(kb, cm)
        QN = kb.sb([128, NH, S], BF16, "QN")
        QR = kb.sb([64, NH, S], BF16, "QR")
        KN = kb.sb([128, NH, S], BF16, "KN")
        KR = kb.sb([64, 1, S], BF16, "KR")
        V = kb.sb([128, NKT, NH * 128], BF16, "V")
        negC = kb.sb([128, 4, 512], BF16, "negC")
        kb.add("sp", lambda e: e.dma_start(out=negC[:], in_=negC_d), writes=["negC"], dma_key="negC")
        load_fm(kb, KR, "KR", kr_d, 1, rows=64)
        for h in range(NH):
            kb.add("sp", (lambda e, h=h: e.dma_start(out=KN[:, h, :], in_=kn_d[h * 128:(h + 1) * 128, :])),
                   writes=[("KN", h)], dma_key=("KN", h % 4))
            kb.add("sp", (lambda e, h=h: e.dma_start(out=QN[:, h, :], in_=qn_d[h * 128:(h + 1) * 128, :])),
                   writes=[("QN", h)], dma_key=("QN", h % 4))
            kb.add("sp", (lambda e, h=h: e.dma_start(out=QR[:, h, :], in_=qr_d[h * 64:(h + 1) * 64, :])),
                   writes=[("QR", h)], dma_key=("QR", h % 4))
            if h == 0:
                load_tm(kb, V, "V", v_d, NKT)
        vkeys = [("V", i) for i in range(NKT)]
        for h in range(NH):
            for qb in range(NQB):
                q0 = qb * 512

                def score_fn(kt, h=h, qb=qb, q0=q0):
                    t = [(KN[:, h, kt * 128:(kt + 1) * 128], QN[:, h, q0:q0 + 512], [("KN", h), ("QN", h)]),
                         (KR[:, 0, kt * 128:(kt + 1) * 128], QR[:, h, q0:q0 + 512], [("KR", 0), ("QR", h)])]
                    j = kt - 4 * qb
                    if j >= 0:
                        t.append((cm.ident[:], negC[:, j, :], ["ident", "negC"]))
                    return t

                def v_fn(kt, h=h):
                    return V[:, kt, h * 128:(h + 1) * 128], [("V", kt)]

                po, pok, psm, psmk = attn_qblock(kb, R, list(range(4 * qb + 4)), score_fn, v_fn, lambda kt: 128)
                finalize_plain(kb, R, po, pok, psm, psmk, o_d, h * 128, q0)
        kb.finish()
    return nc


def consts_B_np(S=SEQ):
    negC, negW = neg_masks_np()
    n = np.arange(128)[:, None]
    t = np.arange(S)[None, :]
    negcm = np.where((16 * n + 31 <= t) & (n < 127), 0.0, NEG).astype(np.float32).astype(NPBF)
    n_cmp, n_slc = 127, S // 64
    c0 = np.arange(n_cmp) * 16
    c1 = c0 + 32
    s0 = np.arange(n_slc) * 64
    s1 = s0 + 64
    ovm = np.clip(np.minimum(c1[:, None], s1[None, :]) - np.maximum(c0[:, None], s0[None, :]), 0, None) / 32.0
    ov = np.zeros((128, n_slc), np.float32)
    ov[:127] = ovm
    E = (np.arange(S)[None, :] // 64 == np.arange(n_slc)[:, None]).astype(np.float32)
    tt = np.arange(S)
    cur = tt // 64
    j = np.arange(n_slc)[None, :]
    forced = (j == 0) | (j == cur[:, None]) | (j == cur[:, None] - 1)
    fbv = np.where(j > cur[:, None], -1e9, np.where(forced, 1e9, 0.0)).astype(np.float32)
    fb = np.ascontiguousarray(fbv.reshape(S // 128, 128, n_slc).transpose(1, 0, 2))
    sel4 = np.zeros((4, 4 * 128), np.float32)
    selrowneg = np.zeros((4, 4 * 512), np.float32)
    for h in range(4):
        sel4[h, h * 128:(h + 1) * 128] = 1.0
        selrowneg[h, h * 512:(h + 1) * 512] = -1.0
    selg = np.zeros((12, 12 * 128), np.float32)
    for i in range(12):
        selg[i, i * 128:(i + 1) * 128] = 1.0
    bf = lambda a: np.ascontiguousarray(a.astype(np.float32).astype(NPBF))
    return {"negC": negC, "negW": negW, "negcm": np.ascontiguousarray(negcm), "ov": bf(ov), "E": bf(E), "fb": fb,
            "sel4": bf(sel4), "selrowneg": bf(selrowneg), "selg": bf(selg), "ident": ident_np()}


def split3(kb, src, skey, npart, S, name):
    outs = []
    cur, curk = src, skey
    for i in range(3):
        c = kb.sb([npart, S], BF16, "%s_c%d" % (name, i))
        ck = "%s_c%d" % (name, i)
        kb.add("dve", (lambda e, c=c, cur=cur: e.tensor_copy(out=c[:, :], in_=cur[:, :])), reads=[curk], writes=[ck])
        outs.append((c, ck))
        if i < 2:
            r = kb.sb([npart, S], F32, "%s_r%d" % (name, i))
            rk = "%s_r%d" % (name, i)
            kb.add("dve", (lambda e, r=r, cur=cur, c=c: e.tensor_tensor(out=r[:, :], in0=cur[:, :], in1=c[:, :], op=ALU.subtract)),
                   reads=[curk, ck], writes=[rk])
            cur, curk = r, rk
    return outs


def _build_B_fox(kb, nc, cm, R, C, FM, TM, fl_d, fbias_d, o_d, S, QF, KF):
    NQB = S // 512
    ident = cm.ident
    negC = C["negC"]
    one1 = kb.sb([128, 1], F32, "one1f")
    kb.add("dve", lambda e: e.memset(one1[:], 1.0), writes=["one1"])
    FL = kb.sb([4, S], F32, "FL")
    FBI = kb.sb([4, 1], F32, "FBI")
    kb.add("sp", lambda e: e.dma_start(out=FL[:], in_=fl_d), writes=["FL"], dma_key="FL")
    kb.add("sp", lambda e: e.dma_start(out=FBI[:], in_=fbias_d), writes=["FBI"], dma_key="FBI")
    Z = kb.sb([4, S], F32, "Z")
    ONES4 = kb.sb([4, S], F32, "ONES4")
    CUM = kb.sb([4, S], F32, "CUM")
    kb.add("dve", lambda e: e.memset(ONES4[:], 1.0), writes=["ONES4"])
    kb.add("dve", lambda e: e.tensor_scalar(out=Z[:, :], in0=FL[:, :], scalar1=FBI[:, 0:1], scalar2=None, op0=ALU.add),
           reads=["FL", "FBI"], writes=["Z"])
    kb.add("act", lambda e: e.activation(out=Z[:, :], in_=Z[:, :], func=AF.Exp, scale=-1.0), reads=["Z"], writes=["Z"])
    kb.add("act", lambda e: e.activation(out=Z[:, :], in_=Z[:, :], func=AF.Ln, bias=one1[:4, 0:1]),
           reads=["Z", "one1"], writes=["Z"])
    kb.add("dve", lambda e: e.tensor_scalar_mul(out=Z[:, :], in0=Z[:, :], scalar1=-1.0), reads=["Z"], writes=["Z"])
    kb.add("dve", lambda e: e.tensor_tensor_scan(out=CUM[:, :], data0=ONES4[:, :], data1=Z[:, :], initial=0.0,
                                                 op0=ALU.mult, op1=ALU.add),
           reads=["Z", "ONES4"], writes=["CUM"])
    cparts = split3(kb, CUM, "CUM", 4, S, "cum")
    sel4, selrowneg = C["sel4"], C["selrowneg"]

    def fox_head(h):
        for qb in range(NQB):
            q0 = qb * 512

            def score_fn(kt, h=h, qb=qb, q0=q0):
                t = [(FM[:, KF + h, kt * 128:(kt + 1) * 128], FM[:, QF + h, q0:q0 + 512], [("FM", KF + h), ("FM", QF + h)])]
                for (c, ck) in cparts:
                    t.append((sel4[:, h * 128:(h + 1) * 128], c[:, q0:q0 + 512], ["c_sel4", ck]))
                    t.append((c[:, kt * 128:(kt + 1) * 128], selrowneg[:, h * 512:(h + 1) * 512], ["c_selrowneg", ck]))
                j = kt - 4 * qb
                if j >= 0:
                    t.append((ident[:], negC[:, j, :], ["ident", "c_negC"]))
                return t

            def v_fn(kt, h=h):
                return TM[:, kt, h * 128:(h + 1) * 128], [("TM", kt)]

            po, pok, psm, psmk = attn_qblock(kb, R, list(range(4 * qb + 4)), score_fn, v_fn, lambda kt: 128)
            finalize_plain(kb, R, po, pok, psm, psmk, o_d, h * 128, q0)


    for h in range(4):
        fox_head(h)
    kb.finish()
    return nc


def build_B(mode, S=SEQ, dbg=()):
    nc = bass.Bass("TRN2", target_bir_lowering=False)
    NKT = S // 128
    NQB = S // 512
    with contextlib.ExitStack() as st:
        kb = KB(nc, st)
        NCH = 12 if mode == "nsa" else 8
        TMW = 256 if mode == "nsa" else 512
        fm_d = kb.dram("fmB", [NCH * 128, S], BF16, "ExternalInput")
        tm_d = kb.dram("tmB", [S, TMW], BF16, "ExternalInput")
        g_d = fl_d = fbias_d = pe_d = w1_d = w2_d = None
        if mode == "nsa":
            g_d = kb.dram("gates", [12, S], F32, "ExternalInput")
            pe_d = kb.dram("peT", [128, 64], F32, "ExternalInput")
            w1_d = kb.dram("w1", [2, 4096, 256], F32, "ExternalInput")
            w2_d = kb.dram("w2", [2, 256, 128], F32, "ExternalInput")
            cd = {"negC": ([128, 4, 512], BF16), "negW": ([128, 4, 512], BF16), "negcm": ([128, S], BF16),
                  "ov": ([128, 32], BF16), "E": ([32, S], BF16), "fb": ([128, NKT, 32], F32), "selg": ([12, 1536], BF16)}
        else:
            fl_d = kb.dram("flogT", [4, S], F32, "ExternalInput")
            fbias_d = kb.dram("fbias", [4, 1], F32, "ExternalInput")
            cd = {"negC": ([128, 4, 512], BF16), "sel4": ([4, 512], BF16), "selrowneg": ([4, 2048], BF16)}
        c_d = {n: kb.dram(n, shp, dt, "ExternalInput") for n, (shp, dt) in cd.items()}
        id_d = kb.dram("ident", [128, 128], BF16, "ExternalInput")
        o_d = kb.dram("oT", [512, S], BF16, "ExternalOutput")
        selT_d = kb.dram("selT", [32, S], BF16, "ExternalOutput") if (mode == "nsa" and "dbgsel" in dbg) else None
        imp_d = kb.dram("impd", [S, 32], F32, "ExternalOutput") if (mode == "nsa" and "dbgsel" in dbg) else None
        cm = Common(kb, id_d, nf=7, nb=1)
        R = AttnRes(kb, cm)
        C = {n: load_tab(kb, cd[n][0], c_d[n], "c_" + n) for n in cd}
        FM = kb.sb([128, NCH, S], BF16, "FM")
        TM = kb.sb([128, NKT, TMW], BF16, "TM")
        load_fm(kb, FM, "FM", fm_d, NCH)
        load_tm(kb, TM, "TM", tm_d, NKT)
        QP, QR, KC, VC, KS, KW, QF, KF = 0, 4, 8, 9, 10, 11, 0, 4
        ident = cm.ident
        tiny = kb.sb([128, 1], F32, "tiny")
        one1 = kb.sb([128, 1], F32, "one1")
        kb.add("dve", lambda e: e.memset(tiny[:], 1e-30), writes=["tiny"])
        kb.add("dve", lambda e: e.memset(one1[:], 1.0), writes=["one1"])

        if mode == "fox":
            return _build_B_fox(kb, nc, cm, R, C, FM, TM, fl_d, fbias_d, o_d, S, QF, KF)
        G = kb.sb([12, S], F32, "G")
        kb.add("sp", lambda e: e.dma_start(out=G[:], in_=g_d), writes=["G"], dma_key="G")
        ghi = kb.sb([12, S], BF16, "ghi")
        gres = kb.sb([12, S], F32, "gres")
        glo = kb.sb([12, S], BF16, "glo")
        kb.add("dve", lambda e: e.tensor_copy(out=ghi[:, :], in_=G[:, :]), reads=["G"], writes=["ghi"])
        kb.add("dve", lambda e: e.tensor_tensor(out=gres[:, :], in0=G[:, :], in1=ghi[:, :], op=ALU.subtract),
               reads=["G", "ghi"], writes=["gres"])
        kb.add("dve", lambda e: e.tensor_copy(out=glo[:, :], in_=gres[:, :]), reads=["gres"], writes=["glo"])

        negC, negW, negcm, ovT, Em, fbT = C["negC"], C["negW"], C["negcm"], C["ov"], C["E"], C["fb"]
        selg = C["selg"]
        PE_ = kb.sb([128, 64], F32, "PEt")
        kb.add("sp", lambda e: e.dma_start(out=PE_[:], in_=pe_d), writes=["PEt"], dma_key="PEt")
        W1 = kb.sb([128, 32, 256], BF16, "W1")
        W2 = kb.sb([128, 2, 128], BF16, "W2")
        XB = kb.sb([128, 32, 128], BF16, "XB")
        HX = kb.sb([128, 128], F32, "HX")
        H2 = kb.sb([128, 128], F32, "H2")
        HT = kb.sb([128, 2, 128], BF16, "HT")
        KCC = kb.sb([128, 128], BF16, "KCC")
        VCC = kb.sb([128, 128], BF16, "VCC")
        for which in range(2):
            src = KC if which == 0 else VC
            w1keys = load_w_cols(kb, W1, "W1", w1_d[which].rearrange("(l p) n -> p l n", p=128), 32, [(0, 0, 256)])
            w2keys = load_w_cols(kb, W2, "W2", w2_d[which].rearrange("(c p) n -> p c n", p=128), 2, [(0, 0, 128)])
            for l in range(32):
                kb.add("dve", (lambda e, l=l, src=src, which=which: e.tensor_scalar(
                    out=XB[:, l, :127], in0=FM[:, src, l:l + 16 * 126 + 1:16],
                    scalar1=PE_[:, which * 32 + l:which * 32 + l + 1], scalar2=None, op0=ALU.add)),
                    reads=[("FM", src), "PEt"], writes=[("XB", l)])
            for hc in range(2):
                ph, phk = R.sc.next()
                for l in range(32):
                    kb.add("pe", (lambda e, ph=ph, l=l, hc=hc: e.matmul(
                        ph[:, :127], lhsT=W1[:, l, hc * 128:(hc + 1) * 128], rhs=XB[:, l, :127],
                        start=(l == 0), stop=(l == 31))),
                        reads=w1keys + [("XB", l)], writes=[phk])
                kb.add("act", (lambda e, ph=ph: e.activation(out=HX[:, :127], in_=ph[:, :127], func=AF.Copy)),
                       reads=[phk], writes=["HX"])
                kb.add("dve", lambda e: e.tensor_tensor(out=H2[:, :127], in0=HX[:, :127], in1=HX[:, :127], op=ALU.mult),
                       reads=["HX"], writes=["H2"])
                kb.add("dve", lambda e: e.tensor_tensor(out=H2[:, :127], in0=H2[:, :127], in1=HX[:, :127], op=ALU.mult),
                       reads=["HX", "H2"], writes=["H2"])
                kb.add("dve", lambda e: e.scalar_tensor_tensor(out=H2[:, :127], in0=H2[:, :127], scalar=0.044715,
                                                               in1=HX[:, :127], op0=ALU.mult, op1=ALU.add),
                       reads=["HX", "H2"], writes=["H2"])
                kb.add("act", lambda e: e.activation(out=H2[:, :127], in_=H2[:, :127], func=AF.Sigmoid, scale=1.5957691216057308),
                       reads=["H2"], writes=["H2"])
                kb.add("dve", (lambda e, hc=hc: e.tensor_tensor(out=HT[:, hc, :127], in0=HX[:, :127], in1=H2[:, :127], op=ALU.mult)),
                       reads=["HX", "H2"], writes=[("HT", hc)])
            pc, pck = R.sc.next()
            if which == 0:
                for hc in range(2):
                    kb.add("pe", (lambda e, pc=pc, hc=hc: e.matmul(pc[:, :127], lhsT=W2[:, hc, :], rhs=HT[:, hc, :127],
                                                                   start=(hc == 0), stop=(hc == 1))),
                           reads=w2keys + [("HT", 0), ("HT", 1)], writes=[pck])
                kb.add("dve", (lambda e, pc=pc: e.tensor_copy(out=KCC[:, :127], in_=pc[:, :127])), reads=[pck], writes=["KCC"])
            else:
                for hc in range(2):
                    kb.add("pe", (lambda e, pc=pc, hc=hc: e.matmul(pc[:127, :128], lhsT=HT[:, hc, :127], rhs=W2[:, hc, :],
                                                                   start=(hc == 0), stop=(hc == 1))),
                           reads=w2keys + [("HT", 0), ("HT", 1)], writes=[pck])
                kb.add("dve", (lambda e, pc=pc: e.tensor_copy(out=VCC[:127, :], in_=pc[:127, :128])), reads=[pck], writes=["VCC"])

        ACC = kb.sb([128, 4, 512], F32, "ACC")
        TMPW = Rot([(kb.sb([128, 512], F32), "tmpw%d" % i) for i in range(2)])
        TMPO = Rot([(kb.sb([128, 512], F32), "tmpo%d" % i) for i in range(2)])
        PN = Rot([(kb.sb([128, 512], BF16), "pn%d" % i) for i in range(2)])
        IMP2 = kb.sb([128, 4, 32], F32, "IMP2")
        TOP8 = kb.sb([128, 4, 8], F32, "TOP8")
        NSEL = kb.sb([128, 4, 32], BF16, "NSEL")
        NSELT = kb.sb([32, 512], BF16, "NSELT")
        impP, impk = R.misc
        psb, psbk = cm.psb_list[0]

        def finalize_gated(po, pok, psm, psmk, h, br, q0, rs=None, rsk=None):
            if rs is None:
                rs, rsk = R.rs.next()
                kb.add("dve", (lambda e: e.tensor_scalar(out=rs[:, :], in0=psm[:, :], scalar1=tiny[:, 0:1], scalar2=None, op0=ALU.max)),
                       reads=[psmk, "tiny"], writes=[rsk])
                kb.add("dve", (lambda e: e.reciprocal(out=rs[:, :], in_=rs[:, :])), reads=[rsk], writes=[rsk])
            i = h * 3 + br
            for gi, (gt, gk) in enumerate(((ghi, "ghi"), (glo, "glo"))):
                kb.add("pe", (lambda e, gt=gt, gi=gi: e.matmul(psm[:, :], lhsT=selg[:, i * 128:(i + 1) * 128], rhs=gt[:, q0:q0 + 512],
                                                               start=(gi == 0), stop=(gi == 1))),
                       reads=["c_selg", gk, rsk], writes=[psmk])
            w, wk = TMPW.next()
            kb.add("dve", (lambda e: e.tensor_tensor(out=w[:, :], in0=psm[:, :], in1=rs[:, :], op=ALU.mult)),
                   reads=[psmk, rsk], writes=[wk])
            if br == 0:
                kb.add("dve", (lambda e: e.tensor_tensor(out=ACC[:, h, :], in0=po[:, :], in1=w[:, :], op=ALU.mult)),
                       reads=[pok, wk], writes=[("ACC", h)])
            else:
                to, tok = TMPO.next()
                kb.add("dve", (lambda e: e.tensor_tensor(out=to[:, :], in0=po[:, :], in1=w[:, :], op=ALU.mult)),
                       reads=[pok, wk], writes=[tok])
                kb.add("dve", (lambda e: e.tensor_tensor(out=ACC[:, h, :], in0=ACC[:, h, :], in1=to[:, :], op=ALU.add)),
                       reads=[tok, ("ACC", h)], writes=[("ACC", h)])

        def nsa_qblock(qb):
            q0 = qb * 512
            for h in range(4):
                sc, sck = R.sc.next()
                kb.add("pe", (lambda e, sc=sc, h=h: e.matmul(sc[:127, :], lhsT=KCC[:, :127], rhs=FM[:, QP + h, q0:q0 + 512],
                                                            start=True, stop=False)),
                       reads=["KCC", ("FM", QP + h)], writes=[sck])
                kb.add("pe", (lambda e, sc=sc: e.matmul(sc[:127, :], lhsT=ident[:127, :127], rhs=negcm[:127, q0:q0 + 512],
                                                       start=False, stop=True)),
                       reads=["ident", "c_negcm"], writes=[sck])
                pt, ptk = R.pt.next()
                kb.add("act", (lambda e, pt=pt, sc=sc: e.activation(out=pt[:127, :], in_=sc[:127, :], func=AF.Exp)),
                       reads=[sck], writes=[ptk])
                po, pok = R.O.next()
                psm, psmk = R.S.next()
                kb.add("pe", (lambda e, po=po, pt=pt: e.matmul(po[:, :], lhsT=VCC[:127, :], rhs=pt[:127, :], start=True, stop=True)),
                       reads=["VCC", ptk], writes=[pok])
                kb.add("pe", (lambda e, psm=psm, pt=pt: e.matmul(psm[:, :], lhsT=R.ones[:127, :], rhs=pt[:127, :], start=True, stop=True)),
                       reads=["ones", ptk], writes=[psmk])
                rs, rsk = R.rs.next()
                kb.add("dve", (lambda e, rs=rs, psm=psm: e.tensor_scalar(out=rs[:, :], in0=psm[:, :], scalar1=tiny[:, 0:1], scalar2=None, op0=ALU.max)),
                       reads=[psmk, "tiny"], writes=[rsk])
                kb.add("dve", (lambda e, rs=rs: e.reciprocal(out=rs[:, :], in_=rs[:, :])), reads=[rsk], writes=[rsk])
                pn, pnk = PN.next()
                kb.add("dve", (lambda e, pn=pn, pt=pt, rs=rs: e.tensor_tensor(out=pn[:127, :], in0=pt[:127, :], in1=rs[:127, :], op=ALU.mult)),
                       reads=[ptk, rsk], writes=[pnk])
                for s_ in range(4):
                    kb.add("pe", (lambda e, pn=pn, s_=s_, h=h: e.matmul(
                        impP[:, h * 128 + s_ * 32:h * 128 + (s_ + 1) * 32], lhsT=pn[:127, s_ * 128:(s_ + 1) * 128],
                        rhs=ovT[:127, :], start=True, stop=True)),
                        reads=[pnk, "c_ov"], writes=[impk])
                finalize_gated(po, pok, psm, psmk, h, 0, q0, rs, rsk)
            kb.add("dve", (lambda e: e.tensor_tensor(out=IMP2[:, :, :], in0=impP[:, :128].rearrange("p (s j) -> p s j", j=32),
                                                     in1=fbT[:, 4 * qb:4 * qb + 4, :], op=ALU.add)),
                   reads=[impk, "c_fb"], writes=["IMP2"])
            for h in range(1, 4):
                kb.add("dve", (lambda e, h=h: e.tensor_tensor(
                    out=IMP2[:, :, :], in0=impP[:, h * 128:(h + 1) * 128].rearrange("p (s j) -> p s j", j=32),
                    in1=IMP2[:, :, :], op=ALU.add)),
                    reads=[impk, "IMP2"], writes=["IMP2"])
            if imp_d is not None:
                kb.add("sp", (lambda e: e.dma_start(out=imp_d[q0:q0 + 512, :].rearrange("(s p) j -> p s j", p=128), in_=IMP2[:, :, :])),
                       reads=["IMP2"], dma_key="IMP2_st")
            for s_ in range(4):
                kb.add("dve", (lambda e, s_=s_: e.max(out=TOP8[:, s_, :], in_=IMP2[:, s_, :])), reads=["IMP2"], writes=[("TOP8", s_)])
                kb.add("dve", (lambda e, s_=s_: e.tensor_scalar(out=NSEL[:, s_, :], in0=IMP2[:, s_, :], scalar1=TOP8[:, s_, 7:8],
                                                                scalar2=NEG, op0=ALU.is_lt, op1=ALU.mult)),
                       reads=["IMP2", ("TOP8", s_)], writes=[("NSEL", s_)])
                kb.add("pe", (lambda e, s_=s_: e.transpose(out=psb[:32, s_ * 128:(s_ + 1) * 128], in_=NSEL[:, s_, :], identity=ident[:])),
                       reads=[("NSEL", s_), "ident"], writes=[psbk])
            kb.add("dve", (lambda e: e.tensor_copy(out=NSELT[:, :], in_=psb[:32, :512])), reads=[psbk], writes=["NSELT"])
            if selT_d is not None:
                kb.add("sp", (lambda e: e.dma_start(out=selT_d[:, q0:q0 + 512], in_=NSELT[:, :])), reads=["NSELT"], dma_key="NSELT_st")
            for h in range(4):
                def score_slc(kt, h=h):
                    t = [(FM[:, KS, kt * 128:(kt + 1) * 128], FM[:, QR + h, q0:q0 + 512], [("FM", KS), ("FM", QR + h)]),
                         (Em[:, kt * 128:(kt + 1) * 128], NSELT[:, :], ["c_E", "NSELT"])]
                    j = kt - 4 * qb
                    if j >= 0:
                        t.append((ident[:], negC[:, j, :], ["ident", "c_negC"]))
                    return t

                po, pok, psm, psmk = attn_qblock(kb, R, list(range(4 * qb + 4)), score_slc,
                                                 lambda kt: (TM[:, kt, 0:128], [("TM", kt)]), lambda kt: 128)
                finalize_gated(po, pok, psm, psmk, h, 1, q0)

                def score_win(kt, h=h):
                    t = [(FM[:, KW, kt * 128:(kt + 1) * 128], FM[:, QR + h, q0:q0 + 512], [("FM", KW), ("FM", QR + h)])]
                    j = kt - 4 * qb
                    if j >= 0:
                        t.append((ident[:], negC[:, j, :], ["ident", "c_negC"]))
                    else:
                        t.append((ident[:], negW[:, j + 4, :], ["ident", "c_negW"]))
                    return t

                po, pok, psm, psmk = attn_qblock(kb, R, list(range(max(0, 4 * qb - 4), 4 * qb + 4)), score_win,
                                                 lambda kt: (TM[:, kt, 128:256], [("TM", kt)]), lambda kt: 128)
                finalize_gated(po, pok, psm, psmk, h, 2, q0)
                ost, ostk = R.ost.next()
                kb.add("act", (lambda e, ost=ost, h=h: e.activation(out=ost[:, :], in_=ACC[:, h, :], func=AF.Copy)),
                       reads=[("ACC", h)], writes=[ostk])
                kb.add("sp", (lambda e, ost=ost, h=h: e.dma_start(out=o_d[h * 128:(h + 1) * 128, q0:q0 + 512], in_=ost[:, :])),
                       reads=[ostk], dma_key=(ostk, "st"))

        for qb in range(NQB):
            nsa_qblock(qb)
        kb.finish()
    return nc


def emit_load_aT(kb, aT, akey, a_dram, KT):
    v = a_dram.rearrange("(kt p) t -> p kt t", p=128)
    for kt in range(KT):
        kb.add("sp", (lambda e, kt=kt: e.dma_start(out=aT[:, kt, :], in_=v[:, kt, :])),
               writes=[(akey, "all")], dma_key=(akey, "ld"))


def emit_outproj(kb, cm, X, xkey, NT, aT, akey, KT, W_dram, scr):
    Wv = wview(W_dram)
    for cg in range(D_MODEL // 512):
        wt, wk = scr["wout"].next()
        wkeys = load_w_cols(kb, wt, wk, Wv, KT, [(0, cg * 512, 512)], kgrp=4)
        for i in range(NT):
            pp, ppk = mm_tm(kb, cm, wt, wkeys, 512, aT, (akey, "all"), KT, i)
            kb.add("dve", (lambda e, pp=pp, i=i, cg=cg: e.tensor_tensor(
                out=X[:, i, cg * 512:(cg + 1) * 512], in0=pp[:, :], in1=X[:, i, cg * 512:(cg + 1) * 512], op=ALU.add)),
                reads=[ppk, (xkey, i)], writes=[(xkey, i)])


def build_C1(NT=8):
    T, D, KT = NT * 128, D_MODEL, D_MODEL // 128
    nc = bass.Bass("TRN2", target_bir_lowering=False)
    with contextlib.ExitStack() as st:
        kb = KB(nc, st)
        x_d = kb.dram("x", [T, D], F32, "ExternalInput")
        o_d = kb.dram("oT", [D, T], BF16, "ExternalInput")
        wo_d = kb.dram("w_o", [D, D], F32, "ExternalInput")
        ga_d = kb.dram("ga", [128, KT], F32, "ExternalInput")
        wia_d = kb.dram("w_in_a", [D, 2 * D_FF], F32, "ExternalInput")
        woa_d = kb.dram("w_out_a", [D_FF, D], F32, "ExternalInput")
        gb_d = kb.dram("gb", [128, KT], F32, "ExternalInput")
        wib_d = kb.dram("w_in_b", [D, 2 * D_FF], F32, "ExternalInput")
        wob_d = kb.dram("w_out_b", [D_FF, D], F32, "ExternalInput")
        id_d = kb.dram("ident", [128, 128], BF16, "ExternalInput")
        y_d = kb.dram("y", [T, D], F32, "ExternalOutput")
        cm = Common(kb, id_d)
        X = kb.sb([128, NT, D], F32, "X")
        bufs = make_ffn_bufs(kb, NT, D, D_FF, 11)
        emit_load_x(kb, X, "X", x_d, NT)
        emit_load_aT(kb, bufs["xnT"], "xnTa", o_d, KT)
        emit_outproj(kb, cm, X, "X", NT, bufs["xnT"], "xnTa", KT, wo_d, bufs["scr"])
        kb.add("dve", lambda e: e.memset(cm.eps[:], EPS), writes=[("xnTa", "all"), "eps"] + [("xnT", i) for i in range(NT)])
        emit_ffn(kb, cm, X, "X", NT, D, D_FF, ga_d, wia_d, woa_d, 0.5, bufs)
        emit_ffn(kb, cm, X, "X", NT, D, D_FF, gb_d, wib_d, wob_d, 0.5, bufs)
        emit_store_x(kb, X, "X", y_d, NT)
        kb.finish()
    return nc


def build_E(NT=8):
    T, D, KT = NT * 128, D_MODEL, D_MODEL // 128
    nc = bass.Bass("TRN2", target_bir_lowering=False)
    with contextlib.ExitStack() as st:
        kb = KB(nc, st)
        x_d = kb.dram("x", [T, D], F32, "ExternalInput")
        o_d = kb.dram("oT", [D, T], BF16, "ExternalInput")
        wo_d = kb.dram("w_o", [D, D], F32, "ExternalInput")
        ga_d = kb.dram("ga", [128, KT], F32, "ExternalInput")
        wia_d = kb.dram("w_in_a", [D, 2 * D_FF], F32, "ExternalInput")
        woa_d = kb.dram("w_out_a", [D_FF, D], F32, "ExternalInput")
        gf_d = kb.dram("gfull", [128, D], F32, "ExternalInput")
        id_d = kb.dram("ident", [128, 128], BF16, "ExternalInput")
        y_d = kb.dram("y", [T, D], F32, "ExternalOutput")
        cm = Common(kb, id_d)
        X = kb.sb([128, NT, D], F32, "X")
        bufs = make_ffn_bufs(kb, NT, D, D_FF, 11)
        scr = bufs["scr"]
        GF = load_tab(kb, [128, D], gf_d, "GF")
        YS = kb.sb([128, D], F32, "YS")
        emit_load_x(kb, X, "X", x_d, NT)
        emit_load_aT(kb, bufs["xnT"], "xnTa", o_d, KT)
        emit_outproj(kb, cm, X, "X", NT, bufs["xnT"], "xnTa", KT, wo_d, scr)
        kb.add("dve", lambda e: e.memset(cm.eps[:], EPS), writes=[("xnTa", "all"), "eps"] + [("xnT", i) for i in range(NT)])
        emit_ffn(kb, cm, X, "X", NT, D, D_FF, ga_d, wia_d, woa_d, 0.5, bufs)
        for i in range(NT):
            sq, sqk = scr["sq"].next()
            ss, ssk = scr["ss"].next()
            kb.add("act", (lambda e, i=i, sq=sq, ss=ss: e.activation(out=sq[:, :D], in_=X[:, i, :], func=AF.Square, accum_out=ss[:, 0:1])),
                   reads=[("X", i)], writes=[sqk, ssk])
            kb.add("act", (lambda e, ss=ss: e.activation(out=ss[:, 1:2], in_=ss[:, 0:1], func=AF.Sqrt, scale=1.0 / D, bias=cm.eps[:, 0:1])),
                   reads=[ssk, "eps"], writes=[ssk])
            kb.add("dve", (lambda e, ss=ss: e.reciprocal(out=ss[:, 2:3], in_=ss[:, 1:2])), reads=[ssk], writes=[ssk])
            kb.add("act", (lambda e, i=i, ss=ss: e.activation(out=YS[:, :], in_=X[:, i, :], func=AF.Copy, scale=ss[:, 2:3])),
                   reads=[("X", i), ssk], writes=["YS"])
            kb.add("dve", (lambda e: e.tensor_tensor(out=YS[:, :], in0=YS[:, :], in1=GF[:, :], op=ALU.mult)),
                   reads=["YS", "GF"], writes=["YS"])
            kb.add("sp", (lambda e, i=i: e.dma_start(out=y_d[i * 128:(i + 1) * 128, :], in_=YS[:, :])),
                   reads=["YS"], dma_key="YS_st")
        kb.finish()
    return nc


def build_C2(NT=8):
    T, D, KT = NT * 128, D_MODEL, D_MODEL // 128
    nc = bass.Bass("TRN2", target_bir_lowering=False)
    with contextlib.ExitStack() as st:
        kb = KB(nc, st)
        x_d = kb.dram("x", [T, D], F32, "ExternalInput")
        gm_d = kb.dram("gm", [128, KT], F32, "ExternalInput")
        wi_d = kb.dram("mla_w_in", [D, 1088], F32, "ExternalInput")
        qn_g_d = kb.dram("qnorm", [128, 4], F32, "ExternalInput")
        kvn_g_d = kb.dram("kvnorm", [128, 4], F32, "ExternalInput")
        wuq_d = kb.dram("w_uq", [512, 3072], F32, "ExternalInput")
        wukv_d = kb.dram("w_ukv", [512, 4096], F32, "ExternalInput")
        id_d = kb.dram("ident", [128, 128], BF16, "ExternalInput")
        tab_d = {n: kb.dram(n, [64, T], F32, "ExternalInput") for n in ("cosq", "sinq", "cosk", "sink")}
        qn_d = kb.dram("qn", [2048, T], BF16, "ExternalOutput")
        qr_d = kb.dram("qr", [1024, T], BF16, "ExternalOutput")
        kn_d = kb.dram("kn", [2048, T], BF16, "ExternalOutput")
        kr_d = kb.dram("kr", [64, T], BF16, "ExternalOutput")
        v_d = kb.dram("v", [T, 2048], BF16, "ExternalOutput")
        cm = Common(kb, id_d)
        ev = Evac(kb)
        scr = {
            "sq": Rot([(kb.sb([128, D], BF16), "sq0")]),
            "ss": Rot([(kb.sb([128, 4], F32), "ss%d" % i) for i in range(2)]),
            "xs": Rot([(kb.sb([128, D], BF16), "xs%d" % i) for i in range(1)]),
            "win": Rot([(kb.sb([128, KT, 128], BF16), "win%d" % i) for i in range(4)]),
            "wout": Rot([(kb.sb([128, KT, 512], BF16), "wout%d" % i) for i in range(2)]),
        }
        make_proj_scr(kb, scr)
        xnT = kb.sb([128, KT, T], BF16, "xnT")
        gT = kb.sb([128, KT], F32, "gTvec")
        XR = kb.sb([128, 2, D], F32, "XR")
        CQ = kb.sb([128, NT, 512], F32, "CQ")
        CKV = kb.sb([128, NT, 512], F32, "CKV")
        cqT = kb.sb([128, 4, T], BF16, "cqT")
        ckvT = kb.sb([128, 4, T], BF16, "ckvT")
        gq = kb.sb([128, 4], F32, "gq")
        gkv = kb.sb([128, 4], F32, "gkv")
        tabs_sb = {}
        for n in tab_d:
            t = kb.sb([64, T], F32, "t_" + n)
            kb.add("sp", (lambda e, t=t, n=n: e.dma_start(out=t[:], in_=tab_d[n])), writes=["t_" + n], dma_key="t_" + n)
            tabs_sb[n] = t
        tabs = {"q": (tabs_sb["cosq"], tabs_sb["sinq"], ["t_cosq", "t_sinq"]),
                "k": (tabs_sb["cosk"], tabs_sb["sink"], ["t_cosk", "t_sink"])}
        kb.add("sp", lambda e: e.dma_start(out=gT[:, :KT], in_=gm_d), writes=["gTvec"], dma_key="gT")
        kb.add("sp", lambda e: e.dma_start(out=gq[:, :], in_=qn_g_d), writes=["gq"], dma_key="gq")
        kb.add("sp", lambda e: e.dma_start(out=gkv[:, :], in_=kvn_g_d), writes=["gkv"], dma_key="gkv")

        def pre(i):
            sl = i % 2
            kb.add("sp", (lambda e, i=i, sl=sl: e.dma_start(out=XR[:, sl, :], in_=x_d[i * 128:(i + 1) * 128, :])),
                   writes=[("XR", sl)], dma_key=("XR", sl))
            return XR[:, sl, :], ("XR", sl)

        emit_norm_T(kb, cm, None, None, NT, D, gT, "gTvec", xnT, "xnT", scr, pre=pre)
        akeys = [("xnT", i) for i in range(NT)]
        Wv = wview(wi_d)

        def sink_to(Ct, ckey):
            def f(i, pp, ppk):
                ev.copy(Ct[:, i, :], pp[:, :512], [ppk], [(ckey, i)])
            return f

        groups = [dict(segs=[(0, 0, 512)], n=512, sink=sink_to(CQ, "CQ")),
                  dict(segs=[(0, 512, 512)], n=512, sink=sink_to(CKV, "CKV"))]
        emit_proj_tm(kb, cm, ev, Wv, KT, xnT, lambda i: ("xnT", i), NT, groups, scr)
        emit_proj_fm(kb, cm, ev, Wv, KT, xnT, akeys, T,
                     [dict(col0=1024, m=64, kind="rope", half=32, tab="k", row0=0)], kr_d, tabs, scr)
        emit_norm_T(kb, cm, CQ, "CQ", NT, 512, gq, "gq", cqT, "cqT", scr)
        emit_norm_T(kb, cm, CKV, "CKV", NT, 512, gkv, "gkv", ckvT, "ckvT", scr)
        qkeys = [("cqT", i) for i in range(NT)]
        kvkeys = [("ckvT", i) for i in range(NT)]
        chunks = []
        for h in range(16):
            chunks.append(dict(col0=h * 192, m=128, kind="scale", scale=SC192, row0=h * 128, out=qn_d))
            chunks.append(dict(col0=h * 192 + 128, m=64, kind="rope", half=32, tab="q", row0=h * 64, out=qr_d))
        emit_proj_fm(kb, cm, ev, wview(wuq_d), 4, cqT, qkeys, T, chunks, qn_d, tabs, scr)
        chunks = [dict(col0=h * 256, m=128, kind="copy", row0=h * 128) for h in range(16)]
        emit_proj_fm(kb, cm, ev, wview(wukv_d), 4, ckvT, kvkeys, T, chunks, kn_d, tabs, scr)
        groups = [dict(segs=[(j * 128, (4 * g + j) * 256 + 128, 128) for j in range(4)], n=512, out=(v_d, g * 512), dt="bf16")
                  for g in range(4)]
        emit_proj_tm(kb, cm, ev, wview(wukv_d), 4, ckvT, lambda i: ("ckvT", i), NT, groups, scr)
        kb.finish()
    return nc


TOK = 1024


def _run(nc, in_maps):
    res = run_bass_kernel_spmd(nc, in_maps, core_ids=list(range(8)))
    return res.results


def _cat_t(rs, name, b):
    return np.concatenate([np.asarray(rs[2 * b][name]), np.asarray(rs[2 * b + 1][name])], axis=1)


def _cat_r(rs, name, b):
    return np.concatenate([np.asarray(rs[2 * b][name]), np.asarray(rs[2 * b + 1][name])], axis=0)


def kernel(x, ffn1_norm, ffn1_w_in, ffn1_w_out, mix_norm, ffn2_norm, ffn2_w_in, ffn2_w_out,
           hyb_w_in, hyb_w_out, nsa_cmp_pe, nsa_cmp_w1, nsa_cmp_w2, fox_f_bias,
           mla_w_in, mla_q_norm, mla_kv_norm, mla_w_uq, mla_w_ukv, mla_w_out, final_norm):
    f = lambda a: np.asarray(a, dtype=np.float32)
    x = f(x)
    ffn1_norm, ffn1_w_in, ffn1_w_out = f(ffn1_norm), f(ffn1_w_in), f(ffn1_w_out)
    ffn2_norm, ffn2_w_in, ffn2_w_out = f(ffn2_norm), f(ffn2_w_in), f(ffn2_w_out)
    mix_norm, hyb_w_in, hyb_w_out = f(mix_norm), f(hyb_w_in), f(hyb_w_out)
    nsa_cmp_pe, nsa_cmp_w1, nsa_cmp_w2, fox_f_bias = f(nsa_cmp_pe), f(nsa_cmp_w1), f(nsa_cmp_w2), f(fox_f_bias)
    mla_w_in, mla_q_norm, mla_kv_norm = f(mla_w_in), f(mla_q_norm), f(mla_kv_norm)
    mla_w_uq, mla_w_ukv, mla_w_out, final_norm = f(mla_w_uq), f(mla_w_ukv), f(mla_w_out), f(final_norm)
    ca = np.ascontiguousarray
    T = TOK
    ident = ident_np()
    xf = ca(x.reshape(BATCH * SEQ, D_MODEL))

    in_maps = []
    for c in range(8):
        pos = np.arange((c % 2) * T, (c % 2 + 1) * T)
        cq, sq = rope_tables(128, pos, SC128)
        ck, sk = rope_tables(128, pos, 1.0)
        in_maps.append({"x": xf[c * T:(c + 1) * T], "g1": lay_vec(ffn1_norm[0]), "w_in": ffn1_w_in[0],
                        "w_out": ffn1_w_out[0], "gm": lay_vec(mix_norm[0]), "hyb_w_in": hyb_w_in[0], "ident": ident,
                        "cosq": cq, "sinq": sq, "cosk": ck, "sink": sk})
    rA = _run(build_A(), in_maps)

    CB = consts_B_np()
    peT = ca(np.concatenate([nsa_cmp_pe[0, 0].T, nsa_cmp_pe[0, 1].T], axis=1))
    maps_n, maps_f = [], []
    for b in range(BATCH):
        fm_b = _cat_t(rA, "fmA", b)
        tm_b = _cat_r(rA, "tmA", b)
        g_b = _cat_t(rA, "gatesT", b)
        fl_b = _cat_r(rA, "flog", b)
        for g in range(2):
            fmB = np.concatenate([fm_b[g * 512:(g + 1) * 512], fm_b[1024 + g * 512:1024 + (g + 1) * 512],
                                  fm_b[2048 + g * 128:2048 + (g + 1) * 128], fm_b[2304 + g * 128:2304 + (g + 1) * 128],
                                  fm_b[2560 + g * 128:2560 + (g + 1) * 128], fm_b[2816 + g * 128:2816 + (g + 1) * 128]], axis=0)
            tmB = np.concatenate([tm_b[:, g * 128:(g + 1) * 128], tm_b[:, 256 + g * 128:256 + (g + 1) * 128]], axis=1)
            m = {"fmB": ca(fmB), "tmB": ca(tmB), "gates": ca(g_b[g * 12:(g + 1) * 12]), "peT": peT,
                 "w1": nsa_cmp_w1[0], "w2": nsa_cmp_w2[0], "ident": ident}
            for n in ("negC", "negW", "negcm", "ov", "E", "fb", "selg"):
                m[n] = CB[n]
            maps_n.append(m)
            hh = g
            fmF = np.concatenate([fm_b[3072 + hh * 512:3072 + (hh + 1) * 512], fm_b[4096 + hh * 512:4096 + (hh + 1) * 512]], axis=0)
            m = {"fmB": ca(fmF), "tmB": ca(tm_b[:, 512 + hh * 512:512 + (hh + 1) * 512]),
                 "flogT": ca(fl_b[:, hh * 4:(hh + 1) * 4].T), "fbias": ca(fox_f_bias[0, hh * 4:(hh + 1) * 4].reshape(4, 1)),
                 "ident": ident}
            for n in ("negC", "sel4", "selrowneg"):
                m[n] = CB[n]
            maps_f.append(m)
    rBn = _run(build_B("nsa"), maps_n)
    rBf = _run(build_B("fox"), maps_f)

    in_maps = []
    for c in range(8):
        b, half = c // 2, c % 2
        oT = np.concatenate([np.asarray(rBn[2 * b]["oT"]), np.asarray(rBn[2 * b + 1]["oT"]),
                             np.asarray(rBf[2 * b]["oT"]), np.asarray(rBf[2 * b + 1]["oT"])], axis=0)
        in_maps.append({"x": np.asarray(rA[c]["x1"]), "oT": ca(oT[:, half * T:(half + 1) * T]), "w_o": hyb_w_out[0],
                        "ga": lay_vec(ffn2_norm[0]), "w_in_a": ffn2_w_in[0], "w_out_a": ffn2_w_out[0],
                        "gb": lay_vec(ffn1_norm[1]), "w_in_b": ffn1_w_in[1], "w_out_b": ffn1_w_out[1], "ident": ident})
    rC1 = _run(build_C1(), in_maps)

    in_maps = []
    for c in range(8):
        pos = np.arange((c % 2) * T, (c % 2 + 1) * T)
        cq, sq = rope_tables(64, pos, SC192)
        ck, sk = rope_tables(64, pos, 1.0)
        in_maps.append({"x": np.asarray(rC1[c]["y"]), "gm": lay_vec(mix_norm[1]), "mla_w_in": mla_w_in[0],
                        "qnorm": lay_vec(mla_q_norm[0]), "kvnorm": lay_vec(mla_kv_norm[0]),
                        "w_uq": mla_w_uq[0], "w_ukv": mla_w_ukv[0], "ident": ident,
                        "cosq": cq, "sinq": sq, "cosk": ck, "sink": sk})
    rC2 = _run(build_C2(), in_maps)

    negC = CB["negC"]
    in_maps = []
    for b in range(BATCH):
        qn_b, qr_b, kn_b, kr_b = _cat_t(rC2, "qn", b), _cat_t(rC2, "qr", b), _cat_t(rC2, "kn", b), _cat_t(rC2, "kr", b)
        v_b = _cat_r(rC2, "v", b)
        for hh in range(2):
            in_maps.append({"qn": ca(qn_b[hh * 1024:(hh + 1) * 1024]), "qr": ca(qr_b[hh * 512:(hh + 1) * 512]),
                            "kn": ca(kn_b[hh * 1024:(hh + 1) * 1024]), "kr": ca(kr_b),
                            "v": ca(v_b[:, hh * 1024:(hh + 1) * 1024]), "negC": negC, "ident": ident})
    rD = _run(build_D(), in_maps)

    gfull = ca(np.broadcast_to(final_norm.reshape(1, D_MODEL), (128, D_MODEL)))
    in_maps = []
    for c in range(8):
        b, half = c // 2, c % 2
        oT = np.concatenate([np.asarray(rD[2 * b]["oT"]), np.asarray(rD[2 * b + 1]["oT"])], axis=0)
        in_maps.append({"x": np.asarray(rC1[c]["y"]), "oT": ca(oT[:, half * T:(half + 1) * T]), "w_o": mla_w_out[0],
                        "ga": lay_vec(ffn2_norm[1]), "w_in_a": ffn2_w_in[1], "w_out_a": ffn2_w_out[1],
                        "gfull": gfull, "ident": ident})
    rE = _run(build_E(), in_maps)
    out = np.concatenate([np.asarray(rE[c]["y"]) for c in range(8)], axis=0)
    return out.reshape(BATCH, SEQ, D_MODEL).astype(np.float32)
```

```python
import contextlib
import numpy as np
import ml_dtypes
import concourse.bass as bass
import concourse.mybir as mybir
from concourse.bass_utils import run_bass_kernel_spmd

F32 = mybir.dt.float32
BF16 = mybir.dt.bfloat16
AF = mybir.ActivationFunctionType
ALU = mybir.AluOpType
AX = mybir.AxisListType
NPBF = ml_dtypes.bfloat16

D_MODEL = 2048
SEQ = 2048
BATCH = 4
D_FF = 5632
EPS = 1e-6


class _Op:
    __slots__ = ("eng", "fn", "deps", "dma_key", "signal", "count")

    def __init__(self, eng, fn, dma_key):
        self.eng = eng
        self.fn = fn
        self.deps = set()
        self.dma_key = dma_key
        self.signal = False
        self.count = 0


class Sched:
    ENGS = ("pe", "act", "dve", "pool", "sp")

    def __init__(self):
        self.ops = []
        self.kw = {}
        self.kr = {}
        self.excl = set()

    def add(self, eng, fn, reads=(), writes=(), dma_key=None):
        idx = len(self.ops)
        op = _Op(eng, fn, dma_key)
        for k in reads:
            w = self.kw.get(k)
            if w is not None:
                op.deps.add(w)
            if k in self.excl:
                for r in self.kr.get(k, ()):
                    if self.ops[r].eng != eng:
                        op.deps.add(r)
        for k in writes:
            w = self.kw.get(k)
            if w is not None:
                op.deps.add(w)
            for r in self.kr.get(k, ()):
                op.deps.add(r)
        for k in reads:
            self.kr.setdefault(k, []).append(idx)
        for k in writes:
            self.kw[k] = idx
            self.kr[k] = []
        op.deps.discard(idx)
        self.ops.append(op)
        return idx

    @staticmethod
    def _skip(dop, o):
        return dop.dma_key is None and dop.eng == "pe" and o.eng == "pe" and o.dma_key is None

    def emit(self, nc, stack):
        ops = self.ops
        for o in ops:
            for d in o.deps:
                dop = ops[d]
                if dop.dma_key is None and not self._skip(dop, o):
                    dop.signal = True
        eng_cnt = {e: 0 for e in self.ENGS}
        dma_cnt = {}
        for o in ops:
            if o.dma_key is not None:
                dma_cnt[o.dma_key] = dma_cnt.get(o.dma_key, 0) + 16
                o.count = dma_cnt[o.dma_key]
            elif o.signal:
                eng_cnt[o.eng] += 1
                o.count = eng_cnt[o.eng]
        assert max(eng_cnt.values()) < 60000, eng_cnt
        eng_sem = {e: stack.enter_context(nc.semaphore("es_" + e)) for e in self.ENGS}
        dma_sem = {}
        for i, k in enumerate(dma_cnt):
            dma_sem[k] = stack.enter_context(nc.semaphore("ds_%d" % i))
        streams = {e: [o for o in ops if o.eng == e] for e in self.ENGS}
        block = stack.enter_context(nc.Block())

        def make(e):
            def body(engobj):
                waited = {}
                for o in streams[e]:
                    need = {}
                    for d in o.deps:
                        dop = ops[d]
                        if dop.dma_key is not None:
                            key = ("d", dop.dma_key)
                            sem = dma_sem[dop.dma_key]
                        else:
                            if self._skip(dop, o):
                                continue
                            key = ("e", dop.eng)
                            sem = eng_sem[dop.eng]
                        if need.get(key, (None, 0))[1] < dop.count:
                            need[key] = (sem, dop.count)
                    for key, (sem, v) in need.items():
                        if waited.get(key, 0) < v:
                            engobj.wait_ge(sem, v)
                            waited[key] = v
                    ins = o.fn(engobj)
                    if o.dma_key is not None:
                        ins.then_inc(dma_sem[o.dma_key], 16)
                    elif o.signal:
                        ins.then_inc(eng_sem[e], 1)
                if e == "sp":
                    for k, sem in dma_sem.items():
                        engobj.wait_ge(sem, dma_cnt[k])
            return body

        block.tensor(make("pe"))
        block.scalar(make("act"))
        block.vector(make("dve"))
        block.gpsimd(make("pool"))
        block.sync(make("sp"))


class KB:
    def __init__(self, nc, stack):
        self.nc = nc
        self.stack = stack
        self.s = Sched()
        self.n = 0
        self.psn = 0

    def sb(self, shape, dt, name=None):
        self.n += 1
        return self.stack.enter_context(
            self.nc.sbuf_tensor("s_" + (name or ("sb%d" % self.n)), list(shape), dt))

    def ps(self, shape, dt, name=None):
        self.psn += 1
        return self.stack.enter_context(
            self.nc.psum_tensor("p_" + (name or ("ps%d" % self.psn)), list(shape), dt))

    def dram(self, name, shape, dt, kind):
        return self.nc.dram_tensor(name, list(shape), dt, kind=kind).ap()

    def add(self, *a, **k):
        return self.s.add(*a, **k)

    def finish(self):
        self.s.emit(self.nc, self.stack)


class Rot:
    def __init__(self, bufs):
        self.bufs = bufs
        self.i = 0

    def next(self):
        b = self.bufs[self.i % len(self.bufs)]
        self.i += 1
        return b


class Common:
    def __init__(self, kb, ident_dram, nf=6, nb=2):
        self.kb = kb
        self.psf_list = [(kb.ps([128, 512], F32), "psf%d" % i) for i in range(nf)]
        self.psb_list = [(kb.ps([128, 1024], BF16), "psb%d" % i) for i in range(nb)]
        self.psf = Rot(self.psf_list)
        self.psb = Rot(self.psb_list)
        kb.s.excl.update(k for _, k in self.psf_list + self.psb_list)
        self.ident = kb.sb([128, 128], BF16, "ident")
        kb.add("sp", lambda e: e.dma_start(out=self.ident[:], in_=ident_dram),
               writes=["ident"], dma_key="ident")
        self.eps = kb.sb([128, 1], F32, "eps")
        kb.add("dve", lambda e: e.memset(self.eps[:], EPS), writes=["eps"])


def emit_load_x(kb, X, xkey, x_dram, NT):
    for i in range(NT):
        kb.add("sp", (lambda e, i=i: e.dma_start(out=X[:, i, :], in_=x_dram[i * 128:(i + 1) * 128, :])),
               writes=[(xkey, i)], dma_key=(xkey, "ld", i))


def emit_store_x(kb, X, xkey, out_dram, NT):
    for i in range(NT):
        kb.add("sp", (lambda e, i=i: e.dma_start(out=out_dram[i * 128:(i + 1) * 128, :], in_=X[:, i, :])),
               reads=[(xkey, i)], dma_key=(xkey, "st", i))


def emit_norm_T(kb, cm, X, xkey, NT, D, gT, gkey, xnT, xnkey, scr, pre=None):
    KT = D // 128
    for i in range(NT):
        if pre is not None:
            Xi, xik = pre(i)
        else:
            Xi, xik = X[:, i, :], (xkey, i)
        sq, sqk = scr["sq"].next()
        ss, ssk = scr["ss"].next()
        xs, xsk = scr["xs"].next()
        kb.add("act", (lambda e, i=i, sq=sq, ss=ss, Xi=Xi: e.activation(
            out=sq[:, :D], in_=Xi, func=AF.Square, accum_out=ss[:, 0:1])),
            reads=[xik], writes=[sqk, ssk])
        kb.add("act", (lambda e, ss=ss: e.activation(
            out=ss[:, 1:2], in_=ss[:, 0:1], func=AF.Sqrt, scale=1.0 / D, bias=cm.eps[:, 0:1])),
            reads=[ssk, "eps"], writes=[ssk])
        kb.add("dve", (lambda e, ss=ss: e.reciprocal(out=ss[:, 2:3], in_=ss[:, 1:2])),
            reads=[ssk], writes=[ssk])
        kb.add("act", (lambda e, i=i, xs=xs, ss=ss, Xi=Xi: e.activation(
            out=xs[:, :D], in_=Xi, func=AF.Copy, scale=ss[:, 2:3])),
            reads=[xik, ssk], writes=[xsk])
        for k0 in range(0, KT, 8):
            nk = min(8, KT - k0)
            (pb, pbk) = cm.psb.next()
            for j in range(nk):
                kb.add("pe", (lambda e, j=j, k0=k0, pb=pb, xs=xs: e.transpose(
                    out=pb[:, j * 128:(j + 1) * 128], in_=xs[:, (k0 + j) * 128:(k0 + j + 1) * 128],
                    identity=cm.ident[:])),
                    reads=[xsk, "ident"], writes=[pbk])
            kb.add("dve", (lambda e, k0=k0, nk=nk, pb=pb, i=i: e.tensor_tensor(
                out=xnT[:, k0:k0 + nk, i * 128:(i + 1) * 128],
                in0=pb[:, :nk * 128].rearrange("p (k t) -> p k t", t=128),
                in1=gT[:, k0:k0 + nk].unsqueeze(2).to_broadcast([128, nk, 128]),
                op=ALU.mult)),
                reads=[pbk, gkey], writes=[(xnkey, i)])


def emit_ffn(kb, cm, X, xkey, NT, D, F, g_dram, w_in, w_out, res_scale, bufs):
    T = NT * 128
    KT = D // 128
    FC = F // 128
    xnT, gT, gTb, scr = bufs["xnT"], bufs["gT"], bufs["gTb"], bufs["scr"]
    CG = bufs["CG"]
    assert FC % CG == 0
    NG = FC // CG
    gkey = "gTvec"
    kb.add("sp", lambda e: e.dma_start(out=gT[:, :KT], in_=g_dram), writes=[gkey], dma_key="gT")
    emit_norm_T(kb, cm, X, xkey, NT, D, gT, gkey, xnT, "xnT", scr)
    xn_reads = [("xnT", i) for i in range(NT)]
    w_in_v = w_in.rearrange("(kt p) n -> p kt n", p=128)
    w_out_v = w_out.rearrange("(c p) n -> p c n", p=128)
    TH = (T + 511) // 512
    NCG = D // 512
    for g in range(NG):
        for cc in range(CG):
            c = g * CG + cc
            wg, wgk = scr["win"].next()
            wu, wuk = scr["win"].next()
            wgkeys = load_w_cols(kb, wg, wgk, w_in_v, KT, [(0, c * 128, 128)])
            wukeys = load_w_cols(kb, wu, wuk, w_in_v, KT, [(0, F + c * 128, 128)])
            for th in range(TH):
                t0 = th * 512
                tn = min(512, T - t0)
                pg, pgk = cm.psf.next()
                pu, puk = cm.psf.next()
                for (wt, wks, pp, ppk) in ((wg, wgkeys, pg, pgk), (wu, wukeys, pu, puk)):
                    for kt in range(KT):
                        kb.add("pe", (lambda e, wt=wt, pp=pp, kt=kt, t0=t0, tn=tn: e.matmul(
                            pp[:, :tn], lhsT=wt[:, kt, :], rhs=xnT[:, kt, t0:t0 + tn],
                            start=(kt == 0), stop=(kt == KT - 1))),
                            reads=wks + xn_reads, writes=[ppk])
                sg, sgk = scr["sg"].next()
                kb.add("act", (lambda e, sg=sg, pg=pg, tn=tn: e.activation(
                    out=sg[:, :tn], in_=pg[:, :tn], func=AF.Silu)),
                    reads=[pgk], writes=[sgk])
                kb.add("dve", (lambda e, sg=sg, pu=pu, cc=cc, t0=t0, tn=tn: e.tensor_tensor(
                    out=gTb[:, cc, t0:t0 + tn], in0=sg[:, :tn], in1=pu[:, :tn], op=ALU.mult)),
                    reads=[sgk, puk], writes=[("gTb", cc)])
        for cg in range(NCG):
            wo, wok = scr["wout"].next()
            wokeys = load_w_cols(kb, wo, wok, w_out_v[:, g * CG:(g + 1) * CG, :], CG,
                                 [(0, cg * 512, 512)], kgrp=4)
            for i in range(NT):
                po, pok = cm.psf.next()
                for cc in range(CG):
                    kb.add("pe", (lambda e, po=po, wo=wo, cc=cc, i=i: e.matmul(
                        po[:, :], lhsT=gTb[:, cc, i * 128:(i + 1) * 128], rhs=wo[:, cc, :],
                        start=(cc == 0), stop=(cc == CG - 1))),
                        reads=[("gTb", cc)] + wokeys, writes=[pok])
                kb.add("dve", (lambda e, po=po, i=i, cg=cg: e.scalar_tensor_tensor(
                    out=X[:, i, cg * 512:(cg + 1) * 512], in0=po[:, :], scalar=res_scale,
                    in1=X[:, i, cg * 512:(cg + 1) * 512], op0=ALU.mult, op1=ALU.add)),
                    reads=[pok, (xkey, i)], writes=[(xkey, i)])


def make_ffn_bufs(kb, NT, D, F, CG):
    T = NT * 128
    KT = D // 128
    scr = {
        "sq": Rot([(kb.sb([128, D], BF16), "sq0")]),
        "ss": Rot([(kb.sb([128, 4], F32), "ss%d" % i) for i in range(2)]),
        "xs": Rot([(kb.sb([128, D], BF16), "xs%d" % i) for i in range(1)]),
        "win": Rot([(kb.sb([128, KT, 128], BF16), "win%d" % i) for i in range(4)]),
        "wout": Rot([(kb.sb([128, max(CG, KT), 512], BF16), "wout%d" % i) for i in range(2)]),
        "sg": Rot([(kb.sb([128, 512], F32), "sg%d" % i) for i in range(2)]),
    }
    return {
        "xnT": kb.sb([128, KT, T], BF16, "xnT"),
        "gT": kb.sb([128, KT], F32, "gTvec"),
        "gTb": kb.sb([128, CG, T], BF16, "gTb"),
        "scr": scr,
        "CG": CG,
    }


def ident_np():
    return np.eye(128, dtype=np.float32).astype(NPBF)


def lay_vec(g):
    g = np.asarray(g, dtype=np.float32)
    return np.ascontiguousarray(g.reshape(-1, 128).T)


def wview(W):
    return W.rearrange("(kt p) n -> p kt n", p=128)


def load_w_cols(kb, wt, wk, Wv, KT, segs, kgrp=8):
    for k0 in range(0, KT, kgrp):
        nk = min(kgrp, KT - k0)
        for (d0, s0, n) in segs:
            kb.add("pool", (lambda e, k0=k0, nk=nk, d0=d0, s0=s0, n=n: e.dma_start(
                out=wt[:, k0:k0 + nk, d0:d0 + n], in_=Wv[:, k0:k0 + nk, s0:s0 + n])),
                writes=[(wk, k0, d0)], dma_key=(wk, k0))
    return [(wk, k0, d0) for k0 in range(0, KT, kgrp) for (d0, s0, n) in segs]


def mm_fm(kb, cm, wt, wkeys, m, aT, akeys, KT, t0, tn):
    pp, ppk = cm.psf.next()
    for kt in range(KT):
        kb.add("pe", (lambda e, kt=kt, pp=pp: e.matmul(
            pp[:m, :tn], lhsT=wt[:, kt, :m], rhs=aT[:, kt, t0:t0 + tn],
            start=(kt == 0), stop=(kt == KT - 1))),
            reads=wkeys + akeys, writes=[ppk])
    return pp, ppk


def mm_tm(kb, cm, wt, wkeys, n, aT, akey_i, KT, i):
    pp, ppk = cm.psf.next()
    for kt in range(KT):
        kb.add("pe", (lambda e, kt=kt, pp=pp: e.matmul(
            pp[:, :n], lhsT=aT[:, kt, i * 128:(i + 1) * 128], rhs=wt[:, kt, :n],
            start=(kt == 0), stop=(kt == KT - 1))),
            reads=wkeys + [akey_i], writes=[ppk])
    return pp, ppk


class Evac:
    def __init__(self, kb):
        self.kb = kb
        self.i = 0

    def copy(self, out, in_, reads, writes, scale=None):
        self.i += 1
        kb = self.kb
        if self.i % 2 == 0:
            if scale is None:
                kb.add("act", (lambda e: e.activation(out=out, in_=in_, func=AF.Copy)),
                       reads=reads, writes=writes)
            else:
                kb.add("act", (lambda e: e.activation(out=out, in_=in_, func=AF.Copy, scale=float(scale))),
                       reads=reads, writes=writes)
        else:
            if scale is None:
                kb.add("dve", (lambda e: e.tensor_copy(out=out, in_=in_)), reads=reads, writes=writes)
            else:
                kb.add("dve", (lambda e: e.tensor_scalar_mul(out=out, in0=in_, scalar1=float(scale))),
                       reads=reads, writes=writes)


def emit_rope_evac(kb, pa, pak, pb, pbk, m, tn, cosT, sinT, tkeys, t0, scr, out):
    t1, t1k = scr["r1"].next()
    t2, t2k = scr["r2"].next()
    kb.add("dve", (lambda e: e.tensor_tensor(out=t1[:m, :tn], in0=pa[:m, :tn], in1=cosT[:m, t0:t0 + tn], op=ALU.mult)),
           reads=[pak] + tkeys, writes=[t1k])
    kb.add("dve", (lambda e: e.tensor_tensor(out=t2[:m, :tn], in0=pb[:m, :tn], in1=sinT[:m, t0:t0 + tn], op=ALU.mult)),
           reads=[pbk] + tkeys, writes=[t2k])
    return t1, t1k, t2, t2k


ROPE_DBG = set()


def emit_proj_fm(kb, cm, ev, Wv, KT, aT, akeys, T, chunks, out_dram, tabs, scr):
    default_out = out_dram
    for ch in chunks:
        col0, m = ch["col0"], ch["m"]
        out_dram = ch.get("out", default_out)
        wt, wk = scr["win"].next()
        wkeys = load_w_cols(kb, wt, wk, Wv, KT, [(0, col0, m)])
        rope = ch["kind"] == "rope"
        if rope:
            hs = ch["half"]
            wt2, wk2 = scr["win"].next()
            kb.add("act", (lambda e, wt=wt, wt2=wt2, hs=hs: e.activation(
                out=wt2[:, :KT, 0:hs], in_=wt[:, :KT, hs:2 * hs], func=AF.Copy)),
                reads=wkeys, writes=[(wk2, k0, 0) for k0 in range(0, KT, 8)])
            kb.add("dve", (lambda e, wt=wt, wt2=wt2, hs=hs: e.tensor_copy(
                out=wt2[:, :KT, hs:2 * hs], in_=wt[:, :KT, 0:hs])),
                reads=wkeys, writes=[(wk2, k0, 0) for k0 in range(0, KT, 8)])
            wkeys2 = [(wk2, k0, 0) for k0 in range(0, KT, 8)]
        for t0 in range(0, T, 512):
            tn = min(512, T - t0)
            pa, pak = mm_fm(kb, cm, wt, wkeys, m, aT, akeys, KT, t0, tn)
            plain_key = []
            if ch["kind"] in ("copy", "scale") or ch.get("plain_row0") is not None:
                st, stk = scr["stg"].next()
                plain_key = [stk]
                sc = ch.get("scale")
                ev.copy(st[:m, :tn], pa[:m, :tn], [pak], [stk], scale=sc)
                r0 = ch["row0"] if not rope else ch["plain_row0"]
                od_ = ch.get("plain_out", out_dram) if rope else out_dram
                kb.add("sp", (lambda e, st=st, r0=r0, t0=t0, tn=tn, m=m, od_=od_: e.dma_start(
                    out=od_[r0:r0 + m, t0:t0 + tn], in_=st[:m, :tn])),
                    reads=[stk], dma_key=(stk, "st"))
            if rope:
                pb, pbk = mm_fm(kb, cm, wt2, wkeys2, m, aT, akeys, KT, t0, tn)
                cosT, sinT, tkeys = tabs[ch["tab"]]
                st, stk = scr["stg"].next()
                if "nomul" in ROPE_DBG:
                    ev.copy(st[:m, :tn], pb[:m, :tn], [pbk], [stk])
                else:
                    t1, t1k, t2, t2k = emit_rope_evac(kb, pa, pak, pb, pbk, m, tn, cosT, sinT, tkeys + plain_key, t0, scr, None)
                    kb.add("dve", (lambda e, st=st, t1=t1, t2=t2, m=m, tn=tn: e.tensor_tensor(
                        out=st[:m, :tn], in0=t1[:m, :tn], in1=t2[:m, :tn], op=ALU.add)),
                        reads=[t1k, t2k], writes=[stk])
                r0 = ch["row0"]
                kb.add("sp", (lambda e, st=st, r0=r0, t0=t0, tn=tn, m=m, od_=out_dram: e.dma_start(
                    out=od_[r0:r0 + m, t0:t0 + tn], in_=st[:m, :tn])),
                    reads=[stk], dma_key=(stk, "st"))


def emit_proj_tm(kb, cm, ev, Wv, KT, aT, akey_fn, NT, groups, scr):
    for gp in groups:
        wt, wk = scr["wout"].next()
        wkeys = load_w_cols(kb, wt, wk, Wv, KT, gp["segs"], kgrp=4)
        n = gp["n"]
        for i in range(NT):
            pp, ppk = mm_tm(kb, cm, wt, wkeys, n, aT, akey_fn(i), KT, i)
            if gp.get("sink") is not None:
                gp["sink"](i, pp, ppk)
                continue
            if gp["dt"] == "bf16":
                st, stk = scr["stg"].next()
            else:
                st, stk = scr["stgf"].next()
            ev.copy(st[:, :n], pp[:, :n], [ppk], [stk])
            od, c0 = gp["out"]
            kb.add("sp", (lambda e, st=st, od=od, c0=c0, i=i, n=n: e.dma_start(
                out=od[i * 128:(i + 1) * 128, c0:c0 + n], in_=st[:, :n])),
                reads=[stk], dma_key=(stk, "st"))


def make_proj_scr(kb, scr):
    scr["stg"] = Rot([(kb.sb([128, 512], BF16), "stg%d" % i) for i in range(4)])
    scr["stgf"] = Rot([(kb.sb([128, 512], F32), "stgf%d" % i) for i in range(1)])
    scr["r1"] = Rot([(kb.sb([128, 512], F32), "r1_%d" % i) for i in range(1)])
    scr["r2"] = Rot([(kb.sb([128, 512], F32), "r2_%d" % i) for i in range(1)])


def load_tab(kb, shape, dram, name):
    t = kb.sb(shape, dram.dtype, name)
    kb.add("sp", (lambda e: e.dma_start(out=t[:], in_=dram)), writes=[name], dma_key=name)
    return t


HYB_IN = 5664
A_FM_ROWS = 5120
SC128 = 128.0 ** -0.5
SC192 = 192.0 ** -0.5


def build_A(NT=8, dbg=()):
    T = NT * 128
    D = D_MODEL
    KT = D // 128
    nc = bass.Bass("TRN2", target_bir_lowering=False)
    with contextlib.ExitStack() as st:
        kb = KB(nc, st)
        x_d = kb.dram("x", [T, D], F32, "ExternalInput")
        g1_d = kb.dram("g1", [128, KT], F32, "ExternalInput")
        wi_d = kb.dram("w_in", [D, 2 * D_FF], F32, "ExternalInput")
        wo_d = kb.dram("w_out", [D_FF, D], F32, "ExternalInput")
        gm_d = kb.dram("gm", [128, KT], F32, "ExternalInput")
        hw_d = kb.dram("hyb_w_in", [D, HYB_IN], F32, "ExternalInput")
        id_d = kb.dram("ident", [128, 128], BF16, "ExternalInput")
        tab_d = {n: kb.dram(n, [128, T], F32, "ExternalInput") for n in ("cosq", "sinq", "cosk", "sink")}
        x1_d = kb.dram("x1", [T, D], F32, "ExternalOutput")
        fm_d = kb.dram("fmA", [A_FM_ROWS, T], BF16, "ExternalOutput")
        gt_d = kb.dram("gatesT", [24, T], F32, "ExternalOutput")
        tm_d = kb.dram("tmA", [T, 1536], BF16, "ExternalOutput")
        fl_d = kb.dram("flog", [T, 8], F32, "ExternalOutput")

        cm = Common(kb, id_d)
        ev = Evac(kb)
        X = kb.sb([128, NT, D], F32, "X")
        bufs = make_ffn_bufs(kb, NT, D, D_FF, 11)
        scr = bufs["scr"]
        make_proj_scr(kb, scr)
        tabs_sb = {n: load_tab(kb, [128, T], tab_d[n], "t_" + n) for n in tab_d}
        tabs = {"q": (tabs_sb["cosq"], tabs_sb["sinq"], ["t_cosq", "t_sinq"]),
                "k": (tabs_sb["cosk"], tabs_sb["sink"], ["t_cosk", "t_sink"])}
        emit_load_x(kb, X, "X", x_d, NT)
        if "noffn" not in dbg:
            emit_ffn(kb, cm, X, "X", NT, D, D_FF, g1_d, wi_d, wo_d, 0.5, bufs)
        emit_store_x(kb, X, "X", x1_d, NT)
        xnT, gT = bufs["xnT"], bufs["gT"]
        kb.add("sp", lambda e: e.dma_start(out=gT[:, :KT], in_=gm_d), writes=["gTvec"], dma_key="gT")
        emit_norm_T(kb, cm, X, "X", NT, D, gT, "gTvec", xnT, "xnT", scr)
        akeys = [("xnT", i) for i in range(NT)]
        Wv = wview(hw_d)
        chunks = []
        for h in range(8):
            chunks.append(dict(col0=h * 128, m=128, kind="rope", half=64, tab="q",
                               row0=1024 + h * 128, plain_row0=h * 128, scale=SC128))
        for j in range(2):
            chunks.append(dict(col0=1024 + j * 128, m=128, kind="copy", row0=2048 + j * 128))
            chunks.append(dict(col0=1280 + j * 128, m=128, kind="copy", row0=2304 + j * 128))
            chunks.append(dict(col0=1536 + j * 128, m=128, kind="rope", half=64, tab="k", row0=2560 + j * 128))
            chunks.append(dict(col0=2048 + j * 128, m=128, kind="rope", half=64, tab="k", row0=2816 + j * 128))
        for h in range(8):
            chunks.append(dict(col0=2584 + h * 128, m=128, kind="scale", scale=SC128, row0=3072 + h * 128))
            chunks.append(dict(col0=3608 + h * 128, m=128, kind="copy", row0=4096 + h * 128))
        if "nofm" in dbg:
            chunks = []
        if "fm1" in dbg:
            chunks = chunks[8:]
        if "fm2" in dbg:
            chunks = chunks[:8]
        if "fmcopy" in dbg:
            chunks = [c for c in chunks if c["kind"] != "rope"]
        emit_proj_fm(kb, cm, ev, Wv, KT, xnT, akeys, T, chunks, fm_d, tabs, scr)
        wt, wk = scr["win"].next()
        wkeys = load_w_cols(kb, wt, wk, Wv, KT, [(0, 2560, 24)])
        for t0 in range(0, T if "nogates" not in dbg else 0, 512):
            pa, pak = mm_fm(kb, cm, wt, wkeys, 24, xnT, akeys, KT, t0, 512)
            sg, sgk = scr["stgf"].next()
            kb.add("act", (lambda e, sg=sg, pa=pa: e.activation(out=sg[:24, :], in_=pa[:24, :], func=AF.Sigmoid)),
                   reads=[pak], writes=[sgk])
            kb.add("sp", (lambda e, sg=sg, t0=t0: e.dma_start(out=gt_d[:, t0:t0 + 512], in_=sg[:24, :])),
                   reads=[sgk], dma_key=(sgk, "st"))
        groups = [
            dict(segs=[(0, 1792, 256), (256, 2304, 256)], n=512, out=(tm_d, 0), dt="bf16"),
            dict(segs=[(0, 4632, 512)], n=512, out=(tm_d, 512), dt="bf16"),
            dict(segs=[(0, 5144, 512)], n=512, out=(tm_d, 1024), dt="bf16"),
            dict(segs=[(0, 5656, 8)], n=8, out=(fl_d, 0), dt="f32"),
        ]
        if "notm" in dbg:
            groups = []
        if "noflog" in dbg:
            groups = groups[:3]
        emit_proj_tm(kb, cm, ev, Wv, KT, xnT, lambda i: ("xnT", i), NT, groups, scr)
        kb.finish()
    return nc


def rope_tables(d, pos, scale=1.0):
    inv = 1.0 / (10000.0 ** (np.arange(0, d, 2, dtype=np.float32) / d))
    ang = pos.astype(np.float32)[None, :] * inv.astype(np.float32)[:, None]
    c = np.cos(ang).astype(np.float32)
    s = np.sin(ang).astype(np.float32)
    cosf = np.concatenate([c, c], 0) * np.float32(scale)
    sinf = np.concatenate([-s, s], 0) * np.float32(scale)
    return np.ascontiguousarray(cosf.astype(np.float32)), np.ascontiguousarray(sinf.astype(np.float32))


NEG = -30000.0


class AttnRes:
    def __init__(self, kb, cm):
        self.sc = Rot(cm.psf_list[0:2])
        self.O = Rot(cm.psf_list[2:4])
        self.S = Rot(cm.psf_list[4:6])
        self.misc = cm.psf_list[6]
        self.pt = Rot([(kb.sb([128, 512], BF16), "pt%d" % i) for i in range(3)])
        self.ones = kb.sb([128, 128], BF16, "ones")
        kb.add("dve", lambda e: e.memset(self.ones[:], 1.0), writes=["ones"])
        self.rs = Rot([(kb.sb([128, 512], F32), "rs%d" % i) for i in range(2)])
        self.ost = Rot([(kb.sb([128, 512], BF16), "ost%d" % i) for i in range(2)])


def attn_qblock(kb, R, kts, score_fn, v_fn, np_fn, dv=128):
    po, pok = R.O.next()
    psm, psmk = R.S.next()
    n = len(kts)

    def emit_score(kt):
        sc, sck = R.sc.next()
        terms = score_fn(kt)
        npart = np_fn(kt)
        nt = len(terms)
        for ti, (l, r, rd) in enumerate(terms):
            kb.add("pe", (lambda e, sc=sc, l=l, r=r, ti=ti, nt=nt, npart=npart: e.matmul(
                sc[:npart, :], lhsT=l, rhs=r, start=(ti == 0), stop=(ti == nt - 1))),
                reads=rd, writes=[sck])
        return sc, sck, npart

    def emit_rest(idx, kt, sc, sck, npart):
        pt, ptk = R.pt.next()
        kb.add("act", (lambda e, pt=pt, sc=sc, npart=npart: e.activation(
            out=pt[:npart, :], in_=sc[:npart, :], func=AF.Exp)),
            reads=[sck], writes=[ptk])
        v, vr = v_fn(kt)
        kb.add("pe", (lambda e, po=po, v=v, pt=pt, idx=idx, npart=npart: e.matmul(
            po[:dv, :], lhsT=v, rhs=pt[:npart, :], start=(idx == 0), stop=(idx == n - 1))),
            reads=[ptk] + vr, writes=[pok])
        kb.add("pe", (lambda e, psm=psm, pt=pt, idx=idx, npart=npart: e.matmul(
            psm[:, :], lhsT=R.ones[:npart, :], rhs=pt[:npart, :], start=(idx == 0), stop=(idx == n - 1))),
            reads=[ptk, "ones"], writes=[psmk])

    prev = None
    for idx, kt in enumerate(kts):
        cur = emit_score(kt)
        if prev is not None:
            emit_rest(idx - 1, kts[idx - 1], *prev)
        prev = cur
    emit_rest(n - 1, kts[-1], *prev)
    return po, pok, psm, psmk


def finalize_plain(kb, R, po, pok, psm, psmk, out_dram, row0, q0):
    rs, rsk = R.rs.next()
    kb.add("dve", (lambda e: e.reciprocal(out=rs[:, :], in_=psm[:, :])), reads=[psmk], writes=[rsk])
    ost, ostk = R.ost.next()
    kb.add("dve", (lambda e: e.tensor_tensor(out=ost[:, :], in0=po[:, :], in1=rs[:, :], op=ALU.mult)),
           reads=[pok, rsk], writes=[ostk])
    kb.add("sp", (lambda e: e.dma_start(out=out_dram[row0:row0 + 128, q0:q0 + 512], in_=ost[:, :])),
           reads=[ostk], dma_key=(ostk, "st"))


def load_fm(kb, tile, key, dram, nchunk, rows=128):
    v = dram.rearrange("(c p) t -> p c t", p=rows)
    for c in range(nchunk):
        kb.add("sp", (lambda e, c=c: e.dma_start(out=tile[:rows, c, :], in_=v[:, c, :])),
               writes=[(key, c)], dma_key=(key, c % 4))


def load_tm(kb, tile, key, dram, ntile):
    v = dram.rearrange("(i p) n -> p i n", p=128)
    for i in range(ntile):
        kb.add("sp", (lambda e, i=i: e.dma_start(out=tile[:, i, :], in_=v[:, i, :])),
               writes=[(key, i)], dma_key=(key, i % 4))


def neg_masks_np():
    k = np.arange(128)[:, None, None]
    j = np.arange(4)[None, :, None]
    q = np.arange(512)[None, None, :]
    negC = np.where(q >= 128 * j + k, 0.0, NEG).astype(np.float32).astype(NPBF)
    negW = np.where(q < 128 * j + k, 0.0, NEG).astype(np.float32).astype(NPBF)
    return np.ascontiguousarray(negC), np.ascontiguousarray(negW)


def build_D(S=SEQ, NH=8):
    nc = bass.Bass("TRN2", target_bir_lowering=False)
    NKT = S // 128
    NQB = S // 512
    with contextlib.ExitStack() as st:
        kb = KB(nc, st)
        qn_d = kb.dram("qn", [NH * 128, S], BF16, "ExternalInput")
        qr_d = kb.dram("qr", [NH * 64, S], BF16, "ExternalInput")
        kn_d = kb.dram("kn", [NH * 128, S], BF16, "ExternalInput")
        kr_d = kb.dram("kr", [64, S], BF16, "ExternalInput")
        v_d = kb.dram("v", [S, NH * 128], BF16, "ExternalInput")
        negC_d = kb.dram("negC", [128, 4, 512], BF16, "ExternalInput")
        id_d = kb.dram("ident", [128, 128], BF16, "ExternalInput")
        o_d = kb.dram("oT", [NH * 128, S], BF16, "ExternalOutput")
        cm = Common(kb, id_d, nf=7, nb=1)
        R = AttnRes(kb, cm)
        QN = kb.sb([128, NH, S], BF16, "QN")
        QR = kb.sb([64, NH, S], BF16, "QR")
        KN = kb.sb([128, NH, S], BF16, "KN")
        KR = kb.sb([64, 1, S], BF16, "KR")
        V = kb.sb([128, NKT, NH * 128], BF16, "V")
        negC = kb.sb([128, 4, 512], BF16, "negC")
        kb.add("sp", lambda e: e.dma_start(out=negC[:], in_=negC_d), writes=["negC"], dma_key="negC")
        load_fm(kb, KR, "KR", kr_d, 1, rows=64)
        for h in range(NH):
            kb.add("sp", (lambda e, h=h: e.dma_start(out=KN[:, h, :], in_=kn_d[h * 128:(h + 1) * 128, :])),
                   writes=[("KN", h)], dma_key=("KN", h % 4))
            kb.add("sp", (lambda e, h=h: e.dma_start(out=QN[:, h, :], in_=qn_d[h * 128:(h + 1) * 128, :])),
                   writes=[("QN", h)], dma_key=("QN", h % 4))
            kb.add("sp", (lambda e, h=h: e.dma_start(out=QR[:, h, :], in_=qr_d[h * 64:(h + 1) * 64, :])),
                   writes=[("QR", h)], dma_key=("QR", h % 4))
            if h == 0:
                load_tm(kb, V, "V", v_d, NKT)
        vkeys = [("V", i) for i in range(NKT)]
        for h in range(NH):
            for qb in range(NQB):
                q0 = qb * 512

                def score_fn(kt, h=h, qb=qb, q0=q0):
                    t = [(KN[:, h, kt * 128:(kt + 1) * 128], QN[:, h, q0:q0 + 512], [("KN", h), ("QN", h)]),
                         (KR[:, 0, kt * 128:(kt + 1) * 128], QR[:, h, q0:q0 + 512], [("KR", 0), ("QR", h)])]
                    j = kt - 4 * qb
                    if j >= 0:
                        t.append((cm.ident[:], negC[:, j, :], ["ident", "negC"]))
                    return t

                def v_fn(kt, h=h):
                    return V[:, kt, h * 128:(h + 1) * 128], [("V", kt)]

                po, pok, psm, psmk = attn_qblock(kb, R, list(range(4 * qb + 4)), score_fn, v_fn, lambda kt: 128)
                finalize_plain(kb, R, po, pok, psm, psmk, o_d, h * 128, q0)
        kb.finish()
    return nc


def consts_B_np(S=SEQ):
    negC, negW = neg_masks_np()
    n = np.arange(128)[:, None]
    t = np.arange(S)[None, :]
    negcm = np.where((16 * n + 31 <= t) & (n < 127), 0.0, NEG).astype(np.float32).astype(NPBF)
    n_cmp, n_slc = 127, S // 64
    c0 = np.arange(n_cmp) * 16
    c1 = c0 + 32
    s0 = np.arange(n_slc) * 64
    s1 = s0 + 64
    ovm = np.clip(np.minimum(c1[:, None], s1[None, :]) - np.maximum(c0[:, None], s0[None, :]), 0, None) / 32.0
    ov = np.zeros((128, n_slc), np.float32)
    ov[:127] = ovm
    E = (np.arange(S)[None, :] // 64 == np.arange(n_slc)[:, None]).astype(np.float32)
    tt = np.arange(S)
    cur = tt // 64
    j = np.arange(n_slc)[None, :]
    forced = (j == 0) | (j == cur[:, None]) | (j == cur[:, None] - 1)
    fbv = np.where(j > cur[:, None], -1e9, np.where(forced, 1e9, 0.0)).astype(np.float32)
    fb = np.ascontiguousarray(fbv.reshape(S // 128, 128, n_slc).transpose(1, 0, 2))
    sel4 = np.zeros((4, 4 * 128), np.float32)
    selrowneg = np.zeros((4, 4 * 512), np.float32)
    for h in range(4):
        sel4[h, h * 128:(h + 1) * 128] = 1.0
        selrowneg[h, h * 512:(h + 1) * 512] = -1.0
    sel68 = np.zeros((68, 4 * 128), np.float32)
    selrowneg68 = np.zeros((68, 4 * 512), np.float32)
    for h in range(4):
        for base in (0, 32, 64):
            sel68[base + h, h * 128:(h + 1) * 128] = 1.0
            selrowneg68[base + h, h * 512:(h + 1) * 512] = -1.0
    selg = np.zeros((12, 12 * 128), np.float32)
    for i in range(12):
        selg[i, i * 128:(i + 1) * 128] = 1.0
    bf = lambda a: np.ascontiguousarray(a.astype(np.float32).astype(NPBF))
    return {"negC": negC, "negW": negW, "negcm": np.ascontiguousarray(negcm), "ov": bf(ov), "E": bf(E), "fb": fb,
            "sel4": bf(sel4), "selrowneg": bf(selrowneg), "sel68": bf(sel68), "selrowneg68": bf(selrowneg68),
            "selg": bf(selg), "ident": ident_np()}


def split3(kb, src, skey, npart, S, name):
    outs = []
    cur, curk = src, skey
    for i in range(3):
        c = kb.sb([npart, S], BF16, "%s_c%d" % (name, i))
        ck = "%s_c%d" % (name, i)
        kb.add("dve", (lambda e, c=c, cur=cur: e.tensor_copy(out=c[:, :], in_=cur[:, :])), reads=[curk], writes=[ck])
        outs.append((c, ck))
        if i < 2:
            r = kb.sb([npart, S], F32, "%s_r%d" % (name, i))
            rk = "%s_r%d" % (name, i)
            kb.add("dve", (lambda e, r=r, cur=cur, c=c: e.tensor_tensor(out=r[:, :], in0=cur[:, :], in1=c[:, :], op=ALU.subtract)),
                   reads=[curk, ck], writes=[rk])
            cur, curk = r, rk
    return outs


def _build_B_fox(kb, nc, cm, R, C, FM, TM, fl_d, fbias_d, o_d, S, QF, KF):
    NQB = S // 512
    ident = cm.ident
    negC = C["negC"]
    one1 = kb.sb([128, 1], F32, "one1f")
    kb.add("dve", lambda e: e.memset(one1[:], 1.0), writes=["one1"])
    NP_ = 68
    FL = kb.sb([NP_, S], F32, "FL")
    FBI = kb.sb([NP_, 1], F32, "FBI")
    kb.add("dve", lambda e: e.memset(FL[:], 0.0), writes=["FL"])
    kb.add("dve", lambda e: e.memset(FBI[:], 0.0), writes=["FBI"])
    for base in (0, 32, 64):
        kb.add("sp", (lambda e, base=base: e.dma_start(out=FL[base:base + 4, :], in_=fl_d)), writes=["FL"], dma_key=("FL", base))
        kb.add("sp", (lambda e, base=base: e.dma_start(out=FBI[base:base + 4, :], in_=fbias_d)), writes=["FBI"], dma_key=("FBI", base))
    Z = kb.sb([NP_, S], F32, "Z")
    ONES4 = kb.sb([NP_, S], F32, "ONES4")
    CUM = kb.sb([NP_, S], F32, "CUM")
    kb.add("dve", lambda e: e.memset(ONES4[:], 1.0), writes=["ONES4"])
    kb.add("dve", lambda e: e.tensor_scalar(out=Z[:, :], in0=FL[:, :], scalar1=FBI[:, 0:1], scalar2=None, op0=ALU.add),
           reads=["FL", "FBI"], writes=["Z"])
    kb.add("act", lambda e: e.activation(out=Z[:, :], in_=Z[:, :], func=AF.Exp, scale=-1.0), reads=["Z"], writes=["Z"])
    kb.add("act", lambda e: e.activation(out=Z[:, :], in_=Z[:, :], func=AF.Ln, bias=one1[:NP_, 0:1]),
           reads=["Z", "one1"], writes=["Z"])
    kb.add("dve", lambda e: e.tensor_scalar_mul(out=Z[:, :], in0=Z[:, :], scalar1=-1.0), reads=["Z"], writes=["Z"])
    kb.add("dve", lambda e: e.tensor_tensor_scan(out=CUM[:, :], data0=ONES4[:, :], data1=Z[:, :], initial=0.0,
                                                 op0=ALU.mult, op1=ALU.add),
           reads=["Z", "ONES4"], writes=["CUM"])
    cparts = split3(kb, CUM, "CUM", NP_, S, "cum")
    STK = kb.sb([NP_, S], BF16, "STK")
    kb.add("dve", lambda e: e.memset(STK[:], 0.0), writes=["STK"])
    for (c, ck), base in zip(cparts, (0, 32, 64)):
        kb.add("dve", (lambda e, c=c, base=base: e.tensor_copy(out=STK[base:base + 4, :], in_=c[base:base + 4, :])),
               reads=[ck], writes=["STK"])
    sel68, selrowneg68 = C["sel68"], C["selrowneg68"]

    def fox_head(h):
        for qb in range(NQB):
            q0 = qb * 512

            def score_fn(kt, h=h, qb=qb, q0=q0):
                t = [(FM[:, KF + h, kt * 128:(kt + 1) * 128], FM[:, QF + h, q0:q0 + 512], [("FM", KF + h), ("FM", QF + h)])]
                t.append((sel68[:, h * 128:(h + 1) * 128], STK[:, q0:q0 + 512], ["c_sel68", "STK"]))
                t.append((STK[:, kt * 128:(kt + 1) * 128], selrowneg68[:, h * 512:(h + 1) * 512], ["c_selrowneg68", "STK"]))
                j = kt - 4 * qb
                if j >= 0:
                    t.append((ident[:], negC[:, j, :], ["ident", "c_negC"]))
                return t

            def v_fn(kt, h=h):
                return TM[:, kt, h * 128:(h + 1) * 128], [("TM", kt)]

            po, pok, psm, psmk = attn_qblock(kb, R, list(range(4 * qb + 4)), score_fn, v_fn, lambda kt: 128)
            finalize_plain(kb, R, po, pok, psm, psmk, o_d, h * 128, q0)


    for h in range(4):
        fox_head(h)
    kb.finish()
    return nc


def build_B(mode, S=SEQ, dbg=()):
    nc = bass.Bass("TRN2", target_bir_lowering=False)
    NKT = S // 128
    NQB = S // 512
    with contextlib.ExitStack() as st:
        kb = KB(nc, st)
        NCH = 12 if mode == "nsa" else 8
        TMW = 256 if mode == "nsa" else 512
        fm_d = kb.dram("fmB", [NCH * 128, S], BF16, "ExternalInput")
        tm_d = kb.dram("tmB", [S, TMW], BF16, "ExternalInput")
        g_d = fl_d = fbias_d = pe_d = w1_d = w2_d = None
        if mode == "nsa":
            g_d = kb.dram("gates", [12, S], F32, "ExternalInput")
            pe_d = kb.dram("peT", [128, 64], F32, "ExternalInput")
            w1_d = kb.dram("w1", [2, 4096, 256], F32, "ExternalInput")
            w2_d = kb.dram("w2", [2, 256, 128], F32, "ExternalInput")
            cd = {"negC": ([128, 4, 512], BF16), "negW": ([128, 4, 512], BF16), "negcm": ([128, S], BF16),
                  "ov": ([128, 32], BF16), "E": ([32, S], BF16), "fb": ([128, NKT, 32], F32), "selg": ([12, 1536], BF16)}
        else:
            fl_d = kb.dram("flogT", [4, S], F32, "ExternalInput")
            fbias_d = kb.dram("fbias", [4, 1], F32, "ExternalInput")
            cd = {"negC": ([128, 4, 512], BF16), "sel68": ([68, 512], BF16), "selrowneg68": ([68, 2048], BF16)}
        c_d = {n: kb.dram(n, shp, dt, "ExternalInput") for n, (shp, dt) in cd.items()}
        id_d = kb.dram("ident", [128, 128], BF16, "ExternalInput")
        o_d = kb.dram("oT", [512, S], BF16, "ExternalOutput")
        selT_d = kb.dram("selT", [32, S], BF16, "ExternalOutput") if (mode == "nsa" and "dbgsel" in dbg) else None
        imp_d = kb.dram("impd", [S, 32], F32, "ExternalOutput") if (mode == "nsa" and "dbgsel" in dbg) else None
        cm = Common(kb, id_d, nf=7, nb=1)
        R = AttnRes(kb, cm)
        C = {n: load_tab(kb, cd[n][0], c_d[n], "c_" + n) for n in cd}
        FM = kb.sb([128, NCH, S], BF16, "FM")
        TM = kb.sb([128, NKT, TMW], BF16, "TM")
        load_fm(kb, FM, "FM", fm_d, NCH)
        load_tm(kb, TM, "TM", tm_d, NKT)
        QP, QR, KC, VC, KS, KW, QF, KF = 0, 4, 8, 9, 10, 11, 0, 4
        ident = cm.ident
        tiny = kb.sb([128, 1], F32, "tiny")
        one1 = kb.sb([128, 1], F32, "one1")
        kb.add("dve", lambda e: e.memset(tiny[:], 1e-30), writes=["tiny"])
        kb.add("dve", lambda e: e.memset(one1[:], 1.0), writes=["one1"])

        if mode == "fox":
            return _build_B_fox(kb, nc, cm, R, C, FM, TM, fl_d, fbias_d, o_d, S, QF, KF)
        G = kb.sb([12, S], F32, "G")
        kb.add("sp", lambda e: e.dma_start(out=G[:], in_=g_d), writes=["G"], dma_key="G")
        ghi = kb.sb([12, S], BF16, "ghi")
        gres = kb.sb([12, S], F32, "gres")
        glo = kb.sb([12, S], BF16, "glo")
        kb.add("dve", lambda e: e.tensor_copy(out=ghi[:, :], in_=G[:, :]), reads=["G"], writes=["ghi"])
        kb.add("dve", lambda e: e.tensor_tensor(out=gres[:, :], in0=G[:, :], in1=ghi[:, :], op=ALU.subtract),
               reads=["G", "ghi"], writes=["gres"])
        kb.add("dve", lambda e: e.tensor_copy(out=glo[:, :], in_=gres[:, :]), reads=["gres"], writes=["glo"])

        negC, negW, negcm, ovT, Em, fbT = C["negC"], C["negW"], C["negcm"], C["ov"], C["E"], C["fb"]
        selg = C["selg"]
        PE_ = kb.sb([128, 64], F32, "PEt")
        kb.add("sp", lambda e: e.dma_start(out=PE_[:], in_=pe_d), writes=["PEt"], dma_key="PEt")
        W1 = kb.sb([128, 32, 256], BF16, "W1")
        W2 = kb.sb([128, 2, 128], BF16, "W2")
        XB = kb.sb([128, 32, 128], BF16, "XB")
        HX = kb.sb([128, 128], F32, "HX")
        H2 = kb.sb([128, 128], F32, "H2")
        HT = kb.sb([128, 2, 128], BF16, "HT")
        KCC = kb.sb([128, 128], BF16, "KCC")
        VCC = kb.sb([128, 128], BF16, "VCC")
        for which in range(2):
            src = KC if which == 0 else VC
            w1keys = load_w_cols(kb, W1, "W1", w1_d[which].rearrange("(l p) n -> p l n", p=128), 32, [(0, 0, 256)])
            w2keys = load_w_cols(kb, W2, "W2", w2_d[which].rearrange("(c p) n -> p c n", p=128), 2, [(0, 0, 128)])
            for l in range(32):
                kb.add("dve", (lambda e, l=l, src=src, which=which: e.tensor_scalar(
                    out=XB[:, l, :127], in0=FM[:, src, l:l + 16 * 126 + 1:16],
                    scalar1=PE_[:, which * 32 + l:which * 32 + l + 1], scalar2=None, op0=ALU.add)),
                    reads=[("FM", src), "PEt"], writes=[("XB", l)])
            for hc in range(2):
                ph, phk = R.sc.next()
                for l in range(32):
                    kb.add("pe", (lambda e, ph=ph, l=l, hc=hc: e.matmul(
                        ph[:, :127], lhsT=W1[:, l, hc * 128:(hc + 1) * 128], rhs=XB[:, l, :127],
                        start=(l == 0), stop=(l == 31))),
                        reads=w1keys + [("XB", l)], writes=[phk])
                kb.add("act", (lambda e, ph=ph: e.activation(out=HX[:, :127], in_=ph[:, :127], func=AF.Copy)),
                       reads=[phk], writes=["HX"])
                kb.add("dve", lambda e: e.tensor_tensor(out=H2[:, :127], in0=HX[:, :127], in1=HX[:, :127], op=ALU.mult),
                       reads=["HX"], writes=["H2"])
                kb.add("dve", lambda e: e.tensor_tensor(out=H2[:, :127], in0=H2[:, :127], in1=HX[:, :127], op=ALU.mult),
                       reads=["HX", "H2"], writes=["H2"])
                kb.add("dve", lambda e: e.scalar_tensor_tensor(out=H2[:, :127], in0=H2[:, :127], scalar=0.044715,
                                                               in1=HX[:, :127], op0=ALU.mult, op1=ALU.add),
                       reads=["HX", "H2"], writes=["H2"])
                kb.add("act", lambda e: e.activation(out=H2[:, :127], in_=H2[:, :127], func=AF.Sigmoid, scale=1.5957691216057308),
                       reads=["H2"], writes=["H2"])
                kb.add("dve", (lambda e, hc=hc: e.tensor_tensor(out=HT[:, hc, :127], in0=HX[:, :127], in1=H2[:, :127], op=ALU.mult)),
                       reads=["HX", "H2"], writes=[("HT", hc)])
            pc, pck = R.sc.next()
            if which == 0:
                for hc in range(2):
                    kb.add("pe", (lambda e, pc=pc, hc=hc: e.matmul(pc[:, :127], lhsT=W2[:, hc, :], rhs=HT[:, hc, :127],
                                                                   start=(hc == 0), stop=(hc == 1))),
                           reads=w2keys + [("HT", 0), ("HT", 1)], writes=[pck])
                kb.add("dve", (lambda e, pc=pc: e.tensor_copy(out=KCC[:, :127], in_=pc[:, :127])), reads=[pck], writes=["KCC"])
            else:
                for hc in range(2):
                    kb.add("pe", (lambda e, pc=pc, hc=hc: e.matmul(pc[:127, :128], lhsT=HT[:, hc, :127], rhs=W2[:, hc, :],
                                                                   start=(hc == 0), stop=(hc == 1))),
                           reads=w2keys + [("HT", 0), ("HT", 1)], writes=[pck])
                kb.add("dve", (lambda e, pc=pc: e.tensor_copy(out=VCC[:127, :], in_=pc[:127, :128])), reads=[pck], writes=["VCC"])

        ACC = kb.sb([128, 4, 512], F32, "ACC")
        TMPW = Rot([(kb.sb([128, 512], F32), "tmpw%d" % i) for i in range(2)])
        TMPO = Rot([(kb.sb([128, 512], F32), "tmpo%d" % i) for i in range(2)])
        PN = Rot([(kb.sb([128, 512], BF16), "pn%d" % i) for i in range(2)])
        IMP2 = kb.sb([128, 4, 32], F32, "IMP2")
        TOP8 = kb.sb([128, 4, 8], F32, "TOP8")
        NSEL = kb.sb([128, 4, 32], BF16, "NSEL")
        NSELT = kb.sb([32, 512], BF16, "NSELT")
        impP, impk = R.misc
        psb, psbk = cm.psb_list[0]

        def finalize_gated(po, pok, psm, psmk, h, br, q0, rs=None, rsk=None):
            if rs is None:
                rs, rsk = R.rs.next()
                kb.add("dve", (lambda e: e.tensor_scalar(out=rs[:, :], in0=psm[:, :], scalar1=tiny[:, 0:1], scalar2=None, op0=ALU.max)),
                       reads=[psmk, "tiny"], writes=[rsk])
                kb.add("dve", (lambda e: e.reciprocal(out=rs[:, :], in_=rs[:, :])), reads=[rsk], writes=[rsk])
            i = h * 3 + br
            for gi, (gt, gk) in enumerate(((ghi, "ghi"), (glo, "glo"))):
                kb.add("pe", (lambda e, gt=gt, gi=gi: e.matmul(psm[:, :], lhsT=selg[:, i * 128:(i + 1) * 128], rhs=gt[:, q0:q0 + 512],
                                                               start=(gi == 0), stop=(gi == 1))),
                       reads=["c_selg", gk, rsk], writes=[psmk])
            w, wk = TMPW.next()
            kb.add("dve", (lambda e: e.tensor_tensor(out=w[:, :], in0=psm[:, :], in1=rs[:, :], op=ALU.mult)),
                   reads=[psmk, rsk], writes=[wk])
            if br == 0:
                kb.add("dve", (lambda e: e.tensor_tensor(out=ACC[:, h, :], in0=po[:, :], in1=w[:, :], op=ALU.mult)),
                       reads=[pok, wk], writes=[("ACC", h)])
            else:
                to, tok = TMPO.next()
                kb.add("dve", (lambda e: e.tensor_tensor(out=to[:, :], in0=po[:, :], in1=w[:, :], op=ALU.mult)),
                       reads=[pok, wk], writes=[tok])
                kb.add("dve", (lambda e: e.tensor_tensor(out=ACC[:, h, :], in0=ACC[:, h, :], in1=to[:, :], op=ALU.add)),
                       reads=[tok, ("ACC", h)], writes=[("ACC", h)])

        def nsa_qblock(qb):
            q0 = qb * 512
            for h in range(4):
                sc, sck = R.sc.next()
                kb.add("pe", (lambda e, sc=sc, h=h: e.matmul(sc[:127, :], lhsT=KCC[:, :127], rhs=FM[:, QP + h, q0:q0 + 512],
                                                            start=True, stop=False)),
                       reads=["KCC", ("FM", QP + h)], writes=[sck])
                kb.add("pe", (lambda e, sc=sc: e.matmul(sc[:127, :], lhsT=ident[:127, :127], rhs=negcm[:127, q0:q0 + 512],
                                                       start=False, stop=True)),
                       reads=["ident", "c_negcm"], writes=[sck])
                pt, ptk = R.pt.next()
                kb.add("act", (lambda e, pt=pt, sc=sc: e.activation(out=pt[:127, :], in_=sc[:127, :], func=AF.Exp)),
                       reads=[sck], writes=[ptk])
                po, pok = R.O.next()
                psm, psmk = R.S.next()
                kb.add("pe", (lambda e, po=po, pt=pt: e.matmul(po[:, :], lhsT=VCC[:127, :], rhs=pt[:127, :], start=True, stop=True)),
                       reads=["VCC", ptk], writes=[pok])
                kb.add("pe", (lambda e, psm=psm, pt=pt: e.matmul(psm[:, :], lhsT=R.ones[:127, :], rhs=pt[:127, :], start=True, stop=True)),
                       reads=["ones", ptk], writes=[psmk])
                rs, rsk = R.rs.next()
                kb.add("dve", (lambda e, rs=rs, psm=psm: e.tensor_scalar(out=rs[:, :], in0=psm[:, :], scalar1=tiny[:, 0:1], scalar2=None, op0=ALU.max)),
                       reads=[psmk, "tiny"], writes=[rsk])
                kb.add("dve", (lambda e, rs=rs: e.reciprocal(out=rs[:, :], in_=rs[:, :])), reads=[rsk], writes=[rsk])
                pn, pnk = PN.next()
                kb.add("dve", (lambda e, pn=pn, pt=pt, rs=rs: e.tensor_tensor(out=pn[:127, :], in0=pt[:127, :], in1=rs[:127, :], op=ALU.mult)),
                       reads=[ptk, rsk], writes=[pnk])
                for s_ in range(4):
                    kb.add("pe", (lambda e, pn=pn, s_=s_, h=h: e.matmul(
                        impP[:, h * 128 + s_ * 32:h * 128 + (s_ + 1) * 32], lhsT=pn[:127, s_ * 128:(s_ + 1) * 128],
                        rhs=ovT[:127, :], start=True, stop=True)),
                        reads=[pnk, "c_ov"], writes=[impk])
                finalize_gated(po, pok, psm, psmk, h, 0, q0, rs, rsk)
            kb.add("dve", (lambda e: e.tensor_tensor(out=IMP2[:, :, :], in0=impP[:, :128].rearrange("p (s j) -> p s j", j=32),
                                                     in1=fbT[:, 4 * qb:4 * qb + 4, :], op=ALU.add)),
                   reads=[impk, "c_fb"], writes=["IMP2"])
            for h in range(1, 4):
                kb.add("dve", (lambda e, h=h: e.tensor_tensor(
                    out=IMP2[:, :, :], in0=impP[:, h * 128:(h + 1) * 128].rearrange("p (s j) -> p s j", j=32),
                    in1=IMP2[:, :, :], op=ALU.add)),
                    reads=[impk, "IMP2"], writes=["IMP2"])
            if imp_d is not None:
                kb.add("sp", (lambda e: e.dma_start(out=imp_d[q0:q0 + 512, :].rearrange("(s p) j -> p s j", p=128), in_=IMP2[:, :, :])),
                       reads=["IMP2"], dma_key="IMP2_st")
            for s_ in range(4):
                kb.add("dve", (lambda e, s_=s_: e.max(out=TOP8[:, s_, :], in_=IMP2[:, s_, :])), reads=["IMP2"], writes=[("TOP8", s_)])
                kb.add("dve", (lambda e, s_=s_: e.tensor_scalar(out=NSEL[:, s_, :], in0=IMP2[:, s_, :], scalar1=TOP8[:, s_, 7:8],
                                                                scalar2=NEG, op0=ALU.is_lt, op1=ALU.mult)),
                       reads=["IMP2", ("TOP8", s_)], writes=[("NSEL", s_)])
                kb.add("pe", (lambda e, s_=s_: e.transpose(out=psb[:32, s_ * 128:(s_ + 1) * 128], in_=NSEL[:, s_, :], identity=ident[:])),
                       reads=[("NSEL", s_), "ident"], writes=[psbk])
            kb.add("dve", (lambda e: e.tensor_copy(out=NSELT[:, :], in_=psb[:32, :512])), reads=[psbk], writes=["NSELT"])
            if selT_d is not None:
                kb.add("sp", (lambda e: e.dma_start(out=selT_d[:, q0:q0 + 512], in_=NSELT[:, :])), reads=["NSELT"], dma_key="NSELT_st")
            for h in range(4):
                def score_slc(kt, h=h):
                    t = [(FM[:, KS, kt * 128:(kt + 1) * 128], FM[:, QR + h, q0:q0 + 512], [("FM", KS), ("FM", QR + h)]),
                         (Em[:, kt * 128:(kt + 1) * 128], NSELT[:, :], ["c_E", "NSELT"])]
                    j = kt - 4 * qb
                    if j >= 0:
                        t.append((ident[:], negC[:, j, :], ["ident", "c_negC"]))
                    return t

                po, pok, psm, psmk = attn_qblock(kb, R, list(range(4 * qb + 4)), score_slc,
                                                 lambda kt: (TM[:, kt, 0:128], [("TM", kt)]), lambda kt: 128)
                finalize_gated(po, pok, psm, psmk, h, 1, q0)

                def score_win(kt, h=h):
                    t = [(FM[:, KW, kt * 128:(kt + 1) * 128], FM[:, QR + h, q0:q0 + 512], [("FM", KW), ("FM", QR + h)])]
                    j = kt - 4 * qb
                    if j >= 0:
                        t.append((ident[:], negC[:, j, :], ["ident", "c_negC"]))
                    else:
                        t.append((ident[:], negW[:, j + 4, :], ["ident", "c_negW"]))
                    return t

                po, pok, psm, psmk = attn_qblock(kb, R, list(range(max(0, 4 * qb - 4), 4 * qb + 4)), score_win,
                                                 lambda kt: (TM[:, kt, 128:256], [("TM", kt)]), lambda kt: 128)
                finalize_gated(po, pok, psm, psmk, h, 2, q0)
                ost, ostk = R.ost.next()
                kb.add("act", (lambda e, ost=ost, h=h: e.activation(out=ost[:, :], in_=ACC[:, h, :], func=AF.Copy)),
                       reads=[("ACC", h)], writes=[ostk])
                kb.add("sp", (lambda e, ost=ost, h=h: e.dma_start(out=o_d[h * 128:(h + 1) * 128, q0:q0 + 512], in_=ost[:, :])),
                       reads=[ostk], dma_key=(ostk, "st"))

        for qb in range(NQB):
            nsa_qblock(qb)
        kb.finish()
    return nc


def emit_load_aT(kb, aT, akey, a_dram, KT):
    v = a_dram.rearrange("(kt p) t -> p kt t", p=128)
    for kt in range(KT):
        kb.add("sp", (lambda e, kt=kt: e.dma_start(out=aT[:, kt, :], in_=v[:, kt, :])),
               writes=[(akey, "all")], dma_key=(akey, "ld"))


def emit_outproj(kb, cm, X, xkey, NT, aT, akey, KT, W_dram, scr):
    Wv = wview(W_dram)
    for cg in range(D_MODEL // 512):
        wt, wk = scr["wout"].next()
        wkeys = load_w_cols(kb, wt, wk, Wv, KT, [(0, cg * 512, 512)], kgrp=4)
        for i in range(NT):
            pp, ppk = mm_tm(kb, cm, wt, wkeys, 512, aT, (akey, "all"), KT, i)
            kb.add("dve", (lambda e, pp=pp, i=i, cg=cg: e.tensor_tensor(
                out=X[:, i, cg * 512:(cg + 1) * 512], in0=pp[:, :], in1=X[:, i, cg * 512:(cg + 1) * 512], op=ALU.add)),
                reads=[ppk, (xkey, i)], writes=[(xkey, i)])


def build_C1(NT=8):
    T, D, KT = NT * 128, D_MODEL, D_MODEL // 128
    nc = bass.Bass("TRN2", target_bir_lowering=False)
    with contextlib.ExitStack() as st:
        kb = KB(nc, st)
        x_d = kb.dram("x", [T, D], F32, "ExternalInput")
        o_d = kb.dram("oT", [D, T], BF16, "ExternalInput")
        wo_d = kb.dram("w_o", [D, D], F32, "ExternalInput")
        ga_d = kb.dram("ga", [128, KT], F32, "ExternalInput")
        wia_d = kb.dram("w_in_a", [D, 2 * D_FF], F32, "ExternalInput")
        woa_d = kb.dram("w_out_a", [D_FF, D], F32, "ExternalInput")
        gb_d = kb.dram("gb", [128, KT], F32, "ExternalInput")
        wib_d = kb.dram("w_in_b", [D, 2 * D_FF], F32, "ExternalInput")
        wob_d = kb.dram("w_out_b", [D_FF, D], F32, "ExternalInput")
        id_d = kb.dram("ident", [128, 128], BF16, "ExternalInput")
        y_d = kb.dram("y", [T, D], F32, "ExternalOutput")
        cm = Common(kb, id_d)
        X = kb.sb([128, NT, D], F32, "X")
        bufs = make_ffn_bufs(kb, NT, D, D_FF, 11)
        emit_load_x(kb, X, "X", x_d, NT)
        emit_load_aT(kb, bufs["xnT"], "xnTa", o_d, KT)
        emit_outproj(kb, cm, X, "X", NT, bufs["xnT"], "xnTa", KT, wo_d, bufs["scr"])
        kb.add("dve", lambda e: e.memset(cm.eps[:], EPS), writes=[("xnTa", "all"), "eps"] + [("xnT", i) for i in range(NT)])
        emit_ffn(kb, cm, X, "X", NT, D, D_FF, ga_d, wia_d, woa_d, 0.5, bufs)
        emit_ffn(kb, cm, X, "X", NT, D, D_FF, gb_d, wib_d, wob_d, 0.5, bufs)
        emit_store_x(kb, X, "X", y_d, NT)
        kb.finish()
    return nc


def build_E(NT=8):
    T, D, KT = NT * 128, D_MODEL, D_MODEL // 128
    nc = bass.Bass("TRN2", target_bir_lowering=False)
    with contextlib.ExitStack() as st:
        kb = KB(nc, st)
        x_d = kb.dram("x", [T, D], F32, "ExternalInput")
        o_d = kb.dram("oT", [D, T], BF16, "ExternalInput")
        wo_d = kb.dram("w_o", [D, D], F32, "ExternalInput")
        ga_d = kb.dram("ga", [128, KT], F32, "ExternalInput")
        wia_d = kb.dram("w_in_a", [D, 2 * D_FF], F32, "ExternalInput")
        woa_d = kb.dram("w_out_a", [D_FF, D], F32, "ExternalInput")
        gf_d = kb.dram("gfull", [128, D], F32, "ExternalInput")
        id_d = kb.dram("ident", [128, 128], BF16, "ExternalInput")
        y_d = kb.dram("y", [T, D], F32, "ExternalOutput")
        cm = Common(kb, id_d)
        X = kb.sb([128, NT, D], F32, "X")
        bufs = make_ffn_bufs(kb, NT, D, D_FF, 11)
        scr = bufs["scr"]
        GF = load_tab(kb, [128, D], gf_d, "GF")
        YS = kb.sb([128, D], F32, "YS")
        emit_load_x(kb, X, "X", x_d, NT)
        emit_load_aT(kb, bufs["xnT"], "xnTa", o_d, KT)
        emit_outproj(kb, cm, X, "X", NT, bufs["xnT"], "xnTa", KT, wo_d, scr)
        kb.add("dve", lambda e: e.memset(cm.eps[:], EPS), writes=[("xnTa", "all"), "eps"] + [("xnT", i) for i in range(NT)])
        emit_ffn(kb, cm, X, "X", NT, D, D_FF, ga_d, wia_d, woa_d, 0.5, bufs)
        for i in range(NT):
            sq, sqk = scr["sq"].next()
            ss, ssk = scr["ss"].next()
            kb.add("act", (lambda e, i=i, sq=sq, ss=ss: e.activation(out=sq[:, :D], in_=X[:, i, :], func=AF.Square, accum_out=ss[:, 0:1])),
                   reads=[("X", i)], writes=[sqk, ssk])
            kb.add("act", (lambda e, ss=ss: e.activation(out=ss[:, 1:2], in_=ss[:, 0:1], func=AF.Sqrt, scale=1.0 / D, bias=cm.eps[:, 0:1])),
                   reads=[ssk, "eps"], writes=[ssk])
            kb.add("dve", (lambda e, ss=ss: e.reciprocal(out=ss[:, 2:3], in_=ss[:, 1:2])), reads=[ssk], writes=[ssk])
            kb.add("act", (lambda e, i=i, ss=ss: e.activation(out=YS[:, :], in_=X[:, i, :], func=AF.Copy, scale=ss[:, 2:3])),
                   reads=[("X", i), ssk], writes=["YS"])
            kb.add("dve", (lambda e: e.tensor_tensor(out=YS[:, :], in0=YS[:, :], in1=GF[:, :], op=ALU.mult)),
                   reads=["YS", "GF"], writes=["YS"])
            kb.add("sp", (lambda e, i=i: e.dma_start(out=y_d[i * 128:(i + 1) * 128, :], in_=YS[:, :])),
                   reads=["YS"], dma_key="YS_st")
        kb.finish()
    return nc


def build_C2(NT=8):
    T, D, KT = NT * 128, D_MODEL, D_MODEL // 128
    nc = bass.Bass("TRN2", target_bir_lowering=False)
    with contextlib.ExitStack() as st:
        kb = KB(nc, st)
        x_d = kb.dram("x", [T, D], F32, "ExternalInput")
        gm_d = kb.dram("gm", [128, KT], F32, "ExternalInput")
        wi_d = kb.dram("mla_w_in", [D, 1088], F32, "ExternalInput")
        qn_g_d = kb.dram("qnorm", [128, 4], F32, "ExternalInput")
        kvn_g_d = kb.dram("kvnorm", [128, 4], F32, "ExternalInput")
        wuq_d = kb.dram("w_uq", [512, 3072], F32, "ExternalInput")
        wukv_d = kb.dram("w_ukv", [512, 4096], F32, "ExternalInput")
        id_d = kb.dram("ident", [128, 128], BF16, "ExternalInput")
        tab_d = {n: kb.dram(n, [64, T], F32, "ExternalInput") for n in ("cosq", "sinq", "cosk", "sink")}
        qn_d = kb.dram("qn", [2048, T], BF16, "ExternalOutput")
        qr_d = kb.dram("qr", [1024, T], BF16, "ExternalOutput")
        kn_d = kb.dram("kn", [2048, T], BF16, "ExternalOutput")
        kr_d = kb.dram("kr", [64, T], BF16, "ExternalOutput")
        v_d = kb.dram("v", [T, 2048], BF16, "ExternalOutput")
        cm = Common(kb, id_d)
        ev = Evac(kb)
        scr = {
            "sq": Rot([(kb.sb([128, D], BF16), "sq0")]),
            "ss": Rot([(kb.sb([128, 4], F32), "ss%d" % i) for i in range(2)]),
            "xs": Rot([(kb.sb([128, D], BF16), "xs%d" % i) for i in range(1)]),
            "win": Rot([(kb.sb([128, KT, 128], BF16), "win%d" % i) for i in range(4)]),
            "wout": Rot([(kb.sb([128, KT, 512], BF16), "wout%d" % i) for i in range(2)]),
        }
        make_proj_scr(kb, scr)
        xnT = kb.sb([128, KT, T], BF16, "xnT")
        gT = kb.sb([128, KT], F32, "gTvec")
        XR = kb.sb([128, 2, D], F32, "XR")
        CQ = kb.sb([128, NT, 512], F32, "CQ")
        CKV = kb.sb([128, NT, 512], F32, "CKV")
        cqT = kb.sb([128, 4, T], BF16, "cqT")
        ckvT = kb.sb([128, 4, T], BF16, "ckvT")
        gq = kb.sb([128, 4], F32, "gq")
        gkv = kb.sb([128, 4], F32, "gkv")
        tabs_sb = {}
        for n in tab_d:
            t = kb.sb([64, T], F32, "t_" + n)
            kb.add("sp", (lambda e, t=t, n=n: e.dma_start(out=t[:], in_=tab_d[n])), writes=["t_" + n], dma_key="t_" + n)
            tabs_sb[n] = t
        tabs = {"q": (tabs_sb["cosq"], tabs_sb["sinq"], ["t_cosq", "t_sinq"]),
                "k": (tabs_sb["cosk"], tabs_sb["sink"], ["t_cosk", "t_sink"])}
        kb.add("sp", lambda e: e.dma_start(out=gT[:, :KT], in_=gm_d), writes=["gTvec"], dma_key="gT")
        kb.add("sp", lambda e: e.dma_start(out=gq[:, :], in_=qn_g_d), writes=["gq"], dma_key="gq")
        kb.add("sp", lambda e: e.dma_start(out=gkv[:, :], in_=kvn_g_d), writes=["gkv"], dma_key="gkv")

        def pre(i):
            sl = i % 2
            kb.add("sp", (lambda e, i=i, sl=sl: e.dma_start(out=XR[:, sl, :], in_=x_d[i * 128:(i + 1) * 128, :])),
                   writes=[("XR", sl)], dma_key=("XR", sl))
            return XR[:, sl, :], ("XR", sl)

        emit_norm_T(kb, cm, None, None, NT, D, gT, "gTvec", xnT, "xnT", scr, pre=pre)
        akeys = [("xnT", i) for i in range(NT)]
        Wv = wview(wi_d)

        def sink_to(Ct, ckey):
            def f(i, pp, ppk):
                ev.copy(Ct[:, i, :], pp[:, :512], [ppk], [(ckey, i)])
            return f

        groups = [dict(segs=[(0, 0, 512)], n=512, sink=sink_to(CQ, "CQ")),
                  dict(segs=[(0, 512, 512)], n=512, sink=sink_to(CKV, "CKV"))]
        emit_proj_tm(kb, cm, ev, Wv, KT, xnT, lambda i: ("xnT", i), NT, groups, scr)
        emit_proj_fm(kb, cm, ev, Wv, KT, xnT, akeys, T,
                     [dict(col0=1024, m=64, kind="rope", half=32, tab="k", row0=0)], kr_d, tabs, scr)
        emit_norm_T(kb, cm, CQ, "CQ", NT, 512, gq, "gq", cqT, "cqT", scr)
        emit_norm_T(kb, cm, CKV, "CKV", NT, 512, gkv, "gkv", ckvT, "ckvT", scr)
        qkeys = [("cqT", i) for i in range(NT)]
        kvkeys = [("ckvT", i) for i in range(NT)]
        chunks = []
        for h in range(16):
            chunks.append(dict(col0=h * 192, m=128, kind="scale", scale=SC192, row0=h * 128, out=qn_d))
            chunks.append(dict(col0=h * 192 + 128, m=64, kind="rope", half=32, tab="q", row0=h * 64, out=qr_d))
        emit_proj_fm(kb, cm, ev, wview(wuq_d), 4, cqT, qkeys, T, chunks, qn_d, tabs, scr)
        chunks = [dict(col0=h * 256, m=128, kind="copy", row0=h * 128) for h in range(16)]
        emit_proj_fm(kb, cm, ev, wview(wukv_d), 4, ckvT, kvkeys, T, chunks, kn_d, tabs, scr)
        groups = [dict(segs=[(j * 128, (4 * g + j) * 256 + 128, 128) for j in range(4)], n=512, out=(v_d, g * 512), dt="bf16")
                  for g in range(4)]
        emit_proj_tm(kb, cm, ev, wview(wukv_d), 4, ckvT, lambda i: ("ckvT", i), NT, groups, scr)
        kb.finish()
    return nc


TOK = 1024


def _run(nc, in_maps):
    res = run_bass_kernel_spmd(nc, in_maps, core_ids=list(range(8)))
    return res.results


def _cat_t(rs, name, b):
    return np.concatenate([np.asarray(rs[2 * b][name]), np.asarray(rs[2 * b + 1][name])], axis=1)


def _cat_r(rs, name, b):
    return np.concatenate([np.asarray(rs[2 * b][name]), np.asarray(rs[2 * b + 1][name])], axis=0)


def kernel(x, ffn1_norm, ffn1_w_in, ffn1_w_out, mix_norm, ffn2_norm, ffn2_w_in, ffn2_w_out,
           hyb_w_in, hyb_w_out, nsa_cmp_pe, nsa_cmp_w1, nsa_cmp_w2, fox_f_bias,
           mla_w_in, mla_q_norm, mla_kv_norm, mla_w_uq, mla_w_ukv, mla_w_out, final_norm):
    f = lambda a: np.asarray(a, dtype=np.float32)
    x = f(x)
    ffn1_norm, ffn1_w_in, ffn1_w_out = f(ffn1_norm), f(ffn1_w_in), f(ffn1_w_out)
    ffn2_norm, ffn2_w_in, ffn2_w_out = f(ffn2_norm), f(ffn2_w_in), f(ffn2_w_out)
    mix_norm, hyb_w_in, hyb_w_out = f(mix_norm), f(hyb_w_in), f(hyb_w_out)
    nsa_cmp_pe, nsa_cmp_w1, nsa_cmp_w2, fox_f_bias = f(nsa_cmp_pe), f(nsa_cmp_w1), f(nsa_cmp_w2), f(fox_f_bias)
    mla_w_in, mla_q_norm, mla_kv_norm = f(mla_w_in), f(mla_q_norm), f(mla_kv_norm)
    mla_w_uq, mla_w_ukv, mla_w_out, final_norm = f(mla_w_uq), f(mla_w_ukv), f(mla_w_out), f(final_norm)
    ca = np.ascontiguousarray
    T = TOK
    ident = ident_np()
    xf = ca(x.reshape(BATCH * SEQ, D_MODEL))

    in_maps = []
    for c in range(8):
        pos = np.arange((c % 2) * T, (c % 2 + 1) * T)
        cq, sq = rope_tables(128, pos, SC128)
        ck, sk = rope_tables(128, pos, 1.0)
        in_maps.append({"x": xf[c * T:(c + 1) * T], "g1": lay_vec(ffn1_norm[0]), "w_in": ffn1_w_in[0],
                        "w_out": ffn1_w_out[0], "gm": lay_vec(mix_norm[0]), "hyb_w_in": hyb_w_in[0], "ident": ident,
                        "cosq": cq, "sinq": sq, "cosk": ck, "sink": sk})
    rA = _run(build_A(), in_maps)

    CB = consts_B_np()
    peT = ca(np.concatenate([nsa_cmp_pe[0, 0].T, nsa_cmp_pe[0, 1].T], axis=1))
    maps_n, maps_f = [], []
    for b in range(BATCH):
        fm_b = _cat_t(rA, "fmA", b)
        tm_b = _cat_r(rA, "tmA", b)
        g_b = _cat_t(rA, "gatesT", b)
        fl_b = _cat_r(rA, "flog", b)
        for g in range(2):
            fmB = np.concatenate([fm_b[g * 512:(g + 1) * 512], fm_b[1024 + g * 512:1024 + (g + 1) * 512],
                                  fm_b[2048 + g * 128:2048 + (g + 1) * 128], fm_b[2304 + g * 128:2304 + (g + 1) * 128],
                                  fm_b[2560 + g * 128:2560 + (g + 1) * 128], fm_b[2816 + g * 128:2816 + (g + 1) * 128]], axis=0)
            tmB = np.concatenate([tm_b[:, g * 128:(g + 1) * 128], tm_b[:, 256 + g * 128:256 + (g + 1) * 128]], axis=1)
            m = {"fmB": ca(fmB), "tmB": ca(tmB), "gates": ca(g_b[g * 12:(g + 1) * 12]), "peT": peT,
                 "w1": nsa_cmp_w1[0], "w2": nsa_cmp_w2[0], "ident": ident}
            for n in ("negC", "negW", "negcm", "ov", "E", "fb", "selg"):
                m[n] = CB[n]
            maps_n.append(m)
            hh = g
            fmF = np.concatenate([fm_b[3072 + hh * 512:3072 + (hh + 1) * 512], fm_b[4096 + hh * 512:4096 + (hh + 1) * 512]], axis=0)
            m = {"fmB": ca(fmF), "tmB": ca(tm_b[:, 512 + hh * 512:512 + (hh + 1) * 512]),
                 "flogT": ca(fl_b[:, hh * 4:(hh + 1) * 4].T), "fbias": ca(fox_f_bias[0, hh * 4:(hh + 1) * 4].reshape(4, 1)),
                 "ident": ident}
            for n in ("negC", "sel68", "selrowneg68"):
                m[n] = CB[n]
            maps_f.append(m)
    rBn = _run(build_B("nsa"), maps_n)
    rBf = _run(build_B("fox"), maps_f)

    in_maps = []
    for c in range(8):
        b, half = c // 2, c % 2
        oT = np.concatenate([np.asarray(rBn[2 * b]["oT"]), np.asarray(rBn[2 * b + 1]["oT"]),
                             np.asarray(rBf[2 * b]["oT"]), np.asarray(rBf[2 * b + 1]["oT"])], axis=0)
        in_maps.append({"x": np.asarray(rA[c]["x1"]), "oT": ca(oT[:, half * T:(half + 1) * T]), "w_o": hyb_w_out[0],
                        "ga": lay_vec(ffn2_norm[0]), "w_in_a": ffn2_w_in[0], "w_out_a": ffn2_w_out[0],
                        "gb": lay_vec(ffn1_norm[1]), "w_in_b": ffn1_w_in[1], "w_out_b": ffn1_w_out[1], "ident": ident})
    rC1 = _run(build_C1(), in_maps)

    in_maps = []
    for c in range(8):
        pos = np.arange((c % 2) * T, (c % 2 + 1) * T)
        cq, sq = rope_tables(64, pos, SC192)
        ck, sk = rope_tables(64, pos, 1.0)
        in_maps.append({"x": np.asarray(rC1[c]["y"]), "gm": lay_vec(mix_norm[1]), "mla_w_in": mla_w_in[0],
                        "qnorm": lay_vec(mla_q_norm[0]), "kvnorm": lay_vec(mla_kv_norm[0]),
                        "w_uq": mla_w_uq[0], "w_ukv": mla_w_ukv[0], "ident": ident,
                        "cosq": cq, "sinq": sq, "cosk": ck, "sink": sk})
    rC2 = _run(build_C2(), in_maps)

    negC = CB["negC"]
    in_maps = []
    for b in range(BATCH):
        qn_b, qr_b, kn_b, kr_b = _cat_t(rC2, "qn", b), _cat_t(rC2, "qr", b), _cat_t(rC2, "kn", b), _cat_t(rC2, "kr", b)
        v_b = _cat_r(rC2, "v", b)
        for hh in range(2):
            in_maps.append({"qn": ca(qn_b[hh * 1024:(hh + 1) * 1024]), "qr": ca(qr_b[hh * 512:(hh + 1) * 512]),
                            "kn": ca(kn_b[hh * 1024:(hh + 1) * 1024]), "kr": ca(kr_b),
                            "v": ca(v_b[:, hh * 1024:(hh + 1) * 1024]), "negC": negC, "ident": ident})
    rD = _run(build_D(), in_maps)

    gfull = ca(np.broadcast_to(final_norm.reshape(1, D_MODEL), (128, D_MODEL)))
    in_maps = []
    for c in range(8):
        b, half = c // 2, c % 2
        oT = np.concatenate([np.asarray(rD[2 * b]["oT"]), np.asarray(rD[2 * b + 1]["oT"])], axis=0)
        in_maps.append({"x": np.asarray(rC1[c]["y"]), "oT": ca(oT[:, half * T:(half + 1) * T]), "w_o": mla_w_out[0],
                        "ga": lay_vec(ffn2_norm[1]), "w_in_a": ffn2_w_in[1], "w_out_a": ffn2_w_out[1],
                        "gfull": gfull, "ident": ident})
    rE = _run(build_E(), in_maps)
    out = np.concatenate([np.asarray(rE[c]["y"]) for c in range(8)], axis=0)
    return out.reshape(BATCH, SEQ, D_MODEL).astype(np.float32)
```
